# Optimizing a Trainium2 kernel written in Bass

```python
import jax, jax.numpy as jnp
from jax import lax
import numpy as np

D_MODEL = 1024
BATCH = 2
SEQ = 16384
DEPTH = 2
DEC_BATCH = 16
DEC_SEQ = 32
PAST_LEN = 2048

CHUNK = 64
WA = 256
CONV_W = 31
HB = 4
DKB = 128
DVB = 128
WB = HB * DVB
HC = 4
DHC = 64
WC = HC * DHC
BAND_CHUNKS = 8
BAND_PAST = BAND_CHUNKS * CHUNK
BAND_LEN = (BAND_CHUNKS + 1) * CHUNK
REL_CLIP = 128
D_FF = 2816
FFN_CONV_W = 3
EPS = 1e-6
N_A = 2 * WA
N_BQ = HB * DKB
N_BF = HB * DKB
N_BI = HB * DVB
N_BG = WB
N_C = 3 * WC
N_GATE = 3 * D_MODEL
N_IN = N_A + N_BQ + N_BF + N_BI + N_BG + N_C + N_GATE
SPLIT_IDX = (N_A, N_A + N_BQ, N_A + N_BQ + N_BF, N_A + N_BQ + N_BF + N_BI,
             N_A + N_BQ + N_BF + N_BI + N_BG, N_A + N_BQ + N_BF + N_BI + N_BG + N_C)

kernel_name = 'hybrid_stream_conv_hgrn2_bandattn_step'


def rmsnorm(x, g):
    xf = x.astype(jnp.float32)
    y = xf * lax.rsqrt(jnp.mean(xf * xf, axis=-1, keepdims=True) + EPS)
    return (y * g).astype(x.dtype)


def layernorm(x, g, b):
    xf = x.astype(jnp.float32)
    mu = jnp.mean(xf, axis=-1, keepdims=True)
    xc = xf - mu
    var = jnp.mean(xc * xc, axis=-1, keepdims=True)
    return (xc * lax.rsqrt(var + EPS) * g + b).astype(x.dtype)


def dwconv_valid(x_ext, w):
    c = x_ext.shape[-1]
    return lax.conv_general_dilated(x_ext, w.astype(x_ext.dtype)[:, None, :], window_strides=(1,),
                                    padding='VALID', dimension_numbers=('NWC', 'WIO', 'NWC'),
                                    feature_group_count=c)


def conv_module(a_glu, buf, dw_w, dw_b, ln_g, ln_b, w_out):
    u = a_glu[..., :WA] * jax.nn.sigmoid(a_glu[..., WA:])
    ext = jnp.concatenate([buf, u], axis=1)
    h = dwconv_valid(ext, dw_w) + dw_b
    h = jax.nn.silu(layernorm(h, ln_g, ln_b))
    return h @ w_out, ext[:, -(CONV_W - 1):]


def hgrn_chunk(S, q, logf, k, v):
    L = q.shape[2]
    b = jnp.cumsum(logf, axis=2)
    o_inter = jnp.einsum('nhtk,nhkv->nhtv', q * jnp.exp(b), S)
    causal = jnp.tril(jnp.ones((L, L), dtype=bool))[:, :, None]
    diff = b[:, :, :, None, :] - b[:, :, None, :, :]
    decay = jnp.where(causal, jnp.exp(jnp.where(causal, diff, 0.0)), 0.0)
    A = jnp.einsum('nhtk,nhtsk,nhsk->nhts', q, decay, k)
    o = o_inter + jnp.einsum('nhts,nhsv->nhtv', A, v)
    b_last = b[:, :, -1:, :]
    S_new = jnp.exp(b_last[:, :, 0, :, None]) * S + jnp.einsum('nhsk,nhsv->nhkv', k * jnp.exp(b_last - b), v)
    return S_new, o


def hgrn_mixer(S0, q_in, f_in, i_in, g_in, lb, norm_g, w_out, blk):
    N, T, _ = q_in.shape
    nb = T // blk

    def heads(z, d):
        return z.reshape(N, nb, blk, HB, d).transpose(1, 0, 3, 2, 4).astype(jnp.float32)

    zf = f_in.astype(jnp.float32)
    f = lb + (1.0 - lb) * jax.nn.sigmoid(zf)
    log_f = jnp.log(f)
    k = (1.0 - lb) * jax.nn.sigmoid(-zf)
    q = jax.nn.silu(q_in)
    xs = (heads(q, DKB), heads(log_f, DKB), heads(k, DKB), heads(i_in, DVB))
    S_fin, o = lax.scan(lambda S, c: hgrn_chunk(S, *c), S0.astype(jnp.float32), xs)
    o = o.transpose(1, 0, 3, 2, 4).reshape(N, T, HB, DVB)
    o = o * lax.rsqrt(jnp.mean(o * o, axis=-1, keepdims=True) + EPS)
    o = (o.reshape(N, T, WB) * norm_g).astype(q_in.dtype) * jax.nn.silu(g_in)
    return o @ w_out, S_fin


def rel_bias_table(tab, rel):
    return tab[:, jnp.clip(rel, -REL_CLIP, REL_CLIP) + REL_CLIP]


def attend(q, k, v, bias, valid):
    s = jnp.einsum('...qhd,...khd->...hqk', q, k).astype(jnp.float32) * (DHC ** -0.5) + bias
    s = jnp.where(valid[..., None, None, :], s, -1e30)
    p = jax.nn.softmax(s, axis=-1).astype(v.dtype)
    return jnp.einsum('...hqk,...khd->...qhd', p, v)


def conv_ffn(h, buf, w_up, dw_w, w_down):
    u = h @ w_up
    ext = jnp.concatenate([buf, u], axis=1)
    u = dwconv_valid(ext, dw_w)
    a, b = jnp.split(u, 2, axis=-1)
    return (jax.nn.silu(a) * b) @ w_down, ext[:, -(FFN_CONV_W - 1):]


def setup_inputs(seed: int = 0) -> dict:
    key = jax.random.key(seed)
    ks = jax.random.split(key, 26)
    nrm = jax.random.normal
    f32 = jnp.float32
    c_cache = min(BAND_PAST, PAST_LEN)
    return {
        'x_prompt': nrm(ks[0], (BATCH, SEQ, D_MODEL), f32),
        'x_sample': nrm(ks[1], (DEC_BATCH, DEC_SEQ, D_MODEL), f32),
        'state_conv': 0.5 * nrm(ks[2], (DEPTH, DEC_BATCH, CONV_W - 1, WA), f32),
        'state_hgrn': 0.5 * nrm(ks[3], (DEPTH, DEC_BATCH, HB, DKB, DVB), f32),
        'cache_attn_k': nrm(ks[4], (DEPTH, DEC_BATCH, c_cache, HC, DHC), f32),
        'cache_attn_v': nrm(ks[5], (DEPTH, DEC_BATCH, c_cache, HC, DHC), f32),
        'state_ffn': nrm(ks[6], (DEPTH, DEC_BATCH, FFN_CONV_W - 1, 2 * D_FF), f32),
        'w_in': nrm(ks[7], (DEPTH, D_MODEL, N_IN), f32) * D_MODEL ** -0.5,
        'conv_dw_w': nrm(ks[8], (DEPTH, CONV_W, WA), f32) * CONV_W ** -0.5,
        'conv_dw_b': 0.01 * nrm(ks[9], (DEPTH, WA), f32),
        'conv_ln_g': 1.0 + 0.02 * nrm(ks[10], (DEPTH, WA), f32),
        'conv_ln_b': 0.02 * nrm(ks[11], (DEPTH, WA), f32),
        'w_conv_out': nrm(ks[12], (DEPTH, WA, D_MODEL), f32) * WA ** -0.5,
        'hgrn_lb_logits': 0.1 * nrm(ks[13], (DEPTH, HB * DKB), f32),
        'hgrn_norm_g': 1.0 + 0.02 * nrm(ks[14], (DEPTH, WB), f32),
        'w_hgrn_out': nrm(ks[15], (DEPTH, WB, D_MODEL), f32) * WB ** -0.5,
        'attn_rel_bias': 0.5 * nrm(ks[16], (DEPTH, HC, 2 * REL_CLIP + 1), f32),
        'w_attn_out': nrm(ks[17], (DEPTH, WC, D_MODEL), f32) * WC ** -0.5,
        'w_mix_out': nrm(ks[18], (DEPTH, D_MODEL, D_MODEL), f32) * D_MODEL ** -0.5,
        'g_mix': 1.0 + 0.02 * nrm(ks[19], (DEPTH, D_MODEL), f32),
        'w_ffn_up': nrm(ks[20], (DEPTH, D_MODEL, 2 * D_FF), f32) * D_MODEL ** -0.5,
        'ffn_dw_w': nrm(ks[21], (DEPTH, FFN_CONV_W, 2 * D_FF), f32) * FFN_CONV_W ** -0.5,
        'w_ffn_down': nrm(ks[22], (DEPTH, D_FF, D_MODEL), f32) * D_FF ** -0.5,
        'g_ffn': 1.0 + 0.02 * nrm(ks[23], (DEPTH, D_MODEL), f32),
        'g_final': 1.0 + 0.02 * nrm(ks[24], (D_MODEL,), f32),
    }


def reference(x_prompt, x_sample, state_conv, state_hgrn, cache_attn_k, cache_attn_v, state_ffn,
              w_in, conv_dw_w, conv_dw_b, conv_ln_g, conv_ln_b, w_conv_out,
              hgrn_lb_logits, hgrn_norm_g, w_hgrn_out, attn_rel_bias, w_attn_out,
              w_mix_out, g_mix, w_ffn_up, ffn_dw_w, w_ffn_down, g_ffn, g_final):
    p_lb = jax.nn.softmax(hgrn_lb_logits.astype(jnp.float32), axis=0)
    lbs = jnp.cumsum(p_lb, axis=0) - p_lb[0]

    def attn_prompt(q, k, v, rel_tab, l):
        N, T = q.shape[0], q.shape[1]
        nC = T // CHUNK

        def band(t):
            tc = t.reshape(N, nC, CHUNK, HC, DHC)
            tp = jnp.pad(tc, ((0, 0), (BAND_CHUNKS, 0), (0, 0), (0, 0), (0, 0)))
            return jnp.concatenate([tp[:, j:j + nC] for j in range(BAND_CHUNKS + 1)], axis=2)

        qpos = jnp.arange(CHUNK) + BAND_PAST
        kpos = jnp.arange(BAND_LEN)
        bias = rel_bias_table(rel_tab, qpos[:, None] - kpos[None, :])
        valid = (jnp.arange(nC)[:, None] - BAND_CHUNKS) * CHUNK + kpos[None, :] >= 0
        o = attend(q.reshape(N, nC, CHUNK, HC, DHC), band(k), band(v), bias, valid)
        return o.reshape(N, T, HC, DHC), k[:, -BAND_PAST:], v[:, -BAND_PAST:]

    def attn_sample(q, k, v, rel_tab, l):
        T = q.shape[1]
        c_cache = cache_attn_k.shape[2]
        kc = jnp.concatenate([cache_attn_k[l], k], axis=1)
        vc = jnp.concatenate([cache_attn_v[l], v], axis=1)
        qpos = PAST_LEN + jnp.arange(T)
        kpos = jnp.concatenate([PAST_LEN - c_cache + jnp.arange(c_cache), PAST_LEN + jnp.arange(T)])
        bias = rel_bias_table(rel_tab, qpos[:, None] - kpos[None, :])
        valid = kpos >= (PAST_LEN // CHUNK - BAND_CHUNKS) * CHUNK
        return attend(q, kc, vc, bias, valid), k, v

    def run_layer(x, l, conv_buf, S0, ffn_buf, attn_fn, blk):
        N, T, _ = x.shape
        xn = rmsnorm(x, g_mix[l])
        z = xn @ w_in[l]
        a_glu, bq, bf, bi, bg, c_qkv, gates = jnp.split(z, SPLIT_IDX, axis=-1)
        yA, conv_new = conv_module(a_glu, conv_buf, conv_dw_w[l], conv_dw_b[l],
                                   conv_ln_g[l], conv_ln_b[l], w_conv_out[l])
        yB, S_new = hgrn_mixer(S0, bq, bf, bi, bg, lbs[l], hgrn_norm_g[l], w_hgrn_out[l], blk)
        cq, ck, cv = jnp.split(c_qkv, 3, axis=-1)
        oC, k_new, v_new = attn_fn(cq.reshape(N, T, HC, DHC), ck.reshape(N, T, HC, DHC),
                                   cv.reshape(N, T, HC, DHC), attn_rel_bias[l], l)
        yC = oC.reshape(N, T, WC) @ w_attn_out[l]
        g = jax.nn.sigmoid(gates)
        m = g[..., :D_MODEL] * yA + g[..., D_MODEL:2 * D_MODEL] * yB + g[..., 2 * D_MODEL:] * yC
        h = x + m @ w_mix_out[l]
        f, ffn_new = conv_ffn(rmsnorm(h, g_ffn[l]), ffn_buf, w_ffn_up[l], ffn_dw_w[l], w_ffn_down[l])
        return h + f, conv_new, S_new, k_new, v_new, ffn_new

    nP = x_prompt.shape[0]
    zero_conv = jnp.zeros((nP, CONV_W - 1, WA), x_prompt.dtype)
    zero_S = jnp.zeros((nP, HB, DKB, DVB), jnp.float32)
    zero_ffn = jnp.zeros((nP, FFN_CONV_W - 1, 2 * D_FF), x_prompt.dtype)

    xp, xs = x_prompt, x_sample
    cp, cs, sp, ss, kp, vp, ksn, vsn, fp, fs = [], [], [], [], [], [], [], [], [], []
    for l in range(DEPTH):
        xp, c1, s1, k1, v1, f1 = run_layer(xp, l, zero_conv, zero_S, zero_ffn, attn_prompt, CHUNK)
        xs, c2, s2, k2, v2, f2 = run_layer(xs, l, state_conv[l], state_hgrn[l], state_ffn[l],
                                           attn_sample, x_sample.shape[1])
        cp.append(c1); sp.append(s1); kp.append(k1); vp.append(v1); fp.append(f1)
        cs.append(c2); ss.append(s2); ksn.append(k2); vsn.append(v2); fs.append(f2)

    y_prompt = rmsnorm(xp, g_final)
    y_sample = rmsnorm(xs, g_final)
    new_conv_p = jnp.stack(cp)
    new_conv_s = jnp.stack(cs)
    new_hgrn_p = jnp.stack(sp)
    new_hgrn_s = jnp.stack(ss)
    new_k_p = jnp.stack(kp)
    new_v_p = jnp.stack(vp)
    new_k_s = jnp.stack(ksn)
    new_v_s = jnp.stack(vsn)
    new_ffn_p = jnp.stack(fp)
    new_ffn_s = jnp.stack(fs)
    return (y_prompt, y_sample, new_conv_p, new_conv_s, new_hgrn_p, new_hgrn_s,
            new_k_p, new_v_p, new_k_s, new_v_s, new_ffn_p, new_ffn_s)
```

```python
import types
import numpy as np
from contextlib import ExitStack
import concourse.bass as bass
import concourse.mybir as mybir
from concourse.bass_utils import run_bass_kernel_spmd

F32 = mybir.dt.float32
BF16 = mybir.dt.bfloat16
AF = mybir.ActivationFunctionType
ALU = mybir.AluOpType
AX = mybir.AxisListType

D = 1024
NIN = 6400
DFF = 2816
EPS = 1e-6
NEG = -30000.0
N_CORES = 8
SEG = 4096
H_A0, H_B0, H_A1, H_B1 = 1280, 768, 640, 128
TS = 32
NSEQ = 2


class Prog:
    def __init__(self):
        self.ops = []

    @staticmethod
    def _freeze(fn):
        if fn.__closure__ is None:
            return fn
        cells = []
        for c in fn.__closure__:
            try:
                cells.append(types.CellType(c.cell_contents))
            except ValueError:
                cells.append(c)
        return types.FunctionType(fn.__code__, fn.__globals__, fn.__name__, fn.__defaults__, tuple(cells))

    def add(self, eng, fn, r=(), w=(), dma=None, cont=False):
        self.ops.append(dict(eng=eng, fn=self._freeze(fn), r=tuple(r), w=tuple(w), dma=dma, cont=cont))

    def pe(self, fn, r=(), w=()):
        self.add('pe', fn, r, w)

    def act(self, fn, r=(), w=()):
        self.add('act', fn, r, w)

    def dve(self, fn, r=(), w=()):
        self.add('dve', fn, r, w)

    def pool(self, fn, r=(), w=()):
        self.add('pool', fn, r, w)

    def dma(self, fn, r=(), w=(), key='d', cont=False, q='sp'):
        self.add(q, fn, r, w, dma=key, cont=cont)

    def barrier(self):
        self.ops.append(dict(eng=None, fn=None, r=(), w=(), dma=None, cont=False))

    def emit(self, nc, stack):
        ops = self.ops
        n = len(ops)
        lastw = {}
        readers = {}
        deps = [None] * n
        dma_hist = {}
        last_eng = {}
        last_dma = {}
        pending = {}
        for i, o in enumerate(ops):
            if o['eng'] is None:
                snap = set(last_eng.values()) | set(last_dma.values())
                for e_ in ('sp', 'pe', 'act', 'dve', 'pool'):
                    pending[e_] = pending.get(e_, set()) | snap
                deps[i] = set()
                continue
            d = set()
            if pending.get(o['eng']):
                d |= pending.pop(o['eng'])
            if o['dma'] is None:
                last_eng[o['eng']] = i
            else:
                last_dma[o['dma']] = i
            for t in o['r']:
                for j in lastw.get(t, ()):
                    d.add(j)
                if isinstance(t, tuple) and t[0] == 'ps':
                    for j in readers.get(t, {}).values():
                        d.add(j)
            for t in o['w']:
                for j in lastw.get(t, ()):
                    d.add(j)
                for j in readers.get(t, {}).values():
                    d.add(j)
            if o['dma'] is not None:
                groups = dma_hist.setdefault(o['dma'], [])
                if o['cont'] and groups:
                    groups[-1].append(i)
                    prev = groups[:-1]
                else:
                    prev = list(groups)
                    groups.append([i])
                if prev:
                    d.add(prev[-1][-1])
                for j in list(d):
                    if j in groups[-1]:
                        d.discard(j)
            d.discard(i)
            deps[i] = d
            for t in o['r']:
                rd = readers.setdefault(t, {})
                if o['dma'] is not None:
                    rd[('dma', i)] = i
                else:
                    rd[o['eng']] = i
            for t in o['w']:
                lastw[t] = [i]
                readers[t] = {}
        dcount = {}
        for key, groups in dma_hist.items():
            c = 0
            for g in groups:
                c += 16 * len(g)
                for i in g:
                    dcount[i] = c
        need = [False] * n
        for i, o in enumerate(ops):
            if o['eng'] is None:
                continue
            for j in deps[i]:
                pj = ops[j]
                if pj['dma'] is not None:
                    continue
                if pj['eng'] == o['eng'] and o['dma'] is None and o['eng'] == 'pe':
                    continue
                need[j] = True
        cnt = [0] * n
        ec = {}
        for i, o in enumerate(ops):
            if o['eng'] is not None and o['dma'] is None and need[i]:
                ec[o['eng']] = ec.get(o['eng'], 0) + 1
                cnt[i] = ec[o['eng']]
        sems = {}
        for e in ('pe', 'act', 'dve', 'pool'):
            sems[e] = stack.enter_context(nc.semaphore('tl_' + e))
        dsems = {}
        for k, key in enumerate(dma_hist):
            dsems[key] = stack.enter_context(nc.semaphore('dq%d' % k))
        block = stack.enter_context(nc.Block())
        engs = ('sp', 'pe', 'act', 'dve', 'pool')
        per = {e: [i for i in range(n) if ops[i]['eng'] == e] for e in engs}
        totals = {key: 16 * sum(len(g) for g in groups) for key, groups in dma_hist.items()}

        def run(eng_name, h):
            seen = {}
            for i in per[eng_name]:
                o = ops[i]
                waits = {}
                for j in deps[i]:
                    pj = ops[j]
                    if pj['dma'] is not None:
                        s, v = ('d', pj['dma']), dcount[j]
                    else:
                        if pj['eng'] == eng_name and eng_name == 'pe' and o['dma'] is None:
                            continue
                        s, v = ('e', pj['eng']), cnt[j]
                    if v > waits.get(s, 0):
                        waits[s] = v
                for s, v in waits.items():
                    if seen.get(s, 0) >= v:
                        continue
                    seen[s] = v
                    sem = dsems[s[1]] if s[0] == 'd' else sems[s[1]]
                    h.wait_ge(sem, v)
                ins = o['fn'](h)
                if o['dma'] is not None:
                    ins.then_inc(dsems[o['dma']], 16)
                elif need[i]:
                    ins.then_inc(sems[eng_name], 1)
            if eng_name == 'sp':
                for key, tot in totals.items():
                    h.wait_ge(dsems[key], tot)

        @block.sync
        def _(h):
            run('sp', h)

        @block.tensor
        def _(h):
            run('pe', h)

        @block.scalar
        def _(h):
            run('act', h)

        @block.vector
        def _(h):
            run('dve', h)

        @block.gpsimd
        def _(h):
            run('pool', h)


def build_program(seg=SEG, halos=(H_A0, H_B0, H_A1, H_B1)):
    hA0, hB0, hA1, hB1 = halos
    W = hA0
    NTOK = W + seg
    NT = NTOK // 128
    nc = bass.Bass("TRN2", target_bir_lowering=False)
    P = Prog()
    stack = ExitStack()

    def din(name, shape, dt=F32):
        return nc.dram_tensor(name, list(shape), dt, kind="ExternalInput").ap()

    def dout(name, shape, dt=F32):
        return nc.dram_tensor(name, list(shape), dt, kind="ExternalOutput").ap()

    def dscr(name, shape, dt=F32):
        return nc.dram_tensor(name, list(shape), dt, kind="Internal").ap()

    xp = din("xp", [NTOK, D])
    valid = din("valid", [128, NT])
    km_in = [din("km0", [128, 512 + hA0]), din("km1", [128, 512 + hA1])]
    xs = din("xs", [NSEQ * TS, D])
    sconvT = din("sconvT", [2, NSEQ, 128, 60])
    shgrn = din("shgrn", [2, NSEQ, 128, 512])
    skT = din("skT", [2, NSEQ, 128, 1024])
    sv = din("sv", [2, NSEQ, 128, 1024])
    sffnT = din("sffnT", [2, NSEQ, 128, 88])
    w_in = din("w_in", [2, D, NIN])
    w_conv_out = din("w_conv_out", [2, 256, D])
    w_hgrn_out = din("w_hgrn_out", [2, 512, D])
    w_attn_out = din("w_attn_out", [2, 256, D])
    w_mix_out = din("w_mix_out", [2, D, D])
    w_ffn_up = din("w_ffn_up", [2, D, 2 * DFF])
    w_ffn_down = din("w_ffn_down", [2, DFF, D])
    gmixT = din("gmixT", [2, 128, 8])
    gffnT = din("gffnT", [2, 128, 8])
    dwT_in = din("dwT", [2, 128, 62])
    dwb_in = din("dwb", [2, 128, 2])
    lng_in = din("lng", [2, 128, 2])
    lnb_in = din("lnb", [2, 128, 2])
    lbl_in = din("lbl", [2, 128, 4])
    hng_in = din("hng", [2, 128, 4])
    fdw_in = din("fdw", [2, 128, 132])
    R_in = din("R", [2, 128, 2560])
    gfin_in = din("gfin", [128, D])
    y_o = dout("y", [seg, D])
    ys_o = dout("ys", [NSEQ * TS, D])
    convp_o = dout("convp", [2, 128, 60])
    hgrnp_o = dout("hgrnp", [2, 128, 512])
    kp_o = dout("kp", [2, 128, 1024])
    vp_o = dout("vp", [2, 512, 256])
    ffnp_o = dout("ffnp", [2, 128, 88])
    convs_o = dout("convs", [2, NSEQ, 128, 60])
    hgrns_o = dout("hgrns", [2, NSEQ, 128, 512])
    ks_o = dout("ks", [2, NSEQ, 128, 64])
    vs_o = dout("vs", [2, NSEQ, TS, 256])
    ffns_o = dout("ffns", [2, NSEQ, 128, 88])
    x1_d = dscr("x1_d", [NTOK, D])
    hm_d = dscr("hm_d", [NTOK, D])
    ft_d = dscr("ft_d", [NT, 128, 1024], BF16)
    x1s_d = dscr("x1s_d", [NSEQ * TS, D])
    hms_d = dscr("hms_d", [NSEQ * TS, D])
    fts_d = dscr("fts_d", [NSEQ, 128, 1024], BF16)

    def sb(name, cols, dt=F32):
        return stack.enter_context(nc.sbuf_tensor("s_" + name, [128, cols], dt))

    arena = sb("arena", 67584, BF16)
    stg = [sb("stg%d" % i, 1024) for i in range(2)]
    xt = [sb("xt%d" % i, D) for i in range(3)]
    junk = sb("junk", D, BF16)
    xn = sb("xn", D, BF16)
    xnT = sb("xnT", 1024, BF16)
    sm = sb("sm", 64)
    hout = sb("hout", D)
    ident_f = sb("ident_f", 128)
    ident_b = sb("ident_b", 128, BF16)
    ones_f = sb("ones_f", 128)
    o128 = sb("o128", 128)
    o256 = sb("o256", 128)
    maskT = sb("maskT", 128)
    valid_sb = sb("valid_sb", NT)
    gfin = sb("gfin", D)
    epsc = sb("epsc", 2)
    negones = sb("negones", 128)
    gT = sb("gT", 8)
    dwT = sb("dwT", 62)
    dwb = sb("dwb", 2)
    lng = sb("lng", 2)
    lnb = sb("lnb", 2)
    lbl = sb("lbl", 8)
    lb = sb("lb", 4)
    oml = sb("oml", 4)
    hng = sb("hng", 4)
    fdw = sb("fdw", 132)
    U = sb("U", 9216)

    class Carve:
        def __init__(self, arena_from=67584):
            self.a_off = arena_from
            self.u_off = 0

        def _take(self, nbytes):
            nb2 = (nbytes + 3) // 4 * 2
            if self.a_off + nb2 <= 67584:
                a = arena[:, self.a_off:self.a_off + nb2]
                self.a_off += nb2
                return a, BF16
            c32 = nb2 // 2
            a = U[:, self.u_off:self.u_off + c32]
            self.u_off += c32
            assert self.u_off <= 9216, self.u_off
            return a, F32

        def f32(self, cols):
            a, dt = self._take(4 * cols)
            return a if dt == F32 else a.bitcast(F32)

        def bf(self, cols):
            a, dt = self._take(2 * cols)
            a = a if dt == BF16 else a.bitcast(BF16)
            return a[:, 0:cols]

    psum = [stack.enter_context(nc.psum_tensor("ps%d" % i, [128, 512], F32)) for i in range(8)]
    pctr = [0]

    def bank(fixed=None):
        if fixed is not None:
            return psum[fixed], ('ps', fixed)
        b = pctr[0] % 8
        pctr[0] += 1
        return psum[b], ('ps', b)

    P.pool(lambda e: e.memset(ones_f[:, :], 1.0), w=['ones_f'])
    P.pool(lambda e: e.memset(negones[:, :], -1.0), w=['negones'])
    P.pool(lambda e: e.memset(epsc[:, 0:1], EPS), w=['epsc'])
    P.pool(lambda e: e.memset(epsc[:, 1:2], 1.0), w=['epsc'])
    P.pool(lambda e: e.memset(o128[:, :], 1.0 / 128), w=['o128'])
    P.pool(lambda e: e.memset(o256[:, :], 1.0 / 256), w=['o256'])
    P.pool(lambda e: e.memset(ident_f[:, :], 0.0), w=['ident_f'])
    P.pool(lambda e: e.affine_select(out=ident_f[:, :], in_=ident_f[:, :], pattern=[[-1, 128]],
                                     compare_op=ALU.not_equal, fill=1.0, base=0, channel_multiplier=1),
           r=['ident_f'], w=['ident_f'])
    P.pool(lambda e: e.tensor_copy(out=ident_b[:, :], in_=ident_f[:, :]), r=['ident_f'], w=['ident_b'])
    P.pool(lambda e: e.affine_select(out=maskT[:, :], in_=ones_f[:, :], pattern=[[1, 128]],
                                     compare_op=ALU.is_ge, fill=0.0, base=0, channel_multiplier=-1),
           r=['ones_f'], w=['maskT'])
    P.dma(lambda e: e.dma_start(out=valid_sb[:, :], in_=valid[:, :]), w=['valid_sb'], key='c0')
    P.dma(lambda e: e.dma_start(out=gfin[:, :], in_=gfin_in[:, :]), w=['gfin'], key='c0', cont=True)

    wctr = [0]
    stg_all = list(stg) + [U[:, i * 1024:(i + 1) * 1024] for i in range(8)]

    def load_weight(dst_ap, dst_tok, src_rows_ap, ncols, scale_ap=None, scale_tok=None):
        c0 = 0
        while c0 < ncols:
            cw = min(1024, ncols - c0)
            s = wctr[0] % len(stg_all)
            wctr[0] += 1
            st, stok = stg_all[s], 'stg%d' % s
            P.dma(lambda e, st=st, c0=c0, cw=cw: e.dma_start(out=st[:, 0:cw], in_=src_rows_ap[:, c0:c0 + cw]),
                  w=[stok], key=stok)
            d = dst_ap[:, c0:c0 + cw]
            dst_tok = ('Wp', wctr[0])
            eng = ('act', 'dve')[wctr[0] % 2] if scale_ap is not None else ('act', 'dve', 'pool', 'dve')[wctr[0] % 4]
            if scale_ap is None:
                if eng == 'act':
                    P.act(lambda e, d=d, st=st, cw=cw: e.copy(out=d, in_=st[:, 0:cw]), r=[stok], w=[dst_tok])
                else:
                    P.add(eng, lambda e, d=d, st=st, cw=cw: e.tensor_copy(out=d, in_=st[:, 0:cw]), r=[stok], w=[dst_tok])
            else:
                if eng == 'act':
                    P.act(lambda e, d=d, st=st, cw=cw: e.activation(out=d, in_=st[:, 0:cw], func=AF.Copy, scale=scale_ap),
                          r=[stok, scale_tok], w=[dst_tok])
                else:
                    P.add(eng, lambda e, d=d, st=st, cw=cw: e.tensor_scalar(out=d, in0=st[:, 0:cw], scalar1=scale_ap,
                                                                           scalar2=None, op0=ALU.mult),
                          r=[stok, scale_tok], w=[dst_tok])
            c0 += cw

    xn_s = [xn, None]
    xnT_s = [xnT, None]
    xn_tok = ['xn0', 'xn1']

    def pro_load(src_ap, T, xslot, srctok=None):
        x_t, xtok = xt[xslot], 'xt%d' % xslot
        P.dma(lambda e: e.dma_start(out=x_t[:T, :], in_=src_ap), r=([srctok] if srctok else []), w=[xtok], key=xtok)

    def prologue(T, xslot, slot):
        x_t, xtok = xt[xslot], 'xt%d' % xslot
        xn_, xnT_ = xn_s[slot], xnT_s[slot]
        xnt, xTt = xn_tok[slot], 'xnT%d' % slot
        b0 = 16 * slot
        smt = 'smp%d' % slot
        P.act(lambda e: e.activation(out=junk[:T, :], in_=x_t[:T, :], func=AF.Square, accum_out=sm[:T, b0:b0 + 1]),
              r=[xtok], w=['junk', smt])
        P.act(lambda e: e.activation(out=sm[:T, b0 + 1:b0 + 2], in_=sm[:T, b0:b0 + 1], func=AF.Ln, scale=1.0 / D, bias=epsc[:T, 0:1]),
              r=[smt, 'epsc'], w=[smt])
        P.act(lambda e: e.activation(out=sm[:T, b0 + 2:b0 + 3], in_=sm[:T, b0 + 1:b0 + 2], func=AF.Exp, scale=-0.5), r=[smt], w=[smt])
        P.act(lambda e: e.activation(out=xn_[:T, :], in_=x_t[:T, :], func=AF.Copy, scale=sm[:T, b0 + 2:b0 + 3]),
              r=[xtok, smt], w=[xnt])
        for half in range(2):
            ps, ptok = bank()
            for kk in range(4):
                k = half * 4 + kk
                P.pe(lambda e: e.matmul(ps[:, kk * 128:kk * 128 + T], lhsT=xn_[:T, k * 128:(k + 1) * 128],
                                        rhs=ident_b[:T, :T], start=True, stop=True),
                     r=[xnt, 'ident_b'], w=[ptok])
            dst = xnT_[:, half * 512:(half + 1) * 512].rearrange("p (k t) -> p k t", t=128)[:, :, :T]
            src = ps[:, :].rearrange("p (k t) -> p k t", t=128)[:, :, :T]
            if half == 0:
                P.dve(lambda e: e.tensor_copy(out=dst, in_=src), r=[ptok], w=[xTt])
            else:
                P.act(lambda e: e.copy(out=dst, in_=src), r=[ptok], w=[xTt])
        return x_t, xtok, xnT_, xTt

    def proj_feat(wv, wtok, col, T, xT, xTtok, nk=8):
        ps, ptok = bank()
        for k in range(nk):
            P.pe(lambda e: e.matmul(ps[:, :T], lhsT=wv[:, k, col:col + 128], rhs=xT[:, k * 128:k * 128 + T],
                                    start=(k == 0), stop=(k == nk - 1)),
                 r=[wtok, xTtok], w=[ptok])
        return ps, ptok

    def proj_tok(wv, wtok, col, ncol, T, xT, xTtok):
        ps, ptok = bank()
        for k in range(8):
            P.pe(lambda e: e.matmul(ps[:T, :ncol], lhsT=xT[:, k * 128:k * 128 + T], rhs=wv[:, k, col:col + ncol],
                                    start=(k == 0), stop=(k == 7)),
                 r=[wtok, xTtok], w=[ptok])
        return ps, ptok

    def run_tiles(items):
        pros = {}
        n_it = len(items)
        for idx, it in enumerate(items):
            if idx == 0:
                for k_ in range(min(2, n_it)):
                    pro_load(items[k_]['src'], items[k_]['T'], k_ % 3, items[k_]['srctok'])
                pros[0] = prologue(items[0]['T'], 0, 0)
            if idx + 2 < n_it:
                n2 = items[idx + 2]
                pro_load(n2['src'], n2['T'], (idx + 2) % 3, n2['srctok'])
            if idx + 1 < n_it:
                pros[idx + 1] = prologue(items[idx + 1]['T'], (idx + 1) % 3, (idx + 1) % 2)
            if it.get('pre') is not None:
                it['pre']()
            it['call'](pros.pop(idx), idx % 2)

    def w3(off, nk, ncol):
        return arena[:, off:off + nk * ncol].rearrange("p (k c) -> p k c", c=ncol)

    def pass_A1(l, halo, km):
        first_tile = (W - halo) // 128
        wa = w3(0, 8, 3328)
        P.barrier()
        P.dma(lambda e: e.dma_start(out=gT[:, :], in_=gmixT[l]), w=['gT'], key='c1')
        for src, dst in ((dwT_in, dwT), (dwb_in, dwb), (lng_in, lng), (lnb_in, lnb), (hng_in, hng)):
            P.dma(lambda e, src=src, dst=dst: e.dma_start(out=dst[:, :], in_=src[l]), w=['lconst'], key='c1', cont=True)
        P.dma(lambda e: e.dma_start(out=lbl[:, 0:4], in_=lbl_in[0]), w=['lconst'], key='c1', cont=True)
        P.dma(lambda e: e.dma_start(out=lbl[:, 4:8], in_=lbl_in[1]), w=['lconst'], key='c1', cont=True)
        c = Carve(8 * 3328)
        Rt = c.f32(2560)
        kms = c.f32(512 + halo)
        P.dma(lambda e: e.dma_start(out=Rt, in_=R_in[l]), w=['R'], key='c1', cont=True)
        P.dma(lambda e: e.dma_start(out=kms, in_=km), w=['km'], key='c1', cont=True)
        if l == 0:
            P.dve(lambda e: e.memset(lb[:, :], 0.0), w=['lb'])
        else:
            P.dve(lambda e: e.tensor_tensor(out=lb[:, :], in0=lbl[:, 4:8], in1=lbl[:, 0:4], op=ALU.subtract),
                  r=['lconst'], w=['lb'])
            P.act(lambda e: e.activation(out=lb[:, :], in_=lb[:, :], func=AF.Exp, scale=-1.0), r=['lb'], w=['lb'])
            P.act(lambda e: e.activation(out=lb[:, :], in_=lb[:, :], func=AF.Ln, bias=epsc[:, 1:2], scale=1.0), r=['lb', 'epsc'], w=['lb'])
            P.act(lambda e: e.activation(out=lb[:, :], in_=lb[:, :], func=AF.Exp, scale=-1.0), r=['lb'], w=['lb'])
        P.dve(lambda e: e.tensor_scalar(out=oml[:, :], in0=lb[:, :], scalar1=-1.0, scalar2=1.0, op0=ALU.mult, op1=ALU.add),
              r=['lb'], w=['oml'])
        for k in range(8):
            load_weight(wa[:, k, :], 'W', w_in[l, k * 128:(k + 1) * 128, 0:3328], 3328, gT[:, k:k + 1], 'gT')
        xn_s[1] = c.bf(1024)
        xnT_s[1] = c.bf(1024)
        diag = c.bf(62 * 128)
        for j in range(2):
            for tap in range(31):
                eng = 'pool' if (tap % 2) else 'dve'
                P.add(eng, lambda e, j=j, tap=tap: e.tensor_scalar(
                    out=diag[:, (j * 31 + tap) * 128:(j * 31 + tap + 1) * 128], in0=ident_f[:, :],
                    scalar1=dwT[:, j * 31 + tap:j * 31 + tap + 1], scalar2=None, op0=ALU.mult),
                    r=['ident_f', 'lconst'], w=['diag'])
        u32 = c.f32(256)
        ubf = c.bf(2 * 160)
        ubf3 = ubf.rearrange("p (j n) -> p j n", n=160)
        sg = [c.f32(128), c.f32(128)]
        hsb = c.f32(256)
        hsq = c.f32(256)
        st4 = c.f32(512)
        hn = [c.f32(128), c.f32(128)]
        feat = c.bf(1024)
        qT = c.bf(256)
        kT32 = c.f32(256)
        kTb = [c.bf(2 * 640), c.bf(2 * 640)]
        Vb = [c.bf(5 * 256), c.bf(5 * 256)]
        v32 = c.f32(256)
        sc = [c.f32(640), c.f32(640)]
        Pbf = [c.bf(640), c.bf(640)]
        PT = [c.bf(640), c.bf(640)]
        obf = c.bf(256)
        at_sm = c.f32(16)
        HS = []
        for s_ in range(4):
            d_ = dict(q32=c.f32(128), ff=c.f32(128), lf=c.f32(128), kk32=c.f32(128), Bc=c.f32(128), nB=c.f32(128),
                      E1=c.f32(128), E2=c.f32(128), qh=c.bf(128), kh=c.bf(128), qt=c.bf(128), Kb=c.bf(128),
                      ATm=[c.bf(128), c.bf(128)], KbT=[c.bf(128), c.bf(128)], osq=c.f32(128), sgl=c.f32(128), on=c.f32(128),
                      rstd_h=c.f32(128), hs=c.f32(8), o32=c.f32(128), sig=c.f32(384))
            HS.append(d_)
        vbf = c.bf(1024)
        S32 = c.f32(512)
        Sbf = c.bf(512)
        FT = ['feat%d' % i for i in range(8)]

        def init_states():
            P.pool(lambda e: e.memset(ubf, 0.0), w=['ubf'])
            P.pool(lambda e: e.memset(kTb[0], 0.0), w=['kTb0'])
            P.pool(lambda e: e.memset(Vb[0], 0.0), w=['Vb0'])
            P.pool(lambda e: e.memset(S32, 0.0), w=['S32_%d' % h for h in range(4)])
            P.pool(lambda e: e.memset(Sbf, 0.0), w=['Sbf_%d' % h for h in range(4)])

        def tile(pro, T, feat_dst, fttok_d, cur, kmoff, state_out, shift):
            KW = 512 + T
            nxt = 1 - cur
            kb, kbtok = kTb[cur], 'kTb%d' % cur
            vb, vbtok = Vb[cur], 'Vb%d' % cur
            x_t, xtok, xT, xTtok = pro
            so = state_out or {}

            def pf(col):
                return proj_feat(wa, 'W', col, T, xT, xTtok)

            def brA():
                for j in range(2):
                    p1, t1 = pf(j * 128)
                    p2, t2 = pf(256 + j * 128)
                    sgj = sg[j]
                    P.act(lambda e: e.activation(out=sgj[:, :T], in_=p2[:, :T], func=AF.Exp, scale=-1.0), r=[t2], w=['sg%d' % j])
                    P.act(lambda e: e.activation(out=sgj[:, :T], in_=sgj[:, :T], func=AF.Ln, bias=epsc[:, 1:2], scale=1.0), r=['sg%d' % j, 'epsc'], w=['sg%d' % j])
                    P.act(lambda e: e.activation(out=sgj[:, :T], in_=sgj[:, :T], func=AF.Exp, scale=-1.0), r=['sg%d' % j], w=['sg%d' % j])
                    P.dve(lambda e: e.tensor_tensor(out=u32[:, j * 128:j * 128 + T], in0=p1[:, :T], in1=sgj[:, :T], op=ALU.mult),
                          r=[t1, 'sg%d' % j], w=['u32_%d' % j])
                    P.pool(lambda e: e.tensor_copy(out=ubf[:, j * 160 + 30:j * 160 + 30 + T], in_=u32[:, j * 128:j * 128 + T]),
                           r=['u32_%d' % j], w=['ubf'])
                yield
                for j in range(2):
                    ps, ptok = bank()
                    for tap in range(31):
                        P.pe(lambda e: e.matmul(ps[:, :T], lhsT=diag[:, (j * 31 + tap) * 128:(j * 31 + tap + 1) * 128],
                                                rhs=ubf[:, j * 160 + tap:j * 160 + tap + T], start=(tap == 0), stop=(tap == 30)),
                             r=['diag', 'ubf'], w=[ptok])
                    P.act(lambda e: e.activation(out=hsb[:, j * 128:j * 128 + T], in_=ps[:, :T], func=AF.Identity,
                                                 bias=dwb[:, j:j + 1], scale=1.0), r=[ptok, 'lconst'], w=['hsb%d' % j])
                    P.act(lambda e: e.activation(out=hsq[:, j * 128:j * 128 + T], in_=ps[:, :T], func=AF.Square,
                                                 bias=dwb[:, j:j + 1], scale=1.0), r=[ptok, 'lconst'], w=['hsq%d' % j])
                if so.get('conv') is not None:
                    for j in range(2):
                        P.dma(lambda e: e.dma_start(out=so['conv'][:, j * 30:(j + 1) * 30],
                                                    in_=u32[:, j * 128 + T - 30:j * 128 + T]), r=['u32_%d' % j], key='so')
                if shift:
                    P.pool(lambda e: e.tensor_copy(out=ubf3[:, :, 0:30], in_=ubf3[:, :, T:T + 30]), r=['ubf'], w=['ubf'])
                yield
                pm, pmt = bank()
                pq, pqt = bank()
                for j in range(2):
                    P.pe(lambda e: e.matmul(pm[:, :T], lhsT=o256[:, :], rhs=hsb[:, j * 128:j * 128 + T], start=(j == 0), stop=(j == 1)),
                         r=['o256', 'hsb%d' % j], w=[pmt])
                for j in range(2):
                    P.pe(lambda e: e.matmul(pq[:, :T], lhsT=o256[:, :], rhs=hsq[:, j * 128:j * 128 + T], start=(j == 0), stop=(j == 1)),
                         r=['o256', 'hsq%d' % j], w=[pqt])
                msb, tt1, var, rstd = st4[:, 0:128], st4[:, 128:256], st4[:, 256:384], st4[:, 384:512]
                P.dve(lambda e: e.tensor_copy(out=msb[:, :T], in_=pm[:, :T]), r=[pmt], w=['msb'])
                P.dve(lambda e: e.tensor_tensor(out=tt1[:, :T], in0=msb[:, :T], in1=msb[:, :T], op=ALU.mult), r=['msb'], w=['tt1'])
                P.dve(lambda e: e.tensor_tensor(out=var[:, :T], in0=pq[:, :T], in1=tt1[:, :T], op=ALU.subtract), r=[pqt, 'tt1'], w=['var'])
                P.act(lambda e: e.activation(out=var[:, :T], in_=var[:, :T], func=AF.Ln, scale=1.0, bias=epsc[:, 0:1]),
                      r=['var', 'epsc'], w=['var'])
                P.act(lambda e: e.activation(out=rstd[:, :T], in_=var[:, :T], func=AF.Exp, scale=-0.5), r=['var'], w=['rstd'])
                for j in range(2):
                    hnj = hn[j]
                    P.pool(lambda e: e.tensor_tensor(out=hnj[:, :T], in0=hsb[:, j * 128:j * 128 + T], in1=msb[:, :T], op=ALU.subtract),
                           r=['hsb%d' % j, 'msb'], w=['hn%d' % j])
                    P.pool(lambda e: e.tensor_tensor(out=hnj[:, :T], in0=hnj[:, :T], in1=rstd[:, :T], op=ALU.mult),
                           r=['hn%d' % j, 'rstd'], w=['hn%d' % j])
                    P.dve(lambda e: e.tensor_scalar(out=hnj[:, :T], in0=hnj[:, :T], scalar1=lng[:, j:j + 1], scalar2=lnb[:, j:j + 1],
                                                    op0=ALU.mult, op1=ALU.add), r=['hn%d' % j, 'lconst'], w=['hn%d' % j])
                    sgj = sg[j]
                    P.act(lambda e: e.activation(out=sgj[:, :T], in_=hnj[:, :T], func=AF.Exp, scale=-1.0), r=['hn%d' % j], w=['sg%d' % j])
                    P.act(lambda e: e.activation(out=sgj[:, :T], in_=sgj[:, :T], func=AF.Ln, bias=epsc[:, 1:2], scale=1.0), r=['sg%d' % j, 'epsc'], w=['sg%d' % j])
                    P.act(lambda e: e.activation(out=sgj[:, :T], in_=sgj[:, :T], func=AF.Exp, scale=-1.0), r=['sg%d' % j], w=['sg%d' % j])
                    P.dve(lambda e: e.tensor_tensor(out=feat[:, j * 128:j * 128 + T], in0=hnj[:, :T], in1=sgj[:, :T], op=ALU.mult),
                          r=['hn%d' % j, 'sg%d' % j], w=[FT[j]])
                yield

            nb = (KW + 127) // 128

            def C_common():
                for j in range(2):
                    pqq, tq = pf(2560 + j * 128)
                    P.dve(lambda e: e.tensor_copy(out=qT[:, j * 128:j * 128 + T], in_=pqq[:, :T]), r=[tq], w=['qT%d' % j])
                    pk, tk_ = pf(2816 + j * 128)
                    P.act(lambda e: e.copy(out=kT32[:, j * 128:j * 128 + T], in_=pk[:, :T]), r=[tk_], w=['kT32_%d' % j])
                    P.pool(lambda e: e.tensor_copy(out=kb[:, j * 640 + 512:j * 640 + 512 + T], in_=kT32[:, j * 128:j * 128 + T]),
                           r=['kT32_%d' % j], w=[kbtok])
                pv, tv = proj_tok(wa, 'W', 3072, 256, T, xT, xTtok)
                P.act(lambda e: e.copy(out=v32[:T, :], in_=pv[:T, 0:256]), r=[tv], w=['v32'])
                P.pool(lambda e: e.tensor_copy(out=vb[:T, 4 * 256:5 * 256], in_=v32[:T, :]), r=['v32'], w=[vbtok])
                if state_out is not None:
                    for j in range(2):
                        P.dma(lambda e: e.dma_start(out=so['k'][j], in_=kT32[:, j * 128:j * 128 + T]), r=['kT32_%d' % j], key='so')
                    P.dma(lambda e: e.dma_start(out=so['v'], in_=v32[:T, :]), r=['v32'], key='so')

            def C_head(h):
                j, pb = h // 2, 64 * (h % 2)
                s_ = h % 2
                scs, Ps, PTs = sc[s_], Pbf[s_], PT[s_]
                sct, Pt_, PTt = 'sc%d' % s_, 'Pbf%d' % s_, 'PT%d' % s_
                ps1, t1 = bank()
                ps2, t2 = bank()
                P.pe(lambda e: e.matmul(ps1[:T, 0:512], lhsT=qT[pb:pb + 64, j * 128:j * 128 + T],
                                        rhs=kb[pb:pb + 64, j * 640:j * 640 + 512], start=True, stop=True),
                     r=['qT%d' % j, kbtok], w=[t1])
                P.pe(lambda e: e.matmul(ps2[:T, 0:T], lhsT=qT[pb:pb + 64, j * 128:j * 128 + T],
                                        rhs=kb[pb:pb + 64, j * 640 + 512:j * 640 + 512 + T], start=True, stop=True),
                     r=['qT%d' % j, kbtok], w=[t2])
                P.dve(lambda e: e.scalar_tensor_tensor(out=scs[:T, 0:512], in0=ps1[:T, 0:512], scalar=0.125,
                                                       in1=Rt[:T, h * 640:h * 640 + 512], op0=ALU.mult, op1=ALU.add),
                      r=[t1, 'R'], w=[sct])
                P.dve(lambda e: e.scalar_tensor_tensor(out=scs[:T, 512:512 + T], in0=ps2[:T, 0:T], scalar=0.125,
                                                       in1=Rt[:T, h * 640 + 512:h * 640 + 512 + T], op0=ALU.mult, op1=ALU.add),
                      r=[t2, 'R'], w=[sct])
                yield
                if kmoff is not None:
                    wd = min(KW, 512 + halo - kmoff)
                    P.pool(lambda e: e.tensor_tensor(out=scs[:T, 0:wd], in0=scs[:T, 0:wd], in1=kms[:T, kmoff:kmoff + wd], op=ALU.add),
                           r=[sct, 'km'], w=[sct])
                    yield
                P.dve(lambda e: e.reduce_max(out=at_sm[:T, h:h + 1], in_=scs[:T, 0:KW], axis=AX.X), r=[sct], w=['mx%d' % h])
                P.dve(lambda e: e.tensor_scalar(out=at_sm[:T, 4 + h:5 + h], in0=at_sm[:T, h:h + 1], scalar1=-1.0, scalar2=None, op0=ALU.mult),
                      r=['mx%d' % h], w=['nmx%d' % h])
                yield
                P.act(lambda e: e.activation(out=Ps[:T, 0:KW], in_=scs[:T, 0:KW], func=AF.Exp, bias=at_sm[:T, 4 + h:5 + h], scale=1.0,
                                             accum_out=at_sm[:T, 8 + h:9 + h]), r=[sct, 'nmx%d' % h], w=[Pt_, 'rsum%d' % h])
                yield
                P.dve(lambda e: e.reciprocal(out=at_sm[:T, 12 + h:13 + h], in_=at_sm[:T, 8 + h:9 + h]), r=['rsum%d' % h], w=['rinv%d' % h])
                pt1, tt1_ = bank()
                pt2, tt2_ = bank()
                for jb in range(nb):
                    wb = min(128, KW - 128 * jb)
                    dst = pt1[:wb, jb * 128:jb * 128 + T] if jb < 4 else pt2[:wb, 0:T]
                    P.pe(lambda e: e.matmul(dst, lhsT=Ps[:T, jb * 128:jb * 128 + wb], rhs=ident_b[:T, :T], start=True, stop=True),
                         r=[Pt_, 'ident_b'], w=[tt1_ if jb < 4 else tt2_])
                P.act(lambda e: e.copy(out=PTs[:, 0:512], in_=pt1[:, 0:512]), r=[tt1_], w=[PTt])
                wl = KW - 512
                P.dve(lambda e: e.tensor_copy(out=PTs[:wl, 512:512 + T], in_=pt2[:wl, 0:T]), r=[tt2_], w=[PTt])
                yield
                po, pot = bank()
                for jb in range(nb):
                    wb = min(128, KW - 128 * jb)
                    P.pe(lambda e: e.matmul(po[:T, 0:64], lhsT=PTs[:wb, jb * 128:jb * 128 + T],
                                            rhs=vb[:wb, jb * 256 + h * 64:jb * 256 + (h + 1) * 64],
                                            start=(jb == 0), stop=(jb == nb - 1)),
                         r=[PTt, vbtok], w=[pot])
                P.act(lambda e: e.activation(out=obf[:T, h * 64:(h + 1) * 64], in_=po[:T, 0:64], func=AF.Copy,
                                             scale=at_sm[:T, 12 + h:13 + h]), r=[pot, 'rinv%d' % h], w=['obf%d' % h])
                yield

            def C_final():
                pc, pct = bank()
                for j in range(2):
                    P.pe(lambda e: e.matmul(pc[:, j * 128:j * 128 + T], lhsT=obf[:T, j * 128:(j + 1) * 128], rhs=ident_b[:T, :T],
                                            start=True, stop=True), r=['obf%d' % (2 * j), 'obf%d' % (2 * j + 1), 'ident_b'], w=[pct])
                P.dve(lambda e: e.tensor_copy(out=feat[:, 768:1024].rearrange("p (j t) -> p j t", t=128)[:, :, :T],
                                              in_=pc[:, 0:256].rearrange("p (j t) -> p j t", t=128)[:, :, :T]), r=[pct], w=[FT[6], FT[7]])
                if shift:
                    P.pool(lambda e: e.tensor_copy(out=kTb[nxt].rearrange("p (j n) -> p j n", n=640)[:, :, 0:512],
                                                   in_=kb.rearrange("p (j n) -> p j n", n=640)[:, :, 128:640]),
                           r=[kbtok], w=['kTb%d' % nxt])
                    P.pool(lambda e: e.tensor_copy(out=Vb[nxt][:, 0:1024], in_=vb[:, 256:1280]), r=[vbtok], w=['Vb%d' % nxt])

            chunks = [(0, 64), (64, 64)] if T == 128 else [(0, T)]

            def B_common():
                for ci, (c0, L) in enumerate(chunks):
                    pvv, tvv = bank()
                    for k in range(8):
                        P.pe(lambda e: e.matmul(pvv[:L, 0:512], lhsT=xT[:, k * 128 + c0:k * 128 + c0 + L],
                                                rhs=wa[:, k, 1536:2048], start=(k == 0), stop=(k == 7)),
                             r=['W', xTtok], w=[tvv])
                    if ci == 0:
                        P.act(lambda e: e.copy(out=vbf[:L, ci * 512:(ci + 1) * 512], in_=pvv[:L, 0:512]), r=[tvv], w=['vbf%d' % ci])
                    else:
                        P.dve(lambda e: e.tensor_copy(out=vbf[:L, ci * 512:(ci + 1) * 512], in_=pvv[:L, 0:512]), r=[tvv], w=['vbf%d' % ci])

            def B_head(h):
                s_ = h
                B_ = HS[s_]
                q32, ff, lf, kk32, Bc, nB, E1, E2 = (B_[n_] for n_ in ('q32', 'ff', 'lf', 'kk32', 'Bc', 'nB', 'E1', 'E2'))
                qh, kh, qt, Kb, osq, sgl, on, rstd_h, hs, o32 = (B_[n_] for n_ in ('qh', 'kh', 'qt', 'Kb', 'osq', 'sgl', 'on', 'rstd_h', 'hs', 'o32'))

                def tk(n_):
                    return '%s_%d' % (n_, s_)
                ps3, t3 = bank()
                for gi, col in enumerate((512 + h * 128, 1024 + h * 128, 2048 + h * 128)):
                    for k in range(8):
                        P.pe(lambda e: e.matmul(ps3[:, gi * 128:gi * 128 + T], lhsT=wa[:, k, col:col + 128], rhs=xT[:, k * 128:k * 128 + T],
                                                start=(k == 0), stop=(k == 7)), r=['W', xTtok], w=[t3])
                sig = B_['sig']
                sig3 = sig.rearrange("p (g t) -> p g t", t=128)[:, :, :T]
                ps33 = ps3[:, 0:384].rearrange("p (g t) -> p g t", t=128)[:, :, :T]
                P.act(lambda e: e.activation(out=sig3, in_=ps33, func=AF.Exp, scale=-1.0), r=[t3], w=[tk('sig')])
                P.dve(lambda e: e.tensor_copy(out=q32[:, :T], in_=ps3[:, 0:T]), r=[t3], w=[tk('q32')])
                P.dve(lambda e: e.tensor_copy(out=sgl[:, :T], in_=ps3[:, 256:256 + T]), r=[t3], w=[tk('sgl')])
                yield
                P.act(lambda e: e.activation(out=sig3, in_=sig3, func=AF.Ln, bias=epsc[:, 1:2], scale=1.0), r=[tk('sig'), 'epsc'], w=[tk('sig')])
                yield
                P.act(lambda e: e.activation(out=sig3, in_=sig3, func=AF.Exp, scale=-1.0), r=[tk('sig')], w=[tk('sig')])
                yield
                P.dve(lambda e: e.tensor_tensor(out=q32[:, :T], in0=q32[:, :T], in1=sig[:, 0:T], op=ALU.mult),
                      r=[tk('q32'), tk('sig')], w=[tk('q32')])
                P.dve(lambda e: e.tensor_scalar(out=ff[:, :T], in0=sig[:, 128:128 + T], scalar1=oml[:, h:h + 1], scalar2=lb[:, h:h + 1],
                                                op0=ALU.mult, op1=ALU.add), r=[tk('sig'), 'oml', 'lb'], w=[tk('ff')])
                P.dve(lambda e: e.tensor_tensor(out=sgl[:, :T], in0=sgl[:, :T], in1=sig[:, 256:256 + T], op=ALU.mult),
                      r=[tk('sgl'), tk('sig')], w=[tk('sgl')])
                yield
                P.act(lambda e: e.activation(out=lf[:, :T], in_=ff[:, :T], func=AF.Ln), r=[tk('ff')], w=[tk('lf')])
                P.dve(lambda e: e.tensor_scalar(out=kk32[:, :T], in0=ff[:, :T], scalar1=-1.0, scalar2=1.0, op0=ALU.mult, op1=ALU.add),
                      r=[tk('ff')], w=[tk('kk32')])
                yield
                P.dve(lambda e: e.tensor_tensor_scan(out=Bc[:, :T], data0=ones_f[:, :T], data1=lf[:, :T], initial=0.0,
                                                     op0=ALU.mult, op1=ALU.add), r=['ones_f', tk('lf')], w=[tk('Bc')])
                yield
                P.pool(lambda e: e.tensor_tensor(out=nB[:, :T], in0=Bc[:, :T], in1=negones[:, :T], op=ALU.mult),
                       r=[tk('Bc'), 'negones'], w=[tk('nB')])
                yield
                for ci, (c0, L) in enumerate(chunks):
                    r_ = c0 + L // 2 - 1
                    en = c0 + L - 1
                    sl = slice(c0, c0 + L)
                    hc = hs[:, 4 * ci:4 * ci + 4]
                    hct = tk('hs%d' % ci)
                    ATm, KbT = B_['ATm'][ci], B_['KbT'][ci]
                    att, kbt = tk('ATm%d' % ci), tk('KbT%d' % ci)
                    P.act(lambda e: e.activation(out=E1[:, sl], in_=Bc[:, sl], func=AF.Exp, bias=nB[:, r_:r_ + 1], scale=1.0),
                          r=[tk('Bc'), tk('nB'), tk('q32')], w=[tk('E1')])
                    P.act(lambda e: e.activation(out=E2[:, sl], in_=Bc[:, sl], func=AF.Exp, bias=Bc[:, r_:r_ + 1], scale=-1.0),
                          r=[tk('Bc')], w=[tk('E2')])
                    if c0 == 0:
                        P.act(lambda e: e.activation(out=hc[:, 0:1], in_=Bc[:, r_:r_ + 1], func=AF.Exp), r=[tk('Bc')], w=[hct])
                        P.act(lambda e: e.activation(out=hc[:, 2:3], in_=Bc[:, en:en + 1], func=AF.Exp), r=[tk('Bc')], w=[hct])
                    else:
                        P.act(lambda e: e.activation(out=hc[:, 0:1], in_=Bc[:, r_:r_ + 1], func=AF.Exp,
                                                     bias=nB[:, c0 - 1:c0], scale=1.0), r=[tk('Bc'), tk('nB')], w=[hct])
                        P.act(lambda e: e.activation(out=hc[:, 2:3], in_=Bc[:, en:en + 1], func=AF.Exp,
                                                     bias=nB[:, c0 - 1:c0], scale=1.0), r=[tk('Bc'), tk('nB')], w=[hct])
                    P.act(lambda e: e.activation(out=hc[:, 1:2], in_=Bc[:, en:en + 1], func=AF.Exp,
                                                 bias=nB[:, r_:r_ + 1], scale=1.0), r=[tk('Bc'), tk('nB')], w=[hct])
                    yield
                    P.dve(lambda e: e.tensor_tensor(out=qh[:, sl], in0=q32[:, sl], in1=E1[:, sl], op=ALU.mult), r=[tk('q32'), tk('E1')], w=[tk('qh')])
                    P.pool(lambda e: e.tensor_tensor(out=kh[:, sl], in0=kk32[:, sl], in1=E2[:, sl], op=ALU.mult), r=[tk('kk32'), tk('E2')], w=[tk('kh')])
                    P.dve(lambda e: e.scalar_tensor_tensor(out=qt[:, sl], in0=E1[:, sl], scalar=hc[:, 0:1], in1=q32[:, sl],
                                                           op0=ALU.mult, op1=ALU.mult), r=[tk('E1'), hct, tk('q32')], w=[tk('qt')])
                    P.dve(lambda e: e.scalar_tensor_tensor(out=Kb[:, sl], in0=E2[:, sl], scalar=hc[:, 1:2], in1=kk32[:, sl],
                                                           op0=ALU.mult, op1=ALU.mult), r=[tk('E2'), hct, tk('kk32')], w=[tk('Kb')])
                    yield
                    pa, pat = bank()
                    P.pe(lambda e: e.matmul(pa[:L, :L], lhsT=kh[:, sl], rhs=qh[:, sl], start=True, stop=True),
                         r=[tk('kh'), tk('qh')], w=[pat])
                    pkt, pktt = bank()
                    P.pe(lambda e: e.matmul(pkt[:L, 0:128], lhsT=Kb[:, sl], rhs=ident_b[:, :], start=True, stop=True),
                         r=[tk('Kb'), 'ident_b'], w=[pktt])
                    P.dve(lambda e: e.tensor_tensor(out=ATm[:L, :L], in0=pa[:L, :L], in1=maskT[:L, :L], op=ALU.mult),
                          r=[pat, 'maskT'], w=[att])
                    P.dve(lambda e: e.tensor_copy(out=KbT[:L, :], in_=pkt[:L, 0:128]), r=[pktt], w=[kbt])
                    yield
                    poh, poht = bank()
                    P.pe(lambda e: e.matmul(poh[:, 0:L], lhsT=vbf[:L, ci * 512 + h * 128:ci * 512 + (h + 1) * 128],
                                            rhs=ATm[:L, :L], start=True, stop=False),
                         r=['vbf%d' % ci, att], w=[poht])
                    P.pe(lambda e: e.matmul(poh[:, 0:L], lhsT=Sbf[:, h * 128:(h + 1) * 128], rhs=qt[:, sl], start=False, stop=True),
                         r=['Sbf_%d' % h, tk('qt')], w=[poht])
                    pds, pdst = bank()
                    P.pe(lambda e: e.matmul(pds[:, 0:128], lhsT=KbT[:L, :],
                                            rhs=vbf[:L, ci * 512 + h * 128:ci * 512 + (h + 1) * 128], start=True, stop=True),
                         r=[kbt, 'vbf%d' % ci], w=[pdst])
                    P.dve(lambda e: e.tensor_copy(out=o32[:, sl], in_=poh[:, 0:L]), r=[poht], w=[tk('o32')])
                    P.dve(lambda e: e.scalar_tensor_tensor(out=S32[:, h * 128:(h + 1) * 128], in0=S32[:, h * 128:(h + 1) * 128],
                                                           scalar=hc[:, 2:3], in1=pds[:, 0:128], op0=ALU.mult, op1=ALU.add),
                          r=['S32_%d' % h, hct, pdst], w=['S32_%d' % h])
                    yield
                    P.pool(lambda e: e.tensor_copy(out=Sbf[:, h * 128:(h + 1) * 128], in_=S32[:, h * 128:(h + 1) * 128]),
                           r=['S32_%d' % h], w=['Sbf_%d' % h])
                    yield
                P.pool(lambda e: e.tensor_tensor(out=osq[:, :T], in0=o32[:, :T], in1=o32[:, :T], op=ALU.mult), r=[tk('o32')], w=[tk('osq')])
                yield
                pms, pmst = bank()
                P.pe(lambda e: e.matmul(pms[:, :T], lhsT=o128[:, :], rhs=osq[:, :T], start=True, stop=True), r=['o128', tk('osq')], w=[pmst])
                P.act(lambda e: e.activation(out=on[:, :T], in_=pms[:, :T], func=AF.Ln, scale=1.0, bias=epsc[:, 0:1]),
                      r=[pmst, 'epsc'], w=[tk('on')])
                yield
                P.act(lambda e: e.activation(out=rstd_h[:, :T], in_=on[:, :T], func=AF.Exp, scale=-0.5), r=[tk('on')], w=[tk('rstd_h')])
                yield
                P.dve(lambda e: e.tensor_tensor(out=on[:, :T], in0=o32[:, :T], in1=rstd_h[:, :T], op=ALU.mult),
                      r=[tk('o32'), tk('rstd_h')], w=[tk('on')])
                yield
                P.pool(lambda e: e.tensor_tensor(out=feat[:, 256 + h * 128:256 + h * 128 + T], in0=on[:, :T], in1=sgl[:, :T], op=ALU.mult),
                       r=[tk('on'), tk('sgl')], w=[FT[2 + h]])
                yield

            C_common()
            B_common()
            def lane(*gs):
                for g_ in gs:
                    yield from g_
            gens = [brA(), lane(C_head(0), C_head(2)), lane(C_head(1), C_head(3))] + [B_head(h) for h in range(4)]
            while gens:
                for g in list(gens):
                    try:
                        next(g)
                    except StopIteration:
                        gens.remove(g)
            C_final()
            if so.get('hgrn') is not None:
                P.dma(lambda e: e.dma_start(out=so['hgrn'], in_=S32), r=['S32_%d' % h for h in range(4)], key='so')
            P.dma(lambda e: e.dma_start(out=feat_dst, in_=feat), r=FT, w=[fttok_d], key='fo')

        P.barrier()
        init_states()
        items = []
        cur = 0
        for i in range(first_tile, NT):
            src = (xp if l == 0 else x1_d)[i * 128:(i + 1) * 128, :]
            kmoff = (i - first_tile) * 128
            if kmoff >= 512 + halo:
                kmoff = None
            so = None
            if i >= NT - 4:
                q = i - (NT - 4)
                so = dict(k=[kp_o[l][:, j * 512 + q * 128:j * 512 + (q + 1) * 128] for j in range(2)],
                          v=vp_o[l][q * 128:(q + 1) * 128, :])
                if i == NT - 1:
                    so['conv'] = convp_o[l]
                    so['hgrn'] = hgrnp_o[l]
            items.append(dict(src=src, srctok=(('x1', 'p', i) if l == 1 else None), T=128,
                              call=(lambda pro, slot, i=i, cur=cur, kmoff=kmoff, so=so:
                                    tile(pro, 128, ft_d[i], ('ft', 'p', i), cur, kmoff, so, True))))
            cur = 1 - cur
        S_all = ['S32_%d' % h for h in range(4)]
        Sb_all = ['Sbf_%d' % h for h in range(4)]

        def load_states(n):
            P.dma(lambda e: e.dma_start(out=stg[0][:, 0:60], in_=sconvT[l, n]), w=['stg0'], key='si')
            P.pool(lambda e: e.tensor_copy(out=ubf3[:, :, 0:30], in_=stg[0][:, 0:60].rearrange("p (j n) -> p j n", n=30)),
                   r=['stg0'], w=['ubf'])
            P.dma(lambda e: e.dma_start(out=S32, in_=shgrn[l, n]), w=S_all, key='si')
            P.pool(lambda e: e.tensor_copy(out=Sbf, in_=S32), r=S_all, w=Sb_all)
            P.dma(lambda e: e.dma_start(out=stg[1][:, 0:1024], in_=skT[l, n]), w=['stg1'], key='si')
            P.pool(lambda e: e.tensor_copy(out=kTb[0].rearrange("p (j n) -> p j n", n=640)[:, :, 0:512],
                                           in_=stg[1][:, 0:1024].rearrange("p (j n) -> p j n", n=512)), r=['stg1'], w=['kTb0'])
            P.dma(lambda e: e.dma_start(out=stg[0][:, 0:1024], in_=sv[l, n]), w=['stg0'], key='si')
            P.pool(lambda e: e.tensor_copy(out=Vb[0][:, 0:1024], in_=stg[0][:, 0:1024]), r=['stg0'], w=['Vb0'])

        for n in range(NSEQ):
            src = (xs if l == 0 else x1s_d)[n * TS:(n + 1) * TS, :]
            so = dict(conv=convs_o[l, n], hgrn=hgrns_o[l, n], k=[ks_o[l, n][:, j * 32:(j + 1) * 32] for j in range(2)], v=vs_o[l, n])
            items.append(dict(src=src, srctok=(('x1', 's', n) if l == 1 else None), T=TS,
                              pre=(lambda n=n: load_states(n)),
                              call=(lambda pro, slot, n=n, so=so: tile(pro, TS, fts_d[n], ('ft', 's', n), 0, None, so, False))))
        run_tiles(items)

    def pass_A2(l, halo):
        first_tile = (W - halo) // 128
        wg = w3(0, 8, 3072)
        wco = w3(24576, 2, 1024)
        whg = w3(24576 + 2048, 4, 1024)
        wat = w3(24576 + 2048 + 4096, 2, 1024)
        wmx = w3(24576 + 2048 + 4096 + 2048, 8, 1024)
        P.barrier()
        P.dma(lambda e: e.dma_start(out=gT[:, :], in_=gmixT[l]), w=['gT'], key='c1')
        P.dma(lambda e: e.dma_start(out=hng[:, :], in_=hng_in[l]), w=['lconst'], key='c1', cont=True)
        for k in range(8):
            load_weight(wg[:, k, :], 'W', w_in[l, k * 128:(k + 1) * 128, 3328:6400], 3072, gT[:, k:k + 1], 'gT')
        for j in range(2):
            load_weight(wco[:, j, :], 'W', w_conv_out[l, j * 128:(j + 1) * 128, :], 1024)
        for h in range(4):
            load_weight(whg[:, h, :], 'W', w_hgrn_out[l, h * 128:(h + 1) * 128, :], 1024, hng[:, h:h + 1], 'lconst')
        for j in range(2):
            load_weight(wat[:, j, :], 'W', w_attn_out[l, j * 128:(j + 1) * 128, :], 1024)
        for k in range(8):
            load_weight(wmx[:, k, :], 'W', w_mix_out[l, k * 128:(k + 1) * 128, :], 1024)
        c = Carve(24576 + 2048 + 4096 + 2048 + 8192)
        xn_s[1] = c.bf(1024)
        xnT_s[1] = c.bf(1024)
        ftile = [c.bf(1024), c.bf(1024)]
        gsb = [c.f32(512) for _ in range(3)]
        tmp = [c.f32(512) for _ in range(2)]
        m32 = [c.f32(1024), c.f32(1024)]
        mbf = [c.bf(1024), c.bf(1024)]
        mT = [c.bf(1024), c.bf(1024)]
        hout_s = [hout, c.f32(1024)]
        branches = ((0, wco, (0, 1)), (1024, whg, (2, 3, 4, 5)), (2048, wat, (6, 7)))

        def tile(pro, fsrc, fsrctok, dst_ap, dsttok, T, slot, vcol):
            x_t, xtok, xT, xTtok = pro
            ft, fttok = ftile[slot], 'ft%d' % slot
            m32_, mbf_, mT_, ho = m32[slot], mbf[slot], mT[slot], hout_s[slot]
            m32t, mbft, mTt, hot = 'm32_%d' % slot, 'mbf%d' % slot, 'mT%d' % slot, 'hout%d' % slot
            P.dma(lambda e: e.dma_start(out=ft, in_=fsrc), r=[fsrctok], w=[fttok], key=fttok)
            for cb in range(2):
                for bi, (goff, wv, chunks) in enumerate(branches):
                    pg, tg = proj_tok(wg, 'W', goff + cb * 512, 512, T, xT, xTtok)
                    g_, gt = gsb[bi], 'gsb%d' % bi
                    P.act(lambda e: e.activation(out=g_[:T, :], in_=pg[:T, :], func=AF.Sigmoid), r=[tg], w=[gt])
                    py, ty = bank()
                    n = len(chunks)
                    for ci, ch in enumerate(chunks):
                        P.pe(lambda e: e.matmul(py[:T, :], lhsT=ft[:, ch * 128:ch * 128 + T], rhs=wv[:, ci, cb * 512:(cb + 1) * 512],
                                                start=(ci == 0), stop=(ci == n - 1)), r=[fttok, 'W'], w=[ty])
                    mc = m32_[:T, cb * 512:(cb + 1) * 512]
                    mct = '%s_%d' % (m32t, cb)
                    if bi == 0:
                        P.dve(lambda e: e.tensor_tensor(out=mc, in0=py[:T, :], in1=g_[:T, :], op=ALU.mult), r=[ty, gt], w=[mct])
                    else:
                        t_, tt = tmp[bi - 1], 'tmp%d' % (bi - 1)
                        P.dve(lambda e: e.tensor_tensor(out=t_[:T, :], in0=py[:T, :], in1=g_[:T, :], op=ALU.mult), r=[ty, gt], w=[tt])
                        if bi == 1:
                            P.pool(lambda e: e.tensor_tensor(out=mc, in0=mc, in1=t_[:T, :], op=ALU.add), r=[mct, tt], w=[mct])
                        else:
                            P.pool(lambda e: e.tensor_tensor(out=mbf_[:T, cb * 512:(cb + 1) * 512], in0=mc, in1=t_[:T, :], op=ALU.add),
                                   r=[mct, tt], w=['%s_%d' % (mbft, cb)])
            for half in range(2):
                ps, ptok = bank()
                for kk in range(4):
                    k = half * 4 + kk
                    P.pe(lambda e: e.matmul(ps[:, kk * 128:kk * 128 + T], lhsT=mbf_[:T, k * 128:(k + 1) * 128],
                                            rhs=ident_b[:T, :T], start=True, stop=True), r=['%s_%d' % (mbft, half), 'ident_b'], w=[ptok])
                dst = mT_[:, half * 512:(half + 1) * 512].rearrange("p (k t) -> p k t", t=128)[:, :, :T]
                srcp = ps[:, :].rearrange("p (k t) -> p k t", t=128)[:, :, :T]
                if half == 0:
                    P.dve(lambda e: e.tensor_copy(out=dst, in_=srcp), r=[ptok], w=[mTt])
                else:
                    P.act(lambda e: e.copy(out=dst, in_=srcp), r=[ptok], w=[mTt])
            for cb in range(2):
                ps, ptok = bank()
                for k in range(8):
                    P.pe(lambda e: e.matmul(ps[:T, :], lhsT=mT_[:, k * 128:k * 128 + T], rhs=wmx[:, k, cb * 512:(cb + 1) * 512],
                                            start=(k == 0), stop=(k == 7)), r=[mTt, 'W'], w=[ptok])
                P.dve(lambda e: e.tensor_tensor(out=ho[:T, cb * 512:(cb + 1) * 512], in0=ps[:T, :], in1=x_t[:T, cb * 512:(cb + 1) * 512],
                                                op=ALU.add), r=[ptok, xtok], w=[hot])
            if vcol is not None:
                P.dve(lambda e: e.tensor_scalar(out=ho[:T, :], in0=ho[:T, :], scalar1=valid_sb[:T, vcol:vcol + 1], scalar2=None, op0=ALU.mult),
                       r=[hot, 'valid_sb'], w=[hot])
            P.dma(lambda e: e.dma_start(out=dst_ap, in_=ho[:T, :]), r=[hot], w=[dsttok], key='ho%d' % slot)

        P.barrier()
        items = []
        for i in range(first_tile, NT):
            src = (xp if l == 0 else x1_d)[i * 128:(i + 1) * 128, :]
            items.append(dict(src=src, srctok=(('x1', 'p', i) if l == 1 else None), T=128,
                              call=(lambda pro, slot, i=i: tile(pro, ft_d[i], ('ft', 'p', i), hm_d[i * 128:(i + 1) * 128, :], ('hm', 'p', i),
                                                                128, slot, i if i < W // 128 else None))))
        for n in range(NSEQ):
            src = (xs if l == 0 else x1s_d)[n * TS:(n + 1) * TS, :]
            items.append(dict(src=src, srctok=(('x1', 's', n) if l == 1 else None), T=TS,
                              call=(lambda pro, slot, n=n: tile(pro, fts_d[n], ('ft', 's', n), hms_d[n * TS:(n + 1) * TS, :], ('hm', 's', n),
                                                                TS, slot, None))))
        run_tiles(items)

    def pass_B(l, halo):
        first_tile = (W - halo) // 128
        wup = w3(0, 8, 5632)
        wdn = w3(45056, 22, 1024)
        P.barrier()
        P.dma(lambda e: e.dma_start(out=gT[:, :], in_=gffnT[l]), w=['gT'], key='c1')
        P.dma(lambda e: e.dma_start(out=fdw[:, :], in_=fdw_in[l]), w=['lconst'], key='c1', cont=True)
        for k in range(8):
            load_weight(wup[:, k, :], 'W', w_ffn_up[l, k * 128:(k + 1) * 128, :], 5632, gT[:, k:k + 1], 'gT')
        for cch in range(22):
            load_weight(wdn[:, cch, :], 'W', w_ffn_down[l, cch * 128:(cch + 1) * 128, :], 1024)
        c = Carve()
        ub = c.f32(44 * 130)
        ub3 = ub.rearrange("p (c n) -> p c n", n=130)
        t1b = [[c.f32(128), c.f32(128)] for _ in range(2)]
        ucb = [[c.f32(128), c.f32(128)] for _ in range(2)]
        sa = [c.f32(128), c.f32(128)]
        gTt = c.bf(22 * 128)
        xn_s[1] = xn_s[0]
        xn_tok[1] = 'xn0'
        xnT_s[1] = c.bf(1024)
        UB = [('ub', cc) for cc in range(44)]

        def tile(pro, dst_ap, dsttok, T, final_out, state_out):
            x_t, xtok, xT, xTtok = pro

            def S1(c2):
                st = c2 % 2
                for half in range(2):
                    cc = c2 + 22 * half
                    ps, ptok = proj_feat(wup, 'W', cc * 128, T, xT, xTtok)
                    t1, t1t = t1b[st][half], 't1_%d_%d' % (st, half)
                    P.act(lambda e: e.copy(out=ub[:, cc * 130 + 2:cc * 130 + 2 + T], in_=ps[:, :T]), r=[ptok], w=[('ub', cc)])
                    P.pool(lambda e: e.tensor_tensor(out=t1[:, :T], in0=ub[:, cc * 130:cc * 130 + T],
                                                     in1=fdw[:, cc * 3:cc * 3 + 1].to_broadcast([128, T]), op=ALU.mult),
                           r=[('ub', cc), 'lconst'], w=[t1t])

            def S2(c2):
                st = c2 % 2
                for half in range(2):
                    cc = c2 + 22 * half
                    t1, t1t = t1b[st][half], 't1_%d_%d' % (st, half)
                    uc, uct = ucb[st][half], 'uc_%d_%d' % (st, half)
                    P.dve(lambda e: e.scalar_tensor_tensor(out=t1[:, :T], in0=ub[:, cc * 130 + 1:cc * 130 + 1 + T],
                                                           scalar=fdw[:, cc * 3 + 1:cc * 3 + 2], in1=t1[:, :T], op0=ALU.mult, op1=ALU.add),
                          r=[('ub', cc), 'lconst', t1t], w=[t1t])
                    P.dve(lambda e: e.scalar_tensor_tensor(out=uc[:, :T], in0=ub[:, cc * 130 + 2:cc * 130 + 2 + T],
                                                           scalar=fdw[:, cc * 3 + 2:cc * 3 + 3], in1=t1[:, :T], op0=ALU.mult, op1=ALU.add),
                          r=[('ub', cc), 'lconst', t1t], w=[uct])
                    if state_out is None:
                        P.act(lambda e: e.copy(out=ub[:, cc * 130:cc * 130 + 2], in_=ub[:, cc * 130 + T:cc * 130 + T + 2]),
                              r=[('ub', cc)], w=[('ub', cc)])

            def S3(c2):
                st = c2 % 2
                sa_, sat = sa[st], 'sa%d' % st
                P.act(lambda e: e.activation(out=sa_[:, :T], in_=ucb[st][0][:, :T], func=AF.Silu), r=['uc_%d_0' % st], w=[sat])
                P.pool(lambda e: e.tensor_tensor(out=gTt[:, c2 * 128:c2 * 128 + T], in0=sa_[:, :T], in1=ucb[st][1][:, :T], op=ALU.mult),
                       r=[sat, 'uc_%d_1' % st], w=[('gTt', c2)])

            for step in range(22 + 2):
                if step < 22:
                    S1(step)
                if 1 <= step < 23:
                    S2(step - 1)
                if step >= 2:
                    S3(step - 2)
            if state_out is not None:
                P.dma(lambda e: e.dma_start(out=state_out.rearrange("p (c n) -> p c n", n=2), in_=ub3[:, :, T:T + 2]), r=UB, key='so')
            for cb in range(2):
                ps, ptok = bank()
                for c2 in range(22):
                    P.pe(lambda e: e.matmul(ps[:T, :], lhsT=gTt[:, c2 * 128:c2 * 128 + T], rhs=wdn[:, c2, cb * 512:(cb + 1) * 512],
                                            start=(c2 == 0), stop=(c2 == 21)), r=[('gTt', c2), 'W'], w=[ptok])
                P.dve(lambda e: e.tensor_tensor(out=hout[:T, cb * 512:(cb + 1) * 512], in0=ps[:T, :], in1=x_t[:T, cb * 512:(cb + 1) * 512],
                                                op=ALU.add), r=[ptok, xtok], w=['hout'])
            if final_out is None:
                P.dma(lambda e: e.dma_start(out=dst_ap, in_=hout[:T, :]), r=['hout'], w=[dsttok], key='ho')
            elif final_out is not False:
                P.act(lambda e: e.activation(out=junk[:T, :], in_=hout[:T, :], func=AF.Square, accum_out=sm[:T, 8:9]),
                      r=['hout'], w=['junk', 'sm8'])
                P.act(lambda e: e.activation(out=sm[:T, 9:10], in_=sm[:T, 8:9], func=AF.Ln, scale=1.0 / D, bias=epsc[:T, 0:1]),
                      r=['sm8', 'epsc'], w=['sm9'])
                P.act(lambda e: e.activation(out=sm[:T, 10:11], in_=sm[:T, 9:10], func=AF.Exp, scale=-0.5), r=['sm9'], w=['sm10'])
                P.dve(lambda e: e.scalar_tensor_tensor(out=hout[:T, :], in0=hout[:T, :], scalar=sm[:T, 10:11], in1=gfin[:T, :],
                                                       op0=ALU.mult, op1=ALU.mult), r=['hout', 'sm10', 'gfin'], w=['hout'])
                P.dma(lambda e: e.dma_start(out=final_out, in_=hout[:T, :]), r=['hout'], key='ho')

        P.barrier()
        P.pool(lambda e: e.memset(ub, 0.0), w=UB)
        items = []
        for i in range(first_tile, NT):
            src = hm_d[i * 128:(i + 1) * 128, :]
            so = ffnp_o[l] if i == NT - 1 else None
            if l == 0:
                call = (lambda pro, slot, i=i, so=so: tile(pro, x1_d[i * 128:(i + 1) * 128, :], ('x1', 'p', i), 128, None, so))
            else:
                fo = y_o[(i - W // 128) * 128:(i - W // 128 + 1) * 128, :] if i >= W // 128 else False
                call = (lambda pro, slot, fo=fo, so=so: tile(pro, None, None, 128, fo, so))
            items.append(dict(src=src, srctok=('hm', 'p', i), T=128, call=call))

        def load_ffn_state(n):
            P.dma(lambda e: e.dma_start(out=stg[0][:, 0:88], in_=sffnT[l, n]), w=['stg0'], key='si')
            P.pool(lambda e: e.tensor_copy(out=ub3[:, :, 0:2], in_=stg[0][:, 0:88].rearrange("p (c n) -> p c n", n=2)), r=['stg0'], w=UB)

        for n in range(NSEQ):
            src = hms_d[n * TS:(n + 1) * TS, :]
            if l == 0:
                call = (lambda pro, slot, n=n: tile(pro, x1s_d[n * TS:(n + 1) * TS, :], ('x1', 's', n), TS, None, ffns_o[l, n]))
            else:
                call = (lambda pro, slot, n=n: tile(pro, None, None, TS, ys_o[n * TS:(n + 1) * TS, :], ffns_o[l, n]))
            items.append(dict(src=src, srctok=('hm', 's', n), T=TS, pre=(lambda n=n: load_ffn_state(n)), call=call))
        run_tiles(items)

    import os as _os
    npass = int(_os.environ.get("MK_NPASS", "6"))
    plist = [lambda: pass_A1(0, hA0, km_in[0]), lambda: pass_A2(0, hA0), lambda: pass_B(0, hB0),
             lambda: pass_A1(1, hA1, km_in[1]), lambda: pass_A2(1, hA1), lambda: pass_B(1, hB1)]
    for pf in plist[:npass]:
        pf()
    P.emit(nc, stack)
    stack.close()
    return nc


def _rel_table(rel_bias):
    p = np.arange(128)[:, None]
    x = np.arange(640)[None, :]
    idx = np.clip(512 + p - x, -128, 128) + 128
    R = rel_bias[:, :, idx]
    R = np.ascontiguousarray(R.transpose(0, 2, 1, 3)).copy()
    R[:, 0:64, :, 576:640] = NEG
    R[:, 64:128, :, 0:64] = NEG
    return R.reshape(2, 128, 2560).astype(np.float32)


def kernel(x_prompt, x_sample, state_conv, state_hgrn, cache_attn_k, cache_attn_v, state_ffn,
           w_in, conv_dw_w, conv_dw_b, conv_ln_g, conv_ln_b, w_conv_out,
           hgrn_lb_logits, hgrn_norm_g, w_hgrn_out, attn_rel_bias, w_attn_out,
           w_mix_out, g_mix, w_ffn_up, ffn_dw_w, w_ffn_down, g_ffn, g_final,
           _seg=SEG, _halos=(H_A0, H_B0, H_A1, H_B1)):
    f = lambda a: np.ascontiguousarray(np.asarray(a, dtype=np.float32))
    x_prompt, x_sample = f(x_prompt), f(x_sample)
    seg, halos = _seg, _halos
    W = halos[0]
    NTOK = W + seg
    NT = NTOK // 128
    nb, seq = x_prompt.shape[0], x_prompt.shape[1]
    cps = seq // seg
    assert nb * cps == N_CORES
    nc = build_program(seg, halos)

    def fm(v, nch):
        v = f(v)
        return np.ascontiguousarray(v.reshape(v.shape[:-1] + (nch, 128)).swapaxes(-1, -2))

    shared = dict(
        w_in=f(w_in), w_conv_out=f(w_conv_out), w_hgrn_out=f(w_hgrn_out), w_attn_out=f(w_attn_out),
        w_mix_out=f(w_mix_out), w_ffn_up=f(w_ffn_up), w_ffn_down=f(w_ffn_down),
        gmixT=fm(g_mix, 8), gffnT=fm(g_ffn, 8),
        dwT=np.ascontiguousarray(f(conv_dw_w).reshape(2, 31, 2, 128).transpose(0, 3, 2, 1)).reshape(2, 128, 62),
        dwb=fm(conv_dw_b, 2), lng=fm(conv_ln_g, 2), lnb=fm(conv_ln_b, 2),
        lbl=fm(hgrn_lb_logits, 4), hng=fm(hgrn_norm_g, 4),
        fdw=np.ascontiguousarray(f(ffn_dw_w).reshape(2, 3, 44, 128).transpose(0, 3, 2, 1)).reshape(2, 128, 132),
        R=_rel_table(f(attn_rel_bias)),
        gfin=np.ascontiguousarray(np.broadcast_to(f(g_final)[None, :], (128, D))),
    )
    sc_, sh_, sk_, sv_, sf_ = f(state_conv), f(state_hgrn), f(cache_attn_k), f(cache_attn_v), f(state_ffn)
    in_maps = []
    for c in range(N_CORES):
        b, ci = c // cps, c % cps
        s0 = ci * seg
        xpc = np.zeros((NTOK, D), np.float32)
        lo = max(0, s0 - W)
        xpc[W - (s0 - lo):] = x_prompt[b, lo:s0 + seg]
        pos = np.arange(s0 - W, s0 + seg)
        vt = (pos >= 0).astype(np.float32)
        valid = np.ascontiguousarray(vt.reshape(NT, 128).T)
        kms = []
        for hl in (halos[0], halos[2]):
            row = np.full((512 + hl,), NEG, np.float32)
            row[512:] = np.where(np.arange(s0 - hl, s0) >= 0, 0.0, NEG)
            kms.append(np.ascontiguousarray(np.broadcast_to(row[None], (128, 512 + hl))))
        sl = slice(NSEQ * c, NSEQ * (c + 1))
        m = dict(shared)
        m.update(
            xp=xpc, valid=valid, km0=kms[0], km1=kms[1],
            xs=np.ascontiguousarray(x_sample[sl].reshape(NSEQ * TS, D)),
            sconvT=np.ascontiguousarray(sc_[:, sl].reshape(2, NSEQ, 30, 2, 128).transpose(0, 1, 4, 3, 2)).reshape(2, NSEQ, 128, 60),
            shgrn=np.ascontiguousarray(sh_[:, sl].transpose(0, 1, 3, 2, 4)).reshape(2, NSEQ, 128, 512),
            skT=np.ascontiguousarray(sk_[:, sl].reshape(2, NSEQ, 512, 2, 128).transpose(0, 1, 4, 3, 2)).reshape(2, NSEQ, 128, 1024),
            sv=np.ascontiguousarray(sv_[:, sl].reshape(2, NSEQ, 4, 128, 256).transpose(0, 1, 3, 2, 4)).reshape(2, NSEQ, 128, 1024),
            sffnT=np.ascontiguousarray(sf_[:, sl].reshape(2, NSEQ, 2, 44, 128).transpose(0, 1, 4, 3, 2)).reshape(2, NSEQ, 128, 88),
        )
        in_maps.append(m)
    res = run_bass_kernel_spmd(nc, in_maps, core_ids=list(range(N_CORES))).results
    ndb = x_sample.shape[0]
    y_prompt = np.stack([np.concatenate([res[b * cps + ci]["y"] for ci in range(cps)], 0) for b in range(nb)])
    y_sample = np.concatenate([res[c]["ys"].reshape(NSEQ, TS, D) for c in range(N_CORES)], 0)
    last = [b * cps + cps - 1 for b in range(nb)]

    def convT(a):
        return a.reshape(a.shape[:-2] + (128, 2, 30)).swapaxes(-3, -1).reshape(a.shape[:-2] + (30, 256))

    def ffnT(a):
        return a.reshape(a.shape[:-2] + (128, 44, 2)).swapaxes(-3, -1).reshape(a.shape[:-2] + (2, 5632))

    def kT(a, n):
        return a.reshape(a.shape[:-2] + (128, 2, n)).swapaxes(-3, -1).reshape(a.shape[:-2] + (n, 4, 64))

    def hg(a):
        return a.reshape(a.shape[:-2] + (128, 4, 128)).swapaxes(-3, -2)

    new_conv_p = np.stack([convT(res[c]["convp"]) for c in last], 1)
    new_hgrn_p = np.stack([hg(res[c]["hgrnp"]) for c in last], 1)
    new_k_p = np.stack([kT(res[c]["kp"], 512) for c in last], 1)
    new_v_p = np.stack([res[c]["vp"].reshape(2, 512, 4, 64) for c in last], 1)
    new_ffn_p = np.stack([ffnT(res[c]["ffnp"]) for c in last], 1)
    new_conv_s = np.concatenate([convT(res[c]["convs"]) for c in range(N_CORES)], 1)
    new_hgrn_s = np.concatenate([hg(res[c]["hgrns"]) for c in range(N_CORES)], 1)
    new_k_s = np.concatenate([kT(res[c]["ks"], TS) for c in range(N_CORES)], 1)
    new_v_s = np.concatenate([res[c]["vs"].reshape(2, NSEQ, TS, 4, 64) for c in range(N_CORES)], 1)
    new_ffn_s = np.concatenate([ffnT(res[c]["ffns"]) for c in range(N_CORES)], 1)
    outs = (y_prompt, y_sample, new_conv_p, new_conv_s, new_hgrn_p, new_hgrn_s,
            new_k_p, new_v_p, new_k_s, new_v_s, new_ffn_p, new_ffn_s)
    return tuple(np.ascontiguousarray(o, dtype=np.float32) for o in outs)
```

```python
import types
import numpy as np
from contextlib import ExitStack
import concourse.bass as bass
import concourse.mybir as mybir
from concourse.bass_utils import run_bass_kernel_spmd

F32 = mybir.dt.float32
BF16 = mybir.dt.bfloat16
AF = mybir.ActivationFunctionType
ALU = mybir.AluOpType
AX = mybir.AxisListType

D = 1024
NIN = 6400
DFF = 2816
EPS = 1e-6
NEG = -30000.0
N_CORES = 8
SEG = 4096
H_A0, H_B0, H_A1, H_B1 = 1280, 768, 640, 128
TS = 32
NSEQ = 2


class Prog:
    def __init__(self):
        self.ops = []

    @staticmethod
    def _freeze(fn):
        if fn.__closure__ is None:
            return fn
        cells = []
        for c in fn.__closure__:
            try:
                cells.append(types.CellType(c.cell_contents))
            except ValueError:
                cells.append(c)
        return types.FunctionType(fn.__code__, fn.__globals__, fn.__name__, fn.__defaults__, tuple(cells))

    def add(self, eng, fn, r=(), w=(), dma=None, cont=False):
        self.ops.append(dict(eng=eng, fn=self._freeze(fn), r=tuple(r), w=tuple(w), dma=dma, cont=cont))

    def pe(self, fn, r=(), w=()):
        self.add('pe', fn, r, w)

    def act(self, fn, r=(), w=()):
        self.add('act', fn, r, w)

    def dve(self, fn, r=(), w=()):
        self.add('dve', fn, r, w)

    def pool(self, fn, r=(), w=()):
        self.add('pool', fn, r, w)

    def dma(self, fn, r=(), w=(), key='d', cont=False, q='sp'):
        self.add(q, fn, r, w, dma=key, cont=cont)

    def barrier(self):
        self.ops.append(dict(eng=None, fn=None, r=(), w=(), dma=None, cont=False))

    def emit(self, nc, stack):
        ops = self.ops
        n = len(ops)
        lastw = {}
        readers = {}
        deps = [None] * n
        dma_hist = {}
        last_eng = {}
        last_dma = {}
        pending = {}
        for i, o in enumerate(ops):
            if o['eng'] is None:
                snap = set(last_eng.values()) | set(last_dma.values())
                for e_ in ('sp', 'pe', 'act', 'dve', 'pool'):
                    pending[e_] = pending.get(e_, set()) | snap
                deps[i] = set()
                continue
            d = set()
            if pending.get(o['eng']):
                d |= pending.pop(o['eng'])
            if o['dma'] is None:
                last_eng[o['eng']] = i
            else:
                last_dma[o['dma']] = i
            for t in o['r']:
                for j in lastw.get(t, ()):
                    d.add(j)
                if isinstance(t, tuple) and t[0] == 'ps':
                    for j in readers.get(t, {}).values():
                        d.add(j)
            for t in o['w']:
                for j in lastw.get(t, ()):
                    d.add(j)
                for j in readers.get(t, {}).values():
                    d.add(j)
            if o['dma'] is not None:
                groups = dma_hist.setdefault(o['dma'], [])
                if o['cont'] and groups:
                    groups[-1].append(i)
                    prev = groups[:-1]
                else:
                    prev = list(groups)
                    groups.append([i])
                if prev:
                    d.add(prev[-1][-1])
                for j in list(d):
                    if j in groups[-1]:
                        d.discard(j)
            d.discard(i)
            deps[i] = d
            for t in o['r']:
                rd = readers.setdefault(t, {})
                if o['dma'] is not None:
                    rd[('dma', i)] = i
                else:
                    rd[o['eng']] = i
            for t in o['w']:
                lastw[t] = [i]
                readers[t] = {}
        dcount = {}
        for key, groups in dma_hist.items():
            c = 0
            for g in groups:
                c += 16 * len(g)
                for i in g:
                    dcount[i] = c
        need = [False] * n
        for i, o in enumerate(ops):
            if o['eng'] is None:
                continue
            for j in deps[i]:
                pj = ops[j]
                if pj['dma'] is not None:
                    continue
                if pj['eng'] == o['eng'] and o['dma'] is None and o['eng'] == 'pe':
                    continue
                need[j] = True
        cnt = [0] * n
        ec = {}
        for i, o in enumerate(ops):
            if o['eng'] is not None and o['dma'] is None and need[i]:
                ec[o['eng']] = ec.get(o['eng'], 0) + 1
                cnt[i] = ec[o['eng']]
        sems = {}
        for e in ('pe', 'act', 'dve', 'pool'):
            sems[e] = stack.enter_context(nc.semaphore('tl_' + e))
        dsems = {}
        for k, key in enumerate(dma_hist):
            dsems[key] = stack.enter_context(nc.semaphore('dq%d' % k))
        block = stack.enter_context(nc.Block())
        engs = ('sp', 'pe', 'act', 'dve', 'pool')
        per = {e: [i for i in range(n) if ops[i]['eng'] == e] for e in engs}
        totals = {key: 16 * sum(len(g) for g in groups) for key, groups in dma_hist.items()}

        def run(eng_name, h):
            seen = {}
            for i in per[eng_name]:
                o = ops[i]
                waits = {}
                for j in deps[i]:
                    pj = ops[j]
                    if pj['dma'] is not None:
                        s, v = ('d', pj['dma']), dcount[j]
                    else:
                        if pj['eng'] == eng_name and eng_name == 'pe' and o['dma'] is None:
                            continue
                        s, v = ('e', pj['eng']), cnt[j]
                    if v > waits.get(s, 0):
                        waits[s] = v
                for s, v in waits.items():
                    if seen.get(s, 0) >= v:
                        continue
                    seen[s] = v
                    sem = dsems[s[1]] if s[0] == 'd' else sems[s[1]]
                    h.wait_ge(sem, v)
                ins = o['fn'](h)
                if o['dma'] is not None:
                    ins.then_inc(dsems[o['dma']], 16)
                elif need[i]:
                    ins.then_inc(sems[eng_name], 1)
            if eng_name == 'sp':
                for key, tot in totals.items():
                    h.wait_ge(dsems[key], tot)

        @block.sync
        def _(h):
            run('sp', h)

        @block.tensor
        def _(h):
            run('pe', h)

        @block.scalar
        def _(h):
            run('act', h)

        @block.vector
        def _(h):
            run('dve', h)

        @block.gpsimd
        def _(h):
            run('pool', h)


def build_program(seg=SEG, halos=(H_A0, H_B0, H_A1, H_B1)):
    hA0, hB0, hA1, hB1 = halos
    W = hA0
    NTOK = W + seg
    NT = NTOK // 128
    nc = bass.Bass("TRN2", target_bir_lowering=False)
    P = Prog()
    stack = ExitStack()

    def din(name, shape, dt=F32):
        return nc.dram_tensor(name, list(shape), dt, kind="ExternalInput").ap()

    def dout(name, shape, dt=F32):
        return nc.dram_tensor(name, list(shape), dt, kind="ExternalOutput").ap()

    def dscr(name, shape, dt=F32):
        return nc.dram_tensor(name, list(shape), dt, kind="Internal").ap()

    xp = din("xp", [NTOK, D])
    valid = din("valid", [128, NT])
    km_in = [din("km0", [128, 512 + hA0]), din("km1", [128, 512 + hA1])]
    xs = din("xs", [NSEQ * TS, D])
    sconvT = din("sconvT", [2, NSEQ, 128, 60])
    shgrn = din("shgrn", [2, NSEQ, 128, 512])
    skT = din("skT", [2, NSEQ, 128, 1024])
    sv = din("sv", [2, NSEQ, 128, 1024])
    sffnT = din("sffnT", [2, NSEQ, 128, 88])
    w_in = din("w_in", [2, D, NIN])
    w_conv_out = din("w_conv_out", [2, 256, D])
    w_hgrn_out = din("w_hgrn_out", [2, 512, D])
    w_attn_out = din("w_attn_out", [2, 256, D])
    w_mix_out = din("w_mix_out", [2, D, D])
    w_ffn_up = din("w_ffn_up", [2, D, 2 * DFF])
    w_ffn_down = din("w_ffn_down", [2, DFF, D])
    gmixT = din("gmixT", [2, 128, 8])
    gffnT = din("gffnT", [2, 128, 8])
    dwT_in = din("dwT", [2, 128, 62])
    dwb_in = din("dwb", [2, 128, 2])
    lng_in = din("lng", [2, 128, 2])
    lnb_in = din("lnb", [2, 128, 2])
    lbl_in = din("lbl", [2, 128, 4])
    hng_in = din("hng", [2, 128, 4])
    fdw_in = din("fdw", [2, 128, 132])
    R_in = din("R", [2, 128, 2560])
    gfin_in = din("gfin", [128, D])
    y_o = dout("y", [seg, D])
    ys_o = dout("ys", [NSEQ * TS, D])
    convp_o = dout("convp", [2, 128, 60])
    hgrnp_o = dout("hgrnp", [2, 128, 512])
    kp_o = dout("kp", [2, 128, 1024])
    vp_o = dout("vp", [2, 512, 256])
    ffnp_o = dout("ffnp", [2, 128, 88])
    convs_o = dout("convs", [2, NSEQ, 128, 60])
    hgrns_o = dout("hgrns", [2, NSEQ, 128, 512])
    ks_o = dout("ks", [2, NSEQ, 128, 64])
    vs_o = dout("vs", [2, NSEQ, TS, 256])
    ffns_o = dout("ffns", [2, NSEQ, 128, 88])
    x1_d = dscr("x1_d", [NTOK, D])
    hm_d = dscr("hm_d", [NTOK, D])
    ft_d = dscr("ft_d", [NT, 128, 1024], BF16)
    x1s_d = dscr("x1s_d", [NSEQ * TS, D])
    hms_d = dscr("hms_d", [NSEQ * TS, D])
    fts_d = dscr("fts_d", [NSEQ, 128, 1024], BF16)

    def sb(name, cols, dt=F32):
        return stack.enter_context(nc.sbuf_tensor("s_" + name, [128, cols], dt))

    arena = sb("arena", 67584, BF16)
    stg = [sb("stg%d" % i, 1024) for i in range(2)]
    xt = [sb("xt%d" % i, D) for i in range(3)]
    junk = sb("junk", D, BF16)
    xn = sb("xn", D, BF16)
    xnT = sb("xnT", 1024, BF16)
    sm = sb("sm", 64)
    hout = sb("hout", D)
    ident_f = sb("ident_f", 128)
    ident_b = sb("ident_b", 128, BF16)
    ones_f = sb("ones_f", 128)
    o128 = sb("o128", 128)
    o256 = sb("o256", 128)
    maskT = sb("maskT", 128)
    valid_sb = sb("valid_sb", NT)
    gfin = sb("gfin", D)
    epsc = sb("epsc", 2)
    negones = sb("negones", 128)
    gT = sb("gT", 8)
    dwT = sb("dwT", 62)
    dwb = sb("dwb", 2)
    lng = sb("lng", 2)
    lnb = sb("lnb", 2)
    lbl = sb("lbl", 8)
    lb = sb("lb", 4)
    oml = sb("oml", 4)
    hng = sb("hng", 4)
    fdw = sb("fdw", 132)
    U = sb("U", 9216)

    class Carve:
        def __init__(self, arena_from=67584):
            self.a_off = arena_from
            self.u_off = 0

        def _take(self, nbytes):
            nb2 = (nbytes + 3) // 4 * 2
            if self.a_off + nb2 <= 67584:
                a = arena[:, self.a_off:self.a_off + nb2]
                self.a_off += nb2
                return a, BF16
            c32 = nb2 // 2
            a = U[:, self.u_off:self.u_off + c32]
            self.u_off += c32
            assert self.u_off <= 9216, self.u_off
            return a, F32

        def f32(self, cols):
            a, dt = self._take(4 * cols)
            return a if dt == F32 else a.bitcast(F32)

        def bf(self, cols):
            a, dt = self._take(2 * cols)
            a = a if dt == BF16 else a.bitcast(BF16)
            return a[:, 0:cols]

    psum = [stack.enter_context(nc.psum_tensor("ps%d" % i, [128, 512], F32)) for i in range(8)]
    pctr = [0]

    def bank(fixed=None):
        if fixed is not None:
            return psum[fixed], ('ps', fixed)
        b = pctr[0] % 8
        pctr[0] += 1
        return psum[b], ('ps', b)

    P.pool(lambda e: e.memset(ones_f[:, :], 1.0), w=['ones_f'])
    P.pool(lambda e: e.memset(negones[:, :], -1.0), w=['negones'])
    P.pool(lambda e: e.memset(epsc[:, 0:1], EPS), w=['epsc'])
    P.pool(lambda e: e.memset(epsc[:, 1:2], 1.0), w=['epsc'])
    P.pool(lambda e: e.memset(o128[:, :], 1.0 / 128), w=['o128'])
    P.pool(lambda e: e.memset(o256[:, :], 1.0 / 256), w=['o256'])
    P.pool(lambda e: e.memset(ident_f[:, :], 0.0), w=['ident_f'])
    P.pool(lambda e: e.affine_select(out=ident_f[:, :], in_=ident_f[:, :], pattern=[[-1, 128]],
                                     compare_op=ALU.not_equal, fill=1.0, base=0, channel_multiplier=1),
           r=['ident_f'], w=['ident_f'])
    P.pool(lambda e: e.tensor_copy(out=ident_b[:, :], in_=ident_f[:, :]), r=['ident_f'], w=['ident_b'])
    P.pool(lambda e: e.affine_select(out=maskT[:, :], in_=ones_f[:, :], pattern=[[1, 128]],
                                     compare_op=ALU.is_ge, fill=0.0, base=0, channel_multiplier=-1),
           r=['ones_f'], w=['maskT'])
    P.dma(lambda e: e.dma_start(out=valid_sb[:, :], in_=valid[:, :]), w=['valid_sb'], key='c0')
    P.dma(lambda e: e.dma_start(out=gfin[:, :], in_=gfin_in[:, :]), w=['gfin'], key='c0', cont=True)

    wctr = [0]
    stg_all = list(stg) + [U[:, i * 1024:(i + 1) * 1024] for i in range(8)]

    def load_weight(dst_ap, dst_tok, src_rows_ap, ncols, scale_ap=None, scale_tok=None):
        c0 = 0
        while c0 < ncols:
            cw = min(1024, ncols - c0)
            s = wctr[0] % len(stg_all)
            wctr[0] += 1
            st, stok = stg_all[s], 'stg%d' % s
            P.dma(lambda e, st=st, c0=c0, cw=cw: e.dma_start(out=st[:, 0:cw], in_=src_rows_ap[:, c0:c0 + cw]),
                  w=[stok], key=stok)
            d = dst_ap[:, c0:c0 + cw]
            dst_tok = ('Wp', wctr[0])
            eng = ('act', 'dve')[wctr[0] % 2] if scale_ap is not None else ('act', 'dve', 'pool', 'dve')[wctr[0] % 4]
            if scale_ap is None:
                if eng == 'act':
                    P.act(lambda e, d=d, st=st, cw=cw: e.copy(out=d, in_=st[:, 0:cw]), r=[stok], w=[dst_tok])
                else:
                    P.add(eng, lambda e, d=d, st=st, cw=cw: e.tensor_copy(out=d, in_=st[:, 0:cw]), r=[stok], w=[dst_tok])
            else:
                if eng == 'act':
                    P.act(lambda e, d=d, st=st, cw=cw: e.activation(out=d, in_=st[:, 0:cw], func=AF.Copy, scale=scale_ap),
                          r=[stok, scale_tok], w=[dst_tok])
                else:
                    P.add(eng, lambda e, d=d, st=st, cw=cw: e.tensor_scalar(out=d, in0=st[:, 0:cw], scalar1=scale_ap,
                                                                           scalar2=None, op0=ALU.mult),
                          r=[stok, scale_tok], w=[dst_tok])
            c0 += cw

    xn_s = [xn, None]
    xnT_s = [xnT, None]
    xn_tok = ['xn0', 'xn1']

    def pro_load(src_ap, T, xslot, srctok=None):
        x_t, xtok = xt[xslot], 'xt%d' % xslot
        P.dma(lambda e: e.dma_start(out=x_t[:T, :], in_=src_ap), r=([srctok] if srctok else []), w=[xtok], key=xtok)

    def prologue(T, xslot, slot):
        x_t, xtok = xt[xslot], 'xt%d' % xslot
        xn_, xnT_ = xn_s[slot], xnT_s[slot]
        xnt, xTt = xn_tok[slot], 'xnT%d' % slot
        b0 = 16 * slot
        smt = 'smp%d' % slot
        P.act(lambda e: e.activation(out=junk[:T, :], in_=x_t[:T, :], func=AF.Square, accum_out=sm[:T, b0:b0 + 1]),
              r=[xtok], w=['junk', smt])
        P.act(lambda e: e.activation(out=sm[:T, b0 + 1:b0 + 2], in_=sm[:T, b0:b0 + 1], func=AF.Ln, scale=1.0 / D, bias=epsc[:T, 0:1]),
              r=[smt, 'epsc'], w=[smt])
        P.act(lambda e: e.activation(out=sm[:T, b0 + 2:b0 + 3], in_=sm[:T, b0 + 1:b0 + 2], func=AF.Exp, scale=-0.5), r=[smt], w=[smt])
        P.act(lambda e: e.activation(out=xn_[:T, :], in_=x_t[:T, :], func=AF.Copy, scale=sm[:T, b0 + 2:b0 + 3]),
              r=[xtok, smt], w=[xnt])
        for half in range(2):
            ps, ptok = bank()
            for kk in range(4):
                k = half * 4 + kk
                P.pe(lambda e: e.matmul(ps[:, kk * 128:kk * 128 + T], lhsT=xn_[:T, k * 128:(k + 1) * 128],
                                        rhs=ident_b[:T, :T], start=True, stop=True),
                     r=[xnt, 'ident_b'], w=[ptok])
            dst = xnT_[:, half * 512:(half + 1) * 512].rearrange("p (k t) -> p k t", t=128)[:, :, :T]
            src = ps[:, :].rearrange("p (k t) -> p k t", t=128)[:, :, :T]
            if half == 0:
                P.dve(lambda e: e.tensor_copy(out=dst, in_=src), r=[ptok], w=[xTt])
            else:
                P.act(lambda e: e.copy(out=dst, in_=src), r=[ptok], w=[xTt])
        return x_t, xtok, xnT_, xTt

    def proj_feat(wv, wtok, col, T, xT, xTtok, nk=8):
        ps, ptok = bank()
        for k in range(nk):
            P.pe(lambda e: e.matmul(ps[:, :T], lhsT=wv[:, k, col:col + 128], rhs=xT[:, k * 128:k * 128 + T],
                                    start=(k == 0), stop=(k == nk - 1)),
                 r=[wtok, xTtok], w=[ptok])
        return ps, ptok

    def proj_tok(wv, wtok, col, ncol, T, xT, xTtok):
        ps, ptok = bank()
        for k in range(8):
            P.pe(lambda e: e.matmul(ps[:T, :ncol], lhsT=xT[:, k * 128:k * 128 + T], rhs=wv[:, k, col:col + ncol],
                                    start=(k == 0), stop=(k == 7)),
                 r=[wtok, xTtok], w=[ptok])
        return ps, ptok

    def run_tiles(items):
        pros = {}
        n_it = len(items)
        for idx, it in enumerate(items):
            if idx == 0:
                for k_ in range(min(2, n_it)):
                    pro_load(items[k_]['src'], items[k_]['T'], k_ % 3, items[k_]['srctok'])
                pros[0] = prologue(items[0]['T'], 0, 0)
            two_stage = 'front' in it

            def load_ahead():
                if idx + 2 < n_it:
                    n2 = items[idx + 2]
                    pro_load(n2['src'], n2['T'], (idx + 2) % 3, n2['srctok'])
            if not two_stage:
                load_ahead()
            if idx + 1 < n_it:
                pros[idx + 1] = prologue(items[idx + 1]['T'], (idx + 1) % 3, (idx + 1) % 2)
            if it.get('pre') is not None:
                it['pre']()
            if two_stage:
                it['front'](pros[idx], idx % 2)
                if idx >= 1:
                    items[idx - 1]['back'](pros.pop(idx - 1), (idx - 1) % 2)
                load_ahead()
                if idx == n_it - 1:
                    it['back'](pros.pop(idx), idx % 2)
            else:
                it['call'](pros.pop(idx), idx % 2)

    def w3(off, nk, ncol):
        return arena[:, off:off + nk * ncol].rearrange("p (k c) -> p k c", c=ncol)

    def pass_A1(l, halo, km):
        first_tile = (W - halo) // 128
        wa = w3(0, 8, 3328)
        P.barrier()
        P.dma(lambda e: e.dma_start(out=gT[:, :], in_=gmixT[l]), w=['gT'], key='c1')
        for src, dst in ((dwT_in, dwT), (dwb_in, dwb), (lng_in, lng), (lnb_in, lnb), (hng_in, hng)):
            P.dma(lambda e, src=src, dst=dst: e.dma_start(out=dst[:, :], in_=src[l]), w=['lconst'], key='c1', cont=True)
        P.dma(lambda e: e.dma_start(out=lbl[:, 0:4], in_=lbl_in[0]), w=['lconst'], key='c1', cont=True)
        P.dma(lambda e: e.dma_start(out=lbl[:, 4:8], in_=lbl_in[1]), w=['lconst'], key='c1', cont=True)
        c = Carve(8 * 3328)
        Rt = c.f32(2560)
        kms = c.f32(512 + halo)
        P.dma(lambda e: e.dma_start(out=Rt, in_=R_in[l]), w=['R'], key='c1', cont=True)
        P.dma(lambda e: e.dma_start(out=kms, in_=km), w=['km'], key='c1', cont=True)
        if l == 0:
            P.dve(lambda e: e.memset(lb[:, :], 0.0), w=['lb'])
        else:
            P.dve(lambda e: e.tensor_tensor(out=lb[:, :], in0=lbl[:, 4:8], in1=lbl[:, 0:4], op=ALU.subtract),
                  r=['lconst'], w=['lb'])
            P.act(lambda e: e.activation(out=lb[:, :], in_=lb[:, :], func=AF.Exp, scale=-1.0), r=['lb'], w=['lb'])
            P.act(lambda e: e.activation(out=lb[:, :], in_=lb[:, :], func=AF.Ln, bias=epsc[:, 1:2], scale=1.0), r=['lb', 'epsc'], w=['lb'])
            P.act(lambda e: e.activation(out=lb[:, :], in_=lb[:, :], func=AF.Exp, scale=-1.0), r=['lb'], w=['lb'])
        P.dve(lambda e: e.tensor_scalar(out=oml[:, :], in0=lb[:, :], scalar1=-1.0, scalar2=1.0, op0=ALU.mult, op1=ALU.add),
              r=['lb'], w=['oml'])
        for k in range(8):
            load_weight(wa[:, k, :], 'W', w_in[l, k * 128:(k + 1) * 128, 0:3328], 3328, gT[:, k:k + 1], 'gT')
        xn_s[1] = c.bf(1024)
        xnT_s[1] = c.bf(1024)
        diag = c.bf(62 * 128)
        for j in range(2):
            for tap in range(31):
                eng = 'pool' if (tap % 2) else 'dve'
                P.add(eng, lambda e, j=j, tap=tap: e.tensor_scalar(
                    out=diag[:, (j * 31 + tap) * 128:(j * 31 + tap + 1) * 128], in0=ident_f[:, :],
                    scalar1=dwT[:, j * 31 + tap:j * 31 + tap + 1], scalar2=None, op0=ALU.mult),
                    r=['ident_f', 'lconst'], w=['diag'])
        u32 = c.f32(256)
        ubf = c.bf(2 * 160)
        ubf3 = ubf.rearrange("p (j n) -> p j n", n=160)
        sg = [c.f32(128), c.f32(128)]
        hsb = c.f32(256)
        hsq = c.f32(256)
        st4 = c.f32(512)
        hn = [c.f32(128), c.f32(128)]
        feat = c.bf(1024)
        qT = c.bf(256)
        kT32 = c.f32(256)
        kTb = [c.bf(2 * 640), c.bf(2 * 640)]
        Vb = [c.bf(5 * 256), c.bf(5 * 256)]
        v32 = c.f32(256)
        sc = [c.f32(640), c.f32(640)]
        Pbf = [c.bf(640), c.bf(640)]
        PT = [c.bf(640), c.bf(640)]
        obf = c.bf(256)
        at_sm = c.f32(16)
        HS = []
        for s_ in range(4):
            d_ = dict(q32=c.f32(128), ff=c.f32(128), lf=c.f32(128), kk32=c.f32(128), Bc=c.f32(128), nB=c.f32(128),
                      E1=c.f32(128), E2=c.f32(128), qh=c.bf(128), kh=c.bf(128), qt=c.bf(128), Kb=c.bf(128),
                      ATm=[c.bf(128), c.bf(128)], KbT=[c.bf(128), c.bf(128)], osq=c.f32(128), sgl=c.f32(128), on=c.f32(128),
                      rstd_h=c.f32(128), hs=c.f32(8), o32=c.f32(128), sig=c.f32(384))
            HS.append(d_)
        vbf = c.bf(1024)
        S32 = c.f32(512)
        Sbf = c.bf(512)
        FT = ['feat%d' % i for i in range(8)]

        def init_states():
            P.pool(lambda e: e.memset(ubf, 0.0), w=['ubf'])
            P.pool(lambda e: e.memset(kTb[0], 0.0), w=['kTb0'])
            P.pool(lambda e: e.memset(Vb[0], 0.0), w=['Vb0'])
            P.pool(lambda e: e.memset(S32, 0.0), w=['S32_%d' % h for h in range(4)])
            P.pool(lambda e: e.memset(Sbf, 0.0), w=['Sbf_%d' % h for h in range(4)])

        def tile(pro, T, feat_dst, fttok_d, cur, kmoff, state_out, shift):
            KW = 512 + T
            nxt = 1 - cur
            kb, kbtok = kTb[cur], 'kTb%d' % cur
            vb, vbtok = Vb[cur], 'Vb%d' % cur
            x_t, xtok, xT, xTtok = pro
            so = state_out or {}

            def pf(col):
                return proj_feat(wa, 'W', col, T, xT, xTtok)

            def brA():
                for j in range(2):
                    p1, t1 = pf(j * 128)
                    p2, t2 = pf(256 + j * 128)
                    sgj = sg[j]
                    P.act(lambda e: e.activation(out=sgj[:, :T], in_=p2[:, :T], func=AF.Exp, scale=-1.0), r=[t2], w=['sg%d' % j])
                    P.act(lambda e: e.activation(out=sgj[:, :T], in_=sgj[:, :T], func=AF.Ln, bias=epsc[:, 1:2], scale=1.0), r=['sg%d' % j, 'epsc'], w=['sg%d' % j])
                    P.act(lambda e: e.activation(out=sgj[:, :T], in_=sgj[:, :T], func=AF.Exp, scale=-1.0), r=['sg%d' % j], w=['sg%d' % j])
                    P.dve(lambda e: e.tensor_tensor(out=u32[:, j * 128:j * 128 + T], in0=p1[:, :T], in1=sgj[:, :T], op=ALU.mult),
                          r=[t1, 'sg%d' % j], w=['u32_%d' % j])
                    P.pool(lambda e: e.tensor_copy(out=ubf[:, j * 160 + 30:j * 160 + 30 + T], in_=u32[:, j * 128:j * 128 + T]),
                           r=['u32_%d' % j], w=['ubf'])
                yield
                for j in range(2):
                    ps, ptok = bank()
                    for tap in range(31):
                        P.pe(lambda e: e.matmul(ps[:, :T], lhsT=diag[:, (j * 31 + tap) * 128:(j * 31 + tap + 1) * 128],
                                                rhs=ubf[:, j * 160 + tap:j * 160 + tap + T], start=(tap == 0), stop=(tap == 30)),
                             r=['diag', 'ubf'], w=[ptok])
                    P.act(lambda e: e.activation(out=hsb[:, j * 128:j * 128 + T], in_=ps[:, :T], func=AF.Identity,
                                                 bias=dwb[:, j:j + 1], scale=1.0), r=[ptok, 'lconst'], w=['hsb%d' % j])
                    P.act(lambda e: e.activation(out=hsq[:, j * 128:j * 128 + T], in_=ps[:, :T], func=AF.Square,
                                                 bias=dwb[:, j:j + 1], scale=1.0), r=[ptok, 'lconst'], w=['hsq%d' % j])
                if so.get('conv') is not None:
                    for j in range(2):
                        P.dma(lambda e: e.dma_start(out=so['conv'][:, j * 30:(j + 1) * 30],
                                                    in_=u32[:, j * 128 + T - 30:j * 128 + T]), r=['u32_%d' % j], key='so')
                if shift:
                    P.pool(lambda e: e.tensor_copy(out=ubf3[:, :, 0:30], in_=ubf3[:, :, T:T + 30]), r=['ubf'], w=['ubf'])
                yield
                pm, pmt = bank()
                pq, pqt = bank()
                for j in range(2):
                    P.pe(lambda e: e.matmul(pm[:, :T], lhsT=o256[:, :], rhs=hsb[:, j * 128:j * 128 + T], start=(j == 0), stop=(j == 1)),
                         r=['o256', 'hsb%d' % j], w=[pmt])
                for j in range(2):
                    P.pe(lambda e: e.matmul(pq[:, :T], lhsT=o256[:, :], rhs=hsq[:, j * 128:j * 128 + T], start=(j == 0), stop=(j == 1)),
                         r=['o256', 'hsq%d' % j], w=[pqt])
                msb, tt1, var, rstd = st4[:, 0:128], st4[:, 128:256], st4[:, 256:384], st4[:, 384:512]
                P.dve(lambda e: e.tensor_copy(out=msb[:, :T], in_=pm[:, :T]), r=[pmt], w=['msb'])
                P.dve(lambda e: e.tensor_tensor(out=tt1[:, :T], in0=msb[:, :T], in1=msb[:, :T], op=ALU.mult), r=['msb'], w=['tt1'])
                P.dve(lambda e: e.tensor_tensor(out=var[:, :T], in0=pq[:, :T], in1=tt1[:, :T], op=ALU.subtract), r=[pqt, 'tt1'], w=['var'])
                P.act(lambda e: e.activation(out=var[:, :T], in_=var[:, :T], func=AF.Ln, scale=1.0, bias=epsc[:, 0:1]),
                      r=['var', 'epsc'], w=['var'])
                P.act(lambda e: e.activation(out=rstd[:, :T], in_=var[:, :T], func=AF.Exp, scale=-0.5), r=['var'], w=['rstd'])
                for j in range(2):
                    hnj = hn[j]
                    P.pool(lambda e: e.tensor_tensor(out=hnj[:, :T], in0=hsb[:, j * 128:j * 128 + T], in1=msb[:, :T], op=ALU.subtract),
                           r=['hsb%d' % j, 'msb'], w=['hn%d' % j])
                    P.pool(lambda e: e.tensor_tensor(out=hnj[:, :T], in0=hnj[:, :T], in1=rstd[:, :T], op=ALU.mult),
                           r=['hn%d' % j, 'rstd'], w=['hn%d' % j])
                    P.dve(lambda e: e.tensor_scalar(out=hnj[:, :T], in0=hnj[:, :T], scalar1=lng[:, j:j + 1], scalar2=lnb[:, j:j + 1],
                                                    op0=ALU.mult, op1=ALU.add), r=['hn%d' % j, 'lconst'], w=['hn%d' % j])
                    sgj = sg[j]
                    P.act(lambda e: e.activation(out=sgj[:, :T], in_=hnj[:, :T], func=AF.Exp, scale=-1.0), r=['hn%d' % j], w=['sg%d' % j])
                    P.act(lambda e: e.activation(out=sgj[:, :T], in_=sgj[:, :T], func=AF.Ln, bias=epsc[:, 1:2], scale=1.0), r=['sg%d' % j, 'epsc'], w=['sg%d' % j])
                    P.act(lambda e: e.activation(out=sgj[:, :T], in_=sgj[:, :T], func=AF.Exp, scale=-1.0), r=['sg%d' % j], w=['sg%d' % j])
                    P.dve(lambda e: e.tensor_tensor(out=feat[:, j * 128:j * 128 + T], in0=hnj[:, :T], in1=sgj[:, :T], op=ALU.mult),
                          r=['hn%d' % j, 'sg%d' % j], w=[FT[j]])
                yield

            nb = (KW + 127) // 128

            def C_common():
                for j in range(2):
                    pqq, tq = pf(2560 + j * 128)
                    P.dve(lambda e: e.tensor_copy(out=qT[:, j * 128:j * 128 + T], in_=pqq[:, :T]), r=[tq], w=['qT%d' % j])
                    pk, tk_ = pf(2816 + j * 128)
                    P.act(lambda e: e.copy(out=kT32[:, j * 128:j * 128 + T], in_=pk[:, :T]), r=[tk_], w=['kT32_%d' % j])
                    P.pool(lambda e: e.tensor_copy(out=kb[:, j * 640 + 512:j * 640 + 512 + T], in_=kT32[:, j * 128:j * 128 + T]),
                           r=['kT32_%d' % j], w=[kbtok])
                pv, tv = proj_tok(wa, 'W', 3072, 256, T, xT, xTtok)
                P.act(lambda e: e.copy(out=v32[:T, :], in_=pv[:T, 0:256]), r=[tv], w=['v32'])
                P.pool(lambda e: e.tensor_copy(out=vb[:T, 4 * 256:5 * 256], in_=v32[:T, :]), r=['v32'], w=[vbtok])
                if state_out is not None:
                    for j in range(2):
                        P.dma(lambda e: e.dma_start(out=so['k'][j], in_=kT32[:, j * 128:j * 128 + T]), r=['kT32_%d' % j], key='so')
                    P.dma(lambda e: e.dma_start(out=so['v'], in_=v32[:T, :]), r=['v32'], key='so')

            def C_head(h):
                j, pb = h // 2, 64 * (h % 2)
                s_ = h % 2
                scs, Ps, PTs = sc[s_], Pbf[s_], PT[s_]
                sct, Pt_, PTt = 'sc%d' % s_, 'Pbf%d' % s_, 'PT%d' % s_
                ps1, t1 = bank()
                ps2, t2 = bank()
                P.pe(lambda e: e.matmul(ps1[:T, 0:512], lhsT=qT[pb:pb + 64, j * 128:j * 128 + T],
                                        rhs=kb[pb:pb + 64, j * 640:j * 640 + 512], start=True, stop=True),
                     r=['qT%d' % j, kbtok], w=[t1])
                P.pe(lambda e: e.matmul(ps2[:T, 0:T], lhsT=qT[pb:pb + 64, j * 128:j * 128 + T],
                                        rhs=kb[pb:pb + 64, j * 640 + 512:j * 640 + 512 + T], start=True, stop=True),
                     r=['qT%d' % j, kbtok], w=[t2])
                P.dve(lambda e: e.scalar_tensor_tensor(out=scs[:T, 0:512], in0=ps1[:T, 0:512], scalar=0.125,
                                                       in1=Rt[:T, h * 640:h * 640 + 512], op0=ALU.mult, op1=ALU.add),
                      r=[t1, 'R'], w=[sct])
                P.dve(lambda e: e.scalar_tensor_tensor(out=scs[:T, 512:512 + T], in0=ps2[:T, 0:T], scalar=0.125,
                                                       in1=Rt[:T, h * 640 + 512:h * 640 + 512 + T], op0=ALU.mult, op1=ALU.add),
                      r=[t2, 'R'], w=[sct])
                yield
                if kmoff is not None:
                    wd = min(KW, 512 + halo - kmoff)
                    P.pool(lambda e: e.tensor_tensor(out=scs[:T, 0:wd], in0=scs[:T, 0:wd], in1=kms[:T, kmoff:kmoff + wd], op=ALU.add),
                           r=[sct, 'km'], w=[sct])
                    yield
                P.dve(lambda e: e.reduce_max(out=at_sm[:T, h:h + 1], in_=scs[:T, 0:KW], axis=AX.X), r=[sct], w=['mx%d' % h])
                P.dve(lambda e: e.tensor_scalar(out=at_sm[:T, 4 + h:5 + h], in0=at_sm[:T, h:h + 1], scalar1=-1.0, scalar2=None, op0=ALU.mult),
                      r=['mx%d' % h], w=['nmx%d' % h])
                yield
                P.act(lambda e: e.activation(out=Ps[:T, 0:KW], in_=scs[:T, 0:KW], func=AF.Exp, bias=at_sm[:T, 4 + h:5 + h], scale=1.0,
                                             accum_out=at_sm[:T, 8 + h:9 + h]), r=[sct, 'nmx%d' % h], w=[Pt_, 'rsum%d' % h])
                yield
                P.dve(lambda e: e.reciprocal(out=at_sm[:T, 12 + h:13 + h], in_=at_sm[:T, 8 + h:9 + h]), r=['rsum%d' % h], w=['rinv%d' % h])
                pt1, tt1_ = bank()
                pt2, tt2_ = bank()
                for jb in range(nb):
                    wb = min(128, KW - 128 * jb)
                    dst = pt1[:wb, jb * 128:jb * 128 + T] if jb < 4 else pt2[:wb, 0:T]
                    P.pe(lambda e: e.matmul(dst, lhsT=Ps[:T, jb * 128:jb * 128 + wb], rhs=ident_b[:T, :T], start=True, stop=True),
                         r=[Pt_, 'ident_b'], w=[tt1_ if jb < 4 else tt2_])
                P.act(lambda e: e.copy(out=PTs[:, 0:512], in_=pt1[:, 0:512]), r=[tt1_], w=[PTt])
                wl = KW - 512
                P.dve(lambda e: e.tensor_copy(out=PTs[:wl, 512:512 + T], in_=pt2[:wl, 0:T]), r=[tt2_], w=[PTt])
                yield
                po, pot = bank()
                for jb in range(nb):
                    wb = min(128, KW - 128 * jb)
                    P.pe(lambda e: e.matmul(po[:T, 0:64], lhsT=PTs[:wb, jb * 128:jb * 128 + T],
                                            rhs=vb[:wb, jb * 256 + h * 64:jb * 256 + (h + 1) * 64],
                                            start=(jb == 0), stop=(jb == nb - 1)),
                         r=[PTt, vbtok], w=[pot])
                P.act(lambda e: e.activation(out=obf[:T, h * 64:(h + 1) * 64], in_=po[:T, 0:64], func=AF.Copy,
                                             scale=at_sm[:T, 12 + h:13 + h]), r=[pot, 'rinv%d' % h], w=['obf%d' % h])
                yield

            def C_final():
                pc, pct = bank()
                for j in range(2):
                    P.pe(lambda e: e.matmul(pc[:, j * 128:j * 128 + T], lhsT=obf[:T, j * 128:(j + 1) * 128], rhs=ident_b[:T, :T],
                                            start=True, stop=True), r=['obf%d' % (2 * j), 'obf%d' % (2 * j + 1), 'ident_b'], w=[pct])
                P.dve(lambda e: e.tensor_copy(out=feat[:, 768:1024].rearrange("p (j t) -> p j t", t=128)[:, :, :T],
                                              in_=pc[:, 0:256].rearrange("p (j t) -> p j t", t=128)[:, :, :T]), r=[pct], w=[FT[6], FT[7]])
                if shift:
                    P.pool(lambda e: e.tensor_copy(out=kTb[nxt].rearrange("p (j n) -> p j n", n=640)[:, :, 0:512],
                                                   in_=kb.rearrange("p (j n) -> p j n", n=640)[:, :, 128:640]),
                           r=[kbtok], w=['kTb%d' % nxt])
                    P.pool(lambda e: e.tensor_copy(out=Vb[nxt][:, 0:1024], in_=vb[:, 256:1280]), r=[vbtok], w=['Vb%d' % nxt])

            chunks = [(0, 64), (64, 64)] if T == 128 else [(0, T)]

            def B_common():
                for ci, (c0, L) in enumerate(chunks):
                    pvv, tvv = bank()
                    for k in range(8):
                        P.pe(lambda e: e.matmul(pvv[:L, 0:512], lhsT=xT[:, k * 128 + c0:k * 128 + c0 + L],
                                                rhs=wa[:, k, 1536:2048], start=(k == 0), stop=(k == 7)),
                             r=['W', xTtok], w=[tvv])
                    if ci == 0:
                        P.act(lambda e: e.copy(out=vbf[:L, ci * 512:(ci + 1) * 512], in_=pvv[:L, 0:512]), r=[tvv], w=['vbf%d' % ci])
                    else:
                        P.dve(lambda e: e.tensor_copy(out=vbf[:L, ci * 512:(ci + 1) * 512], in_=pvv[:L, 0:512]), r=[tvv], w=['vbf%d' % ci])

            def B_head(h):
                s_ = h
                B_ = HS[s_]
                q32, ff, lf, kk32, Bc, nB, E1, E2 = (B_[n_] for n_ in ('q32', 'ff', 'lf', 'kk32', 'Bc', 'nB', 'E1', 'E2'))
                qh, kh, qt, Kb, osq, sgl, on, rstd_h, hs, o32 = (B_[n_] for n_ in ('qh', 'kh', 'qt', 'Kb', 'osq', 'sgl', 'on', 'rstd_h', 'hs', 'o32'))

                def tk(n_):
                    return '%s_%d' % (n_, s_)
                ps3, t3 = bank()
                for gi, col in enumerate((512 + h * 128, 1024 + h * 128, 2048 + h * 128)):
                    for k in range(8):
                        P.pe(lambda e: e.matmul(ps3[:, gi * 128:gi * 128 + T], lhsT=wa[:, k, col:col + 128], rhs=xT[:, k * 128:k * 128 + T],
                                                start=(k == 0), stop=(k == 7)), r=['W', xTtok], w=[t3])
                sig = B_['sig']
                sig3 = sig.rearrange("p (g t) -> p g t", t=128)[:, :, :T]
                ps33 = ps3[:, 0:384].rearrange("p (g t) -> p g t", t=128)[:, :, :T]
                P.act(lambda e: e.activation(out=sig3, in_=ps33, func=AF.Exp, scale=-1.0), r=[t3], w=[tk('sig')])
                P.dve(lambda e: e.tensor_copy(out=q32[:, :T], in_=ps3[:, 0:T]), r=[t3], w=[tk('q32')])
                P.dve(lambda e: e.tensor_copy(out=sgl[:, :T], in_=ps3[:, 256:256 + T]), r=[t3], w=[tk('sgl')])
                yield
                P.act(lambda e: e.activation(out=sig3, in_=sig3, func=AF.Ln, bias=epsc[:, 1:2], scale=1.0), r=[tk('sig'), 'epsc'], w=[tk('sig')])
                yield
                P.act(lambda e: e.activation(out=sig3, in_=sig3, func=AF.Exp, scale=-1.0), r=[tk('sig')], w=[tk('sig')])
                yield
                P.dve(lambda e: e.tensor_tensor(out=q32[:, :T], in0=q32[:, :T], in1=sig[:, 0:T], op=ALU.mult),
                      r=[tk('q32'), tk('sig')], w=[tk('q32')])
                P.dve(lambda e: e.tensor_scalar(out=ff[:, :T], in0=sig[:, 128:128 + T], scalar1=oml[:, h:h + 1], scalar2=lb[:, h:h + 1],
                                                op0=ALU.mult, op1=ALU.add), r=[tk('sig'), 'oml', 'lb'], w=[tk('ff')])
                P.dve(lambda e: e.tensor_tensor(out=sgl[:, :T], in0=sgl[:, :T], in1=sig[:, 256:256 + T], op=ALU.mult),
                      r=[tk('sgl'), tk('sig')], w=[tk('sgl')])
                yield
                P.act(lambda e: e.activation(out=lf[:, :T], in_=ff[:, :T], func=AF.Ln), r=[tk('ff')], w=[tk('lf')])
                P.dve(lambda e: e.tensor_scalar(out=kk32[:, :T], in0=ff[:, :T], scalar1=-1.0, scalar2=1.0, op0=ALU.mult, op1=ALU.add),
                      r=[tk('ff')], w=[tk('kk32')])
                yield
                P.dve(lambda e: e.tensor_tensor_scan(out=Bc[:, :T], data0=ones_f[:, :T], data1=lf[:, :T], initial=0.0,
                                                     op0=ALU.mult, op1=ALU.add), r=['ones_f', tk('lf')], w=[tk('Bc')])
                yield
                P.pool(lambda e: e.tensor_tensor(out=nB[:, :T], in0=Bc[:, :T], in1=negones[:, :T], op=ALU.mult),
                       r=[tk('Bc'), 'negones'], w=[tk('nB')])
                yield
                for ci, (c0, L) in enumerate(chunks):
                    r_ = c0 + L // 2 - 1
                    en = c0 + L - 1
                    sl = slice(c0, c0 + L)
                    hc = hs[:, 4 * ci:4 * ci + 4]
                    hct = tk('hs%d' % ci)
                    ATm, KbT = B_['ATm'][ci], B_['KbT'][ci]
                    att, kbt = tk('ATm%d' % ci), tk('KbT%d' % ci)
                    P.act(lambda e: e.activation(out=E1[:, sl], in_=Bc[:, sl], func=AF.Exp, bias=nB[:, r_:r_ + 1], scale=1.0),
                          r=[tk('Bc'), tk('nB'), tk('q32')], w=[tk('E1')])
                    P.act(lambda e: e.activation(out=E2[:, sl], in_=Bc[:, sl], func=AF.Exp, bias=Bc[:, r_:r_ + 1], scale=-1.0),
                          r=[tk('Bc')], w=[tk('E2')])
                    if c0 == 0:
                        P.act(lambda e: e.activation(out=hc[:, 0:1], in_=Bc[:, r_:r_ + 1], func=AF.Exp), r=[tk('Bc')], w=[hct])
                        P.act(lambda e: e.activation(out=hc[:, 2:3], in_=Bc[:, en:en + 1], func=AF.Exp), r=[tk('Bc')], w=[hct])
                    else:
                        P.act(lambda e: e.activation(out=hc[:, 0:1], in_=Bc[:, r_:r_ + 1], func=AF.Exp,
                                                     bias=nB[:, c0 - 1:c0], scale=1.0), r=[tk('Bc'), tk('nB')], w=[hct])
                        P.act(lambda e: e.activation(out=hc[:, 2:3], in_=Bc[:, en:en + 1], func=AF.Exp,
                                                     bias=nB[:, c0 - 1:c0], scale=1.0), r=[tk('Bc'), tk('nB')], w=[hct])
                    P.act(lambda e: e.activation(out=hc[:, 1:2], in_=Bc[:, en:en + 1], func=AF.Exp,
                                                 bias=nB[:, r_:r_ + 1], scale=1.0), r=[tk('Bc'), tk('nB')], w=[hct])
                    yield
                    P.dve(lambda e: e.tensor_tensor(out=qh[:, sl], in0=q32[:, sl], in1=E1[:, sl], op=ALU.mult), r=[tk('q32'), tk('E1')], w=[tk('qh')])
                    P.pool(lambda e: e.tensor_tensor(out=kh[:, sl], in0=kk32[:, sl], in1=E2[:, sl], op=ALU.mult), r=[tk('kk32'), tk('E2')], w=[tk('kh')])
                    P.dve(lambda e: e.scalar_tensor_tensor(out=qt[:, sl], in0=E1[:, sl], scalar=hc[:, 0:1], in1=q32[:, sl],
                                                           op0=ALU.mult, op1=ALU.mult), r=[tk('E1'), hct, tk('q32')], w=[tk('qt')])
                    P.dve(lambda e: e.scalar_tensor_tensor(out=Kb[:, sl], in0=E2[:, sl], scalar=hc[:, 1:2], in1=kk32[:, sl],
                                                           op0=ALU.mult, op1=ALU.mult), r=[tk('E2'), hct, tk('kk32')], w=[tk('Kb')])
                    yield
                    pa, pat = bank()
                    P.pe(lambda e: e.matmul(pa[:L, :L], lhsT=kh[:, sl], rhs=qh[:, sl], start=True, stop=True),
                         r=[tk('kh'), tk('qh')], w=[pat])
                    pkt, pktt = bank()
                    P.pe(lambda e: e.matmul(pkt[:L, 0:128], lhsT=Kb[:, sl], rhs=ident_b[:, :], start=True, stop=True),
                         r=[tk('Kb'), 'ident_b'], w=[pktt])
                    P.dve(lambda e: e.tensor_tensor(out=ATm[:L, :L], in0=pa[:L, :L], in1=maskT[:L, :L], op=ALU.mult),
                          r=[pat, 'maskT'], w=[att])
                    P.dve(lambda e: e.tensor_copy(out=KbT[:L, :], in_=pkt[:L, 0:128]), r=[pktt], w=[kbt])
                    yield
                    poh, poht = bank()
                    P.pe(lambda e: e.matmul(poh[:, 0:L], lhsT=vbf[:L, ci * 512 + h * 128:ci * 512 + (h + 1) * 128],
                                            rhs=ATm[:L, :L], start=True, stop=False),
                         r=['vbf%d' % ci, att], w=[poht])
                    P.pe(lambda e: e.matmul(poh[:, 0:L], lhsT=Sbf[:, h * 128:(h + 1) * 128], rhs=qt[:, sl], start=False, stop=True),
                         r=['Sbf_%d' % h, tk('qt')], w=[poht])
                    pds, pdst = bank()
                    P.pe(lambda e: e.matmul(pds[:, 0:128], lhsT=KbT[:L, :],
                                            rhs=vbf[:L, ci * 512 + h * 128:ci * 512 + (h + 1) * 128], start=True, stop=True),
                         r=[kbt, 'vbf%d' % ci], w=[pdst])
                    P.dve(lambda e: e.tensor_copy(out=o32[:, sl], in_=poh[:, 0:L]), r=[poht], w=[tk('o32')])
                    P.dve(lambda e: e.scalar_tensor_tensor(out=S32[:, h * 128:(h + 1) * 128], in0=S32[:, h * 128:(h + 1) * 128],
                                                           scalar=hc[:, 2:3], in1=pds[:, 0:128], op0=ALU.mult, op1=ALU.add),
                          r=['S32_%d' % h, hct, pdst], w=['S32_%d' % h])
                    yield
                    P.pool(lambda e: e.tensor_copy(out=Sbf[:, h * 128:(h + 1) * 128], in_=S32[:, h * 128:(h + 1) * 128]),
                           r=['S32_%d' % h], w=['Sbf_%d' % h])
                    yield
                P.pool(lambda e: e.tensor_tensor(out=osq[:, :T], in0=o32[:, :T], in1=o32[:, :T], op=ALU.mult), r=[tk('o32')], w=[tk('osq')])
                yield
                pms, pmst = bank()
                P.pe(lambda e: e.matmul(pms[:, :T], lhsT=o128[:, :], rhs=osq[:, :T], start=True, stop=True), r=['o128', tk('osq')], w=[pmst])
                P.act(lambda e: e.activation(out=on[:, :T], in_=pms[:, :T], func=AF.Ln, scale=1.0, bias=epsc[:, 0:1]),
                      r=[pmst, 'epsc'], w=[tk('on')])
                yield
                P.act(lambda e: e.activation(out=rstd_h[:, :T], in_=on[:, :T], func=AF.Exp, scale=-0.5), r=[tk('on')], w=[tk('rstd_h')])
                yield
                P.dve(lambda e: e.tensor_tensor(out=on[:, :T], in0=o32[:, :T], in1=rstd_h[:, :T], op=ALU.mult),
                      r=[tk('o32'), tk('rstd_h')], w=[tk('on')])
                yield
                P.pool(lambda e: e.tensor_tensor(out=feat[:, 256 + h * 128:256 + h * 128 + T], in0=on[:, :T], in1=sgl[:, :T], op=ALU.mult),
                       r=[tk('on'), tk('sgl')], w=[FT[2 + h]])
                yield

            C_common()
            B_common()
            def lane(*gs):
                for g_ in gs:
                    yield from g_
            gens = [brA(), lane(C_head(0), C_head(2)), lane(C_head(1), C_head(3))] + [B_head(h) for h in range(4)]
            while gens:
                for g in list(gens):
                    try:
                        next(g)
                    except StopIteration:
                        gens.remove(g)
            C_final()
            if so.get('hgrn') is not None:
                P.dma(lambda e: e.dma_start(out=so['hgrn'], in_=S32), r=['S32_%d' % h for h in range(4)], key='so')
            P.dma(lambda e: e.dma_start(out=feat_dst, in_=feat), r=FT, w=[fttok_d], key='fo')

        P.barrier()
        init_states()
        items = []
        cur = 0
        for i in range(first_tile, NT):
            src = (xp if l == 0 else x1_d)[i * 128:(i + 1) * 128, :]
            kmoff = (i - first_tile) * 128
            if kmoff >= 512 + halo:
                kmoff = None
            so = None
            if i >= NT - 4:
                q = i - (NT - 4)
                so = dict(k=[kp_o[l][:, j * 512 + q * 128:j * 512 + (q + 1) * 128] for j in range(2)],
                          v=vp_o[l][q * 128:(q + 1) * 128, :])
                if i == NT - 1:
                    so['conv'] = convp_o[l]
                    so['hgrn'] = hgrnp_o[l]
            items.append(dict(src=src, srctok=(('x1', 'p', i) if l == 1 else None), T=128,
                              call=(lambda pro, slot, i=i, cur=cur, kmoff=kmoff, so=so:
                                    tile(pro, 128, ft_d[i], ('ft', 'p', i), cur, kmoff, so, True))))
            cur = 1 - cur
        S_all = ['S32_%d' % h for h in range(4)]
        Sb_all = ['Sbf_%d' % h for h in range(4)]

        def load_states(n):
            P.dma(lambda e: e.dma_start(out=stg[0][:, 0:60], in_=sconvT[l, n]), w=['stg0'], key='si')
            P.pool(lambda e: e.tensor_copy(out=ubf3[:, :, 0:30], in_=stg[0][:, 0:60].rearrange("p (j n) -> p j n", n=30)),
                   r=['stg0'], w=['ubf'])
            P.dma(lambda e: e.dma_start(out=S32, in_=shgrn[l, n]), w=S_all, key='si')
            P.pool(lambda e: e.tensor_copy(out=Sbf, in_=S32), r=S_all, w=Sb_all)
            P.dma(lambda e: e.dma_start(out=stg[1][:, 0:1024], in_=skT[l, n]), w=['stg1'], key='si')
            P.pool(lambda e: e.tensor_copy(out=kTb[0].rearrange("p (j n) -> p j n", n=640)[:, :, 0:512],
                                           in_=stg[1][:, 0:1024].rearrange("p (j n) -> p j n", n=512)), r=['stg1'], w=['kTb0'])
            P.dma(lambda e: e.dma_start(out=stg[0][:, 0:1024], in_=sv[l, n]), w=['stg0'], key='si')
            P.pool(lambda e: e.tensor_copy(out=Vb[0][:, 0:1024], in_=stg[0][:, 0:1024]), r=['stg0'], w=['Vb0'])

        for n in range(NSEQ):
            src = (xs if l == 0 else x1s_d)[n * TS:(n + 1) * TS, :]
            so = dict(conv=convs_o[l, n], hgrn=hgrns_o[l, n], k=[ks_o[l, n][:, j * 32:(j + 1) * 32] for j in range(2)], v=vs_o[l, n])
            items.append(dict(src=src, srctok=(('x1', 's', n) if l == 1 else None), T=TS,
                              pre=(lambda n=n: load_states(n)),
                              call=(lambda pro, slot, n=n, so=so: tile(pro, TS, fts_d[n], ('ft', 's', n), 0, None, so, False))))
        run_tiles(items)

    def pass_A2(l, halo):
        first_tile = (W - halo) // 128
        wg = w3(0, 8, 3072)
        wco = w3(24576, 2, 1024)
        whg = w3(24576 + 2048, 4, 1024)
        wat = w3(24576 + 2048 + 4096, 2, 1024)
        wmx = w3(24576 + 2048 + 4096 + 2048, 8, 1024)
        P.barrier()
        P.dma(lambda e: e.dma_start(out=gT[:, :], in_=gmixT[l]), w=['gT'], key='c1')
        P.dma(lambda e: e.dma_start(out=hng[:, :], in_=hng_in[l]), w=['lconst'], key='c1', cont=True)
        for k in range(8):
            load_weight(wg[:, k, :], 'W', w_in[l, k * 128:(k + 1) * 128, 3328:6400], 3072, gT[:, k:k + 1], 'gT')
        for j in range(2):
            load_weight(wco[:, j, :], 'W', w_conv_out[l, j * 128:(j + 1) * 128, :], 1024)
        for h in range(4):
            load_weight(whg[:, h, :], 'W', w_hgrn_out[l, h * 128:(h + 1) * 128, :], 1024, hng[:, h:h + 1], 'lconst')
        for j in range(2):
            load_weight(wat[:, j, :], 'W', w_attn_out[l, j * 128:(j + 1) * 128, :], 1024)
        for k in range(8):
            load_weight(wmx[:, k, :], 'W', w_mix_out[l, k * 128:(k + 1) * 128, :], 1024)
        c = Carve(24576 + 2048 + 4096 + 2048 + 8192)
        xn_s[1] = c.bf(1024)
        xnT_s[1] = c.bf(1024)
        ftile = [c.bf(1024), c.bf(1024)]
        gsb = [c.f32(512) for _ in range(3)]
        tmp = [c.f32(512) for _ in range(2)]
        m32 = [c.f32(1024), c.f32(1024)]
        mbf = [c.bf(1024), c.bf(1024)]
        mT = [c.bf(1024), c.bf(1024)]
        hout_s = [hout, c.f32(1024)]
        branches = ((0, wco, (0, 1)), (1024, whg, (2, 3, 4, 5)), (2048, wat, (6, 7)))

        def tile_front(pro, fsrc, fsrctok, T, slot):
            x_t, xtok, xT, xTtok = pro
            ft, fttok = ftile[slot], 'ft%d' % slot
            m32_, mbf_, mT_, ho = m32[slot], mbf[slot], mT[slot], hout_s[slot]
            m32t, mbft, mTt, hot = 'm32_%d' % slot, 'mbf%d' % slot, 'mT%d' % slot, 'hout%d' % slot
            P.dma(lambda e: e.dma_start(out=ft, in_=fsrc), r=[fsrctok], w=[fttok], key=fttok)
            for cb in range(2):
                for bi, (goff, wv, chunks) in enumerate(branches):
                    pg, tg = proj_tok(wg, 'W', goff + cb * 512, 512, T, xT, xTtok)
                    g_, gt = gsb[bi], 'gsb%d' % bi
                    P.act(lambda e: e.activation(out=g_[:T, :], in_=pg[:T, :], func=AF.Sigmoid), r=[tg], w=[gt])
                    py, ty = bank()
                    n = len(chunks)
                    for ci, ch in enumerate(chunks):
                        P.pe(lambda e: e.matmul(py[:T, :], lhsT=ft[:, ch * 128:ch * 128 + T], rhs=wv[:, ci, cb * 512:(cb + 1) * 512],
                                                start=(ci == 0), stop=(ci == n - 1)), r=[fttok, 'W'], w=[ty])
                    mc = m32_[:T, cb * 512:(cb + 1) * 512]
                    mct = '%s_%d' % (m32t, cb)
                    if bi == 0:
                        P.dve(lambda e: e.tensor_tensor(out=mc, in0=py[:T, :], in1=g_[:T, :], op=ALU.mult), r=[ty, gt], w=[mct])
                    else:
                        t_, tt = tmp[bi - 1], 'tmp%d' % (bi - 1)
                        P.dve(lambda e: e.tensor_tensor(out=t_[:T, :], in0=py[:T, :], in1=g_[:T, :], op=ALU.mult), r=[ty, gt], w=[tt])
                        if bi == 1:
                            P.pool(lambda e: e.tensor_tensor(out=mc, in0=mc, in1=t_[:T, :], op=ALU.add), r=[mct, tt], w=[mct])
                        else:
                            P.pool(lambda e: e.tensor_tensor(out=mbf_[:T, cb * 512:(cb + 1) * 512], in0=mc, in1=t_[:T, :], op=ALU.add),
                                   r=[mct, tt], w=['%s_%d' % (mbft, cb)])

        def tile_back(pro, dst_ap, dsttok, T, slot, vcol):
            x_t, xtok, xT, xTtok = pro
            m32_, mbf_, mT_, ho = m32[slot], mbf[slot], mT[slot], hout_s[slot]
            m32t, mbft, mTt, hot = 'm32_%d' % slot, 'mbf%d' % slot, 'mT%d' % slot, 'hout%d' % slot
            for half in range(2):
                ps, ptok = bank()
                for kk in range(4):
                    k = half * 4 + kk
                    P.pe(lambda e: e.matmul(ps[:, kk * 128:kk * 128 + T], lhsT=mbf_[:T, k * 128:(k + 1) * 128],
                                            rhs=ident_b[:T, :T], start=True, stop=True), r=['%s_%d' % (mbft, half), 'ident_b'], w=[ptok])
                dst = mT_[:, half * 512:(half + 1) * 512].rearrange("p (k t) -> p k t", t=128)[:, :, :T]
                srcp = ps[:, :].rearrange("p (k t) -> p k t", t=128)[:, :, :T]
                if half == 0:
                    P.dve(lambda e: e.tensor_copy(out=dst, in_=srcp), r=[ptok], w=[mTt])
                else:
                    P.act(lambda e: e.copy(out=dst, in_=srcp), r=[ptok], w=[mTt])
            for cb in range(2):
                ps, ptok = bank()
                for k in range(8):
                    P.pe(lambda e: e.matmul(ps[:T, :], lhsT=mT_[:, k * 128:k * 128 + T], rhs=wmx[:, k, cb * 512:(cb + 1) * 512],
                                            start=(k == 0), stop=(k == 7)), r=[mTt, 'W'], w=[ptok])
                P.dve(lambda e: e.tensor_tensor(out=ho[:T, cb * 512:(cb + 1) * 512], in0=ps[:T, :], in1=x_t[:T, cb * 512:(cb + 1) * 512],
                                                op=ALU.add), r=[ptok, xtok], w=[hot])
            if vcol is not None:
                P.dve(lambda e: e.tensor_scalar(out=ho[:T, :], in0=ho[:T, :], scalar1=valid_sb[:T, vcol:vcol + 1], scalar2=None, op0=ALU.mult),
                       r=[hot, 'valid_sb'], w=[hot])
            P.dma(lambda e: e.dma_start(out=dst_ap, in_=ho[:T, :]), r=[hot], w=[dsttok], key='ho%d' % slot)

        P.barrier()
        items = []
        for i in range(first_tile, NT):
            src = (xp if l == 0 else x1_d)[i * 128:(i + 1) * 128, :]
            items.append(dict(src=src, srctok=(('x1', 'p', i) if l == 1 else None), T=128,
                              front=(lambda pro, slot, i=i: tile_front(pro, ft_d[i], ('ft', 'p', i), 128, slot)),
                              back=(lambda pro, slot, i=i: tile_back(pro, hm_d[i * 128:(i + 1) * 128, :], ('hm', 'p', i),
                                                                     128, slot, i if i < W // 128 else None))))
        for n in range(NSEQ):
            src = (xs if l == 0 else x1s_d)[n * TS:(n + 1) * TS, :]
            items.append(dict(src=src, srctok=(('x1', 's', n) if l == 1 else None), T=TS,
                              front=(lambda pro, slot, n=n: tile_front(pro, fts_d[n], ('ft', 's', n), TS, slot)),
                              back=(lambda pro, slot, n=n: tile_back(pro, hms_d[n * TS:(n + 1) * TS, :], ('hm', 's', n), TS, slot, None))))
        run_tiles(items)

    def pass_B(l, halo):
        first_tile = (W - halo) // 128
        wup = w3(0, 8, 5632)
        wdn = w3(45056, 22, 1024)
        P.barrier()
        P.dma(lambda e: e.dma_start(out=gT[:, :], in_=gffnT[l]), w=['gT'], key='c1')
        P.dma(lambda e: e.dma_start(out=fdw[:, :], in_=fdw_in[l]), w=['lconst'], key='c1', cont=True)
        for k in range(8):
            load_weight(wup[:, k, :], 'W', w_ffn_up[l, k * 128:(k + 1) * 128, :], 5632, gT[:, k:k + 1], 'gT')
        for cch in range(22):
            load_weight(wdn[:, cch, :], 'W', w_ffn_down[l, cch * 128:(cch + 1) * 128, :], 1024)
        c = Carve()
        ub = c.f32(44 * 130)
        ub3 = ub.rearrange("p (c n) -> p c n", n=130)
        t1b = [[c.f32(128), c.f32(128)] for _ in range(2)]
        ucb = [[c.f32(128), c.f32(128)] for _ in range(2)]
        sa = [c.f32(128), c.f32(128)]
        gTt = c.bf(22 * 128)
        xn_s[1] = xn_s[0]
        xn_tok[1] = 'xn0'
        xnT_s[1] = c.bf(1024)
        UB = [('ub', cc) for cc in range(44)]

        def tile(pro, dst_ap, dsttok, T, final_out, state_out):
            x_t, xtok, xT, xTtok = pro

            def S1(c2):
                st = c2 % 2
                for half in range(2):
                    cc = c2 + 22 * half
                    ps, ptok = proj_feat(wup, 'W', cc * 128, T, xT, xTtok)
                    t1, t1t = t1b[st][half], 't1_%d_%d' % (st, half)
                    P.act(lambda e: e.copy(out=ub[:, cc * 130 + 2:cc * 130 + 2 + T], in_=ps[:, :T]), r=[ptok], w=[('ub', cc)])
                    P.pool(lambda e: e.tensor_tensor(out=t1[:, :T], in0=ub[:, cc * 130:cc * 130 + T],
                                                     in1=fdw[:, cc * 3:cc * 3 + 1].to_broadcast([128, T]), op=ALU.mult),
                           r=[('ub', cc), 'lconst'], w=[t1t])

            def S2(c2):
                st = c2 % 2
                for half in range(2):
                    cc = c2 + 22 * half
                    t1, t1t = t1b[st][half], 't1_%d_%d' % (st, half)
                    uc, uct = ucb[st][half], 'uc_%d_%d' % (st, half)
                    P.dve(lambda e: e.scalar_tensor_tensor(out=t1[:, :T], in0=ub[:, cc * 130 + 1:cc * 130 + 1 + T],
                                                           scalar=fdw[:, cc * 3 + 1:cc * 3 + 2], in1=t1[:, :T], op0=ALU.mult, op1=ALU.add),
                          r=[('ub', cc), 'lconst', t1t], w=[t1t])
                    P.dve(lambda e: e.scalar_tensor_tensor(out=uc[:, :T], in0=ub[:, cc * 130 + 2:cc * 130 + 2 + T],
                                                           scalar=fdw[:, cc * 3 + 2:cc * 3 + 3], in1=t1[:, :T], op0=ALU.mult, op1=ALU.add),
                          r=[('ub', cc), 'lconst', t1t], w=[uct])

            def S3(c2):
                st = c2 % 2
                sa_, sat = sa[st], 'sa%d' % st
                P.act(lambda e: e.activation(out=sa_[:, :T], in_=ucb[st][0][:, :T], func=AF.Silu), r=['uc_%d_0' % st], w=[sat])
                P.pool(lambda e: e.tensor_tensor(out=gTt[:, c2 * 128:c2 * 128 + T], in0=sa_[:, :T], in1=ucb[st][1][:, :T], op=ALU.mult),
                       r=[sat, 'uc_%d_1' % st], w=[('gTt', c2)])

            for step in range(22 + 2):
                if step < 22:
                    S1(step)
                if 1 <= step < 23:
                    S2(step - 1)
                if step >= 2:
                    S3(step - 2)
            if state_out is not None:
                P.dma(lambda e: e.dma_start(out=state_out.rearrange("p (c n) -> p c n", n=2), in_=ub3[:, :, T:T + 2]), r=UB, key='so')
            else:
                P.pool(lambda e: e.tensor_copy(out=ub3[:, :, 0:2], in_=ub3[:, :, T:T + 2]), r=UB, w=UB)
            for cb in range(2):
                ps, ptok = bank()
                for c2 in range(22):
                    P.pe(lambda e: e.matmul(ps[:T, :], lhsT=gTt[:, c2 * 128:c2 * 128 + T], rhs=wdn[:, c2, cb * 512:(cb + 1) * 512],
                                            start=(c2 == 0), stop=(c2 == 21)), r=[('gTt', c2), 'W'], w=[ptok])
                P.dve(lambda e: e.tensor_tensor(out=hout[:T, cb * 512:(cb + 1) * 512], in0=ps[:T, :], in1=x_t[:T, cb * 512:(cb + 1) * 512],
                                                op=ALU.add), r=[ptok, xtok], w=['hout'])
            if final_out is None:
                P.dma(lambda e: e.dma_start(out=dst_ap, in_=hout[:T, :]), r=['hout'], w=[dsttok], key='ho')
            elif final_out is not False:
                P.act(lambda e: e.activation(out=junk[:T, :], in_=hout[:T, :], func=AF.Square, accum_out=sm[:T, 8:9]),
                      r=['hout'], w=['junk', 'sm8'])
                P.act(lambda e: e.activation(out=sm[:T, 9:10], in_=sm[:T, 8:9], func=AF.Ln, scale=1.0 / D, bias=epsc[:T, 0:1]),
                      r=['sm8', 'epsc'], w=['sm9'])
                P.act(lambda e: e.activation(out=sm[:T, 10:11], in_=sm[:T, 9:10], func=AF.Exp, scale=-0.5), r=['sm9'], w=['sm10'])
                P.dve(lambda e: e.scalar_tensor_tensor(out=hout[:T, :], in0=hout[:T, :], scalar=sm[:T, 10:11], in1=gfin[:T, :],
                                                       op0=ALU.mult, op1=ALU.mult), r=['hout', 'sm10', 'gfin'], w=['hout'])
                P.dma(lambda e: e.dma_start(out=final_out, in_=hout[:T, :]), r=['hout'], key='ho')

        P.barrier()
        P.pool(lambda e: e.memset(ub, 0.0), w=UB)
        items = []
        for i in range(first_tile, NT):
            src = hm_d[i * 128:(i + 1) * 128, :]
            so = ffnp_o[l] if i == NT - 1 else None
            if l == 0:
                call = (lambda pro, slot, i=i, so=so: tile(pro, x1_d[i * 128:(i + 1) * 128, :], ('x1', 'p', i), 128, None, so))
            else:
                fo = y_o[(i - W // 128) * 128:(i - W // 128 + 1) * 128, :] if i >= W // 128 else False
                call = (lambda pro, slot, fo=fo, so=so: tile(pro, None, None, 128, fo, so))
            items.append(dict(src=src, srctok=('hm', 'p', i), T=128, call=call))

        def load_ffn_state(n):
            P.dma(lambda e: e.dma_start(out=stg[0][:, 0:88], in_=sffnT[l, n]), w=['stg0'], key='si')
            P.pool(lambda e: e.tensor_copy(out=ub3[:, :, 0:2], in_=stg[0][:, 0:88].rearrange("p (c n) -> p c n", n=2)), r=['stg0'], w=UB)

        for n in range(NSEQ):
            src = hms_d[n * TS:(n + 1) * TS, :]
            if l == 0:
                call = (lambda pro, slot, n=n: tile(pro, x1s_d[n * TS:(n + 1) * TS, :], ('x1', 's', n), TS, None, ffns_o[l, n]))
            else:
                call = (lambda pro, slot, n=n: tile(pro, None, None, TS, ys_o[n * TS:(n + 1) * TS, :], ffns_o[l, n]))
            items.append(dict(src=src, srctok=('hm', 's', n), T=TS, pre=(lambda n=n: load_ffn_state(n)), call=call))
        run_tiles(items)

    import os as _os
    npass = int(_os.environ.get("MK_NPASS", "6"))
    plist = [lambda: pass_A1(0, hA0, km_in[0]), lambda: pass_A2(0, hA0), lambda: pass_B(0, hB0),
             lambda: pass_A1(1, hA1, km_in[1]), lambda: pass_A2(1, hA1), lambda: pass_B(1, hB1)]
    for pf in plist[:npass]:
        pf()
    P.emit(nc, stack)
    stack.close()
    return nc


def _rel_table(rel_bias):
    p = np.arange(128)[:, None]
    x = np.arange(640)[None, :]
    idx = np.clip(512 + p - x, -128, 128) + 128
    R = rel_bias[:, :, idx]
    R = np.ascontiguousarray(R.transpose(0, 2, 1, 3)).copy()
    R[:, 0:64, :, 576:640] = NEG
    R[:, 64:128, :, 0:64] = NEG
    return R.reshape(2, 128, 2560).astype(np.float32)


def kernel(x_prompt, x_sample, state_conv, state_hgrn, cache_attn_k, cache_attn_v, state_ffn,
           w_in, conv_dw_w, conv_dw_b, conv_ln_g, conv_ln_b, w_conv_out,
           hgrn_lb_logits, hgrn_norm_g, w_hgrn_out, attn_rel_bias, w_attn_out,
           w_mix_out, g_mix, w_ffn_up, ffn_dw_w, w_ffn_down, g_ffn, g_final,
           _seg=SEG, _halos=(H_A0, H_B0, H_A1, H_B1)):
    f = lambda a: np.ascontiguousarray(np.asarray(a, dtype=np.float32))
    x_prompt, x_sample = f(x_prompt), f(x_sample)
    seg, halos = _seg, _halos
    W = halos[0]
    NTOK = W + seg
    NT = NTOK // 128
    nb, seq = x_prompt.shape[0], x_prompt.shape[1]
    cps = seq // seg
    assert nb * cps == N_CORES
    nc = build_program(seg, halos)

    def fm(v, nch):
        v = f(v)
        return np.ascontiguousarray(v.reshape(v.shape[:-1] + (nch, 128)).swapaxes(-1, -2))

    shared = dict(
        w_in=f(w_in), w_conv_out=f(w_conv_out), w_hgrn_out=f(w_hgrn_out), w_attn_out=f(w_attn_out),
        w_mix_out=f(w_mix_out), w_ffn_up=f(w_ffn_up), w_ffn_down=f(w_ffn_down),
        gmixT=fm(g_mix, 8), gffnT=fm(g_ffn, 8),
        dwT=np.ascontiguousarray(f(conv_dw_w).reshape(2, 31, 2, 128).transpose(0, 3, 2, 1)).reshape(2, 128, 62),
        dwb=fm(conv_dw_b, 2), lng=fm(conv_ln_g, 2), lnb=fm(conv_ln_b, 2),
        lbl=fm(hgrn_lb_logits, 4), hng=fm(hgrn_norm_g, 4),
        fdw=np.ascontiguousarray(f(ffn_dw_w).reshape(2, 3, 44, 128).transpose(0, 3, 2, 1)).reshape(2, 128, 132),
        R=_rel_table(f(attn_rel_bias)),
        gfin=np.ascontiguousarray(np.broadcast_to(f(g_final)[None, :], (128, D))),
    )
    sc_, sh_, sk_, sv_, sf_ = f(state_conv), f(state_hgrn), f(cache_attn_k), f(cache_attn_v), f(state_ffn)
    in_maps = []
    for c in range(N_CORES):
        b, ci = c // cps, c % cps
        s0 = ci * seg
        xpc = np.zeros((NTOK, D), np.float32)
        lo = max(0, s0 - W)
        xpc[W - (s0 - lo):] = x_prompt[b, lo:s0 + seg]
        pos = np.arange(s0 - W, s0 + seg)
        vt = (pos >= 0).astype(np.float32)
        valid = np.ascontiguousarray(vt.reshape(NT, 128).T)
        kms = []
        for hl in (halos[0], halos[2]):
            row = np.full((512 + hl,), NEG, np.float32)
            row[512:] = np.where(np.arange(s0 - hl, s0) >= 0, 0.0, NEG)
            kms.append(np.ascontiguousarray(np.broadcast_to(row[None], (128, 512 + hl))))
        sl = slice(NSEQ * c, NSEQ * (c + 1))
        m = dict(shared)
        m.update(
            xp=xpc, valid=valid, km0=kms[0], km1=kms[1],
            xs=np.ascontiguousarray(x_sample[sl].reshape(NSEQ * TS, D)),
            sconvT=np.ascontiguousarray(sc_[:, sl].reshape(2, NSEQ, 30, 2, 128).transpose(0, 1, 4, 3, 2)).reshape(2, NSEQ, 128, 60),
            shgrn=np.ascontiguousarray(sh_[:, sl].transpose(0, 1, 3, 2, 4)).reshape(2, NSEQ, 128, 512),
            skT=np.ascontiguousarray(sk_[:, sl].reshape(2, NSEQ, 512, 2, 128).transpose(0, 1, 4, 3, 2)).reshape(2, NSEQ, 128, 1024),
            sv=np.ascontiguousarray(sv_[:, sl].reshape(2, NSEQ, 4, 128, 256).transpose(0, 1, 3, 2, 4)).reshape(2, NSEQ, 128, 1024),
            sffnT=np.ascontiguousarray(sf_[:, sl].reshape(2, NSEQ, 2, 44, 128).transpose(0, 1, 4, 3, 2)).reshape(2, NSEQ, 128, 88),
        )
        in_maps.append(m)
    res = run_bass_kernel_spmd(nc, in_maps, core_ids=list(range(N_CORES))).results
    ndb = x_sample.shape[0]
    y_prompt = np.stack([np.concatenate([res[b * cps + ci]["y"] for ci in range(cps)], 0) for b in range(nb)])
    y_sample = np.concatenate([res[c]["ys"].reshape(NSEQ, TS, D) for c in range(N_CORES)], 0)
    last = [b * cps + cps - 1 for b in range(nb)]

    def convT(a):
        return a.reshape(a.shape[:-2] + (128, 2, 30)).swapaxes(-3, -1).reshape(a.shape[:-2] + (30, 256))

    def ffnT(a):
        return a.reshape(a.shape[:-2] + (128, 44, 2)).swapaxes(-3, -1).reshape(a.shape[:-2] + (2, 5632))

    def kT(a, n):
        return a.reshape(a.shape[:-2] + (128, 2, n)).swapaxes(-3, -1).reshape(a.shape[:-2] + (n, 4, 64))

    def hg(a):
        return a.reshape(a.shape[:-2] + (128, 4, 128)).swapaxes(-3, -2)

    new_conv_p = np.stack([convT(res[c]["convp"]) for c in last], 1)
    new_hgrn_p = np.stack([hg(res[c]["hgrnp"]) for c in last], 1)
    new_k_p = np.stack([kT(res[c]["kp"], 512) for c in last], 1)
    new_v_p = np.stack([res[c]["vp"].reshape(2, 512, 4, 64) for c in last], 1)
    new_ffn_p = np.stack([ffnT(res[c]["ffnp"]) for c in last], 1)
    new_conv_s = np.concatenate([convT(res[c]["convs"]) for c in range(N_CORES)], 1)
    new_hgrn_s = np.concatenate([hg(res[c]["hgrns"]) for c in range(N_CORES)], 1)
    new_k_s = np.concatenate([kT(res[c]["ks"], TS) for c in range(N_CORES)], 1)
    new_v_s = np.concatenate([res[c]["vs"].reshape(2, NSEQ, TS, 4, 64) for c in range(N_CORES)], 1)
    new_ffn_s = np.concatenate([ffnT(res[c]["ffns"]) for c in range(N_CORES)], 1)
    outs = (y_prompt, y_sample, new_conv_p, new_conv_s, new_hgrn_p, new_hgrn_s,
            new_k_p, new_v_p, new_k_s, new_v_s, new_ffn_p, new_ffn_s)
    return tuple(np.ascontiguousarray(o, dtype=np.float32) for o in outs)
```

```python
import types
import numpy as np
from contextlib import ExitStack
import concourse.bass as bass
import concourse.mybir as mybir
from concourse.bass_utils import run_bass_kernel_spmd

F32 = mybir.dt.float32
BF16 = mybir.dt.bfloat16
AF = mybir.ActivationFunctionType
ALU = mybir.AluOpType
AX = mybir.AxisListType

D = 1024
NIN = 6400
DFF = 2816
EPS = 1e-6
NEG = -30000.0
N_CORES = 8
SEG = 4096
H_A0, H_B0, H_A1, H_B1 = 1280, 768, 640, 128
TS = 32
NSEQ = 2


class Prog:
    def __init__(self):
        self.ops = []

    @staticmethod
    def _freeze(fn):
        if fn.__closure__ is None:
            return fn
        cells = []
        for c in fn.__closure__:
            try:
                cells.append(types.CellType(c.cell_contents))
            except ValueError:
                cells.append(c)
        return types.FunctionType(fn.__code__, fn.__globals__, fn.__name__, fn.__defaults__, tuple(cells))

    def add(self, eng, fn, r=(), w=(), dma=None, cont=False):
        self.ops.append(dict(eng=eng, fn=self._freeze(fn), r=tuple(r), w=tuple(w), dma=dma, cont=cont))

    def pe(self, fn, r=(), w=()):
        self.add('pe', fn, r, w)

    def act(self, fn, r=(), w=()):
        self.add('act', fn, r, w)

    def dve(self, fn, r=(), w=()):
        self.add('dve', fn, r, w)

    def pool(self, fn, r=(), w=()):
        self.add('pool', fn, r, w)

    def dma(self, fn, r=(), w=(), key='d', cont=False, q='sp'):
        self.add(q, fn, r, w, dma=key, cont=cont)

    def barrier(self):
        self.ops.append(dict(eng=None, fn=None, r=(), w=(), dma=None, cont=False))

    def emit(self, nc, stack):
        ops = self.ops
        n = len(ops)
        lastw = {}
        readers = {}
        deps = [None] * n
        dma_hist = {}
        last_eng = {}
        last_dma = {}
        pending = {}
        for i, o in enumerate(ops):
            if o['eng'] is None:
                snap = set(last_eng.values()) | set(last_dma.values())
                for e_ in ('sp', 'pe', 'act', 'dve', 'pool'):
                    pending[e_] = pending.get(e_, set()) | snap
                deps[i] = set()
                continue
            d = set()
            if pending.get(o['eng']):
                d |= pending.pop(o['eng'])
            if o['dma'] is None:
                last_eng[o['eng']] = i
            else:
                last_dma[o['dma']] = i
            for t in o['r']:
                for j in lastw.get(t, ()):
                    d.add(j)
                if isinstance(t, tuple) and t[0] == 'ps':
                    for j in readers.get(t, {}).values():
                        d.add(j)
            for t in o['w']:
                for j in lastw.get(t, ()):
                    d.add(j)
                for j in readers.get(t, {}).values():
                    d.add(j)
            if o['dma'] is not None:
                groups = dma_hist.setdefault(o['dma'], [])
                if o['cont'] and groups:
                    groups[-1].append(i)
                    prev = groups[:-1]
                else:
                    prev = list(groups)
                    groups.append([i])
                if prev:
                    d.add(prev[-1][-1])
                for j in list(d):
                    if j in groups[-1]:
                        d.discard(j)
            d.discard(i)
            deps[i] = d
            for t in o['r']:
                rd = readers.setdefault(t, {})
                if o['dma'] is not None:
                    rd[('dma', i)] = i
                else:
                    rd[o['eng']] = i
            for t in o['w']:
                lastw[t] = [i]
                readers[t] = {}
        dcount = {}
        for key, groups in dma_hist.items():
            c = 0
            for g in groups:
                c += 16 * len(g)
                for i in g:
                    dcount[i] = c
        need = [False] * n
        for i, o in enumerate(ops):
            if o['eng'] is None:
                continue
            for j in deps[i]:
                pj = ops[j]
                if pj['dma'] is not None:
                    continue
                if pj['eng'] == o['eng'] and o['dma'] is None and o['eng'] == 'pe':
                    continue
                need[j] = True
        cnt = [0] * n
        ec = {}
        for i, o in enumerate(ops):
            if o['eng'] is not None and o['dma'] is None and need[i]:
                ec[o['eng']] = ec.get(o['eng'], 0) + 1
                cnt[i] = ec[o['eng']]
        sems = {}
        for e in ('pe', 'act', 'dve', 'pool'):
            sems[e] = stack.enter_context(nc.semaphore('tl_' + e))
        dsems = {}
        for k, key in enumerate(dma_hist):
            dsems[key] = stack.enter_context(nc.semaphore('dq%d' % k))
        block = stack.enter_context(nc.Block())
        engs = ('sp', 'pe', 'act', 'dve', 'pool')
        per = {e: [i for i in range(n) if ops[i]['eng'] == e] for e in engs}
        totals = {key: 16 * sum(len(g) for g in groups) for key, groups in dma_hist.items()}

        def run(eng_name, h):
            seen = {}
            for i in per[eng_name]:
                o = ops[i]
                waits = {}
                for j in deps[i]:
                    pj = ops[j]
                    if pj['dma'] is not None:
                        s, v = ('d', pj['dma']), dcount[j]
                    else:
                        if pj['eng'] == eng_name and eng_name == 'pe' and o['dma'] is None:
                            continue
                        s, v = ('e', pj['eng']), cnt[j]
                    if v > waits.get(s, 0):
                        waits[s] = v
                for s, v in waits.items():
                    if seen.get(s, 0) >= v:
                        continue
                    seen[s] = v
                    sem = dsems[s[1]] if s[0] == 'd' else sems[s[1]]
                    h.wait_ge(sem, v)
                ins = o['fn'](h)
                if o['dma'] is not None:
                    ins.then_inc(dsems[o['dma']], 16)
                elif need[i]:
                    ins.then_inc(sems[eng_name], 1)
            if eng_name == 'sp':
                for key, tot in totals.items():
                    h.wait_ge(dsems[key], tot)

        @block.sync
        def _(h):
            run('sp', h)

        @block.tensor
        def _(h):
            run('pe', h)

        @block.scalar
        def _(h):
            run('act', h)

        @block.vector
        def _(h):
            run('dve', h)

        @block.gpsimd
        def _(h):
            run('pool', h)


def build_program(seg=SEG, halos=(H_A0, H_B0, H_A1, H_B1)):
    hA0, hB0, hA1, hB1 = halos
    W = hA0
    NTOK = W + seg
    NT = NTOK // 128
    nc = bass.Bass("TRN2", target_bir_lowering=False)
    P = Prog()
    stack = ExitStack()

    def din(name, shape, dt=F32):
        return nc.dram_tensor(name, list(shape), dt, kind="ExternalInput").ap()

    def dout(name, shape, dt=F32):
        return nc.dram_tensor(name, list(shape), dt, kind="ExternalOutput").ap()

    def dscr(name, shape, dt=F32):
        return nc.dram_tensor(name, list(shape), dt, kind="Internal").ap()

    xp = din("xp", [NTOK, D])
    valid = din("valid", [128, NT])
    km_in = [din("km0", [128, 512 + hA0]), din("km1", [128, 512 + hA1])]
    xs = din("xs", [NSEQ * TS, D])
    sconvT = din("sconvT", [2, NSEQ, 128, 60])
    shgrn = din("shgrn", [2, NSEQ, 128, 512])
    skT = din("skT", [2, NSEQ, 128, 1024])
    sv = din("sv", [2, NSEQ, 128, 1024])
    sffnT = din("sffnT", [2, NSEQ, 128, 88])
    w_in = din("w_in", [2, D, NIN])
    w_conv_out = din("w_conv_out", [2, 256, D])
    w_hgrn_out = din("w_hgrn_out", [2, 512, D])
    w_attn_out = din("w_attn_out", [2, 256, D])
    w_mix_out = din("w_mix_out", [2, D, D])
    w_ffn_up = din("w_ffn_up", [2, D, 2 * DFF])
    w_ffn_down = din("w_ffn_down", [2, DFF, D])
    gmixT = din("gmixT", [2, 128, 8])
    gffnT = din("gffnT", [2, 128, 8])
    dwT_in = din("dwT", [2, 128, 62])
    dwb_in = din("dwb", [2, 128, 2])
    lng_in = din("lng", [2, 128, 2])
    lnb_in = din("lnb", [2, 128, 2])
    lbl_in = din("lbl", [2, 128, 4])
    hng_in = din("hng", [2, 128, 4])
    fdw_in = din("fdw", [2, 128, 132])
    R_in = din("R", [2, 128, 2560])
    gfin_in = din("gfin", [128, D])
    y_o = dout("y", [seg, D])
    ys_o = dout("ys", [NSEQ * TS, D])
    convp_o = dout("convp", [2, 128, 60])
    hgrnp_o = dout("hgrnp", [2, 128, 512])
    kp_o = dout("kp", [2, 128, 1024])
    vp_o = dout("vp", [2, 512, 256])
    ffnp_o = dout("ffnp", [2, 128, 88])
    convs_o = dout("convs", [2, NSEQ, 128, 60])
    hgrns_o = dout("hgrns", [2, NSEQ, 128, 512])
    ks_o = dout("ks", [2, NSEQ, 128, 64])
    vs_o = dout("vs", [2, NSEQ, TS, 256])
    ffns_o = dout("ffns", [2, NSEQ, 128, 88])
    x1_d = dscr("x1_d", [NTOK, D])
    hm_d = dscr("hm_d", [NTOK, D])
    ft_d = dscr("ft_d", [NT, 128, 1024], BF16)
    x1s_d = dscr("x1s_d", [NSEQ * TS, D])
    hms_d = dscr("hms_d", [NSEQ * TS, D])
    fts_d = dscr("fts_d", [NSEQ, 128, 1024], BF16)

    def sb(name, cols, dt=F32):
        return stack.enter_context(nc.sbuf_tensor("s_" + name, [128, cols], dt))

    arena = sb("arena", 67584, BF16)
    stg = [sb("stg%d" % i, 1024) for i in range(2)]
    xt = [sb("xt%d" % i, D) for i in range(3)]
    junk = sb("junk", D, BF16)
    xn = sb("xn", D, BF16)
    xnT = sb("xnT", 1024, BF16)
    sm = sb("sm", 64)
    hout = sb("hout", D)
    ident_f = sb("ident_f", 128)
    ident_b = sb("ident_b", 128, BF16)
    ones_f = sb("ones_f", 128)
    o128 = sb("o128", 128)
    o256 = sb("o256", 128)
    maskT = sb("maskT", 128)
    valid_sb = sb("valid_sb", NT)
    gfin = sb("gfin", D)
    epsc = sb("epsc", 2)
    negones = sb("negones", 128)
    gT = sb("gT", 8)
    dwT = sb("dwT", 62)
    dwb = sb("dwb", 2)
    lng = sb("lng", 2)
    lnb = sb("lnb", 2)
    lbl = sb("lbl", 8)
    lb = sb("lb", 4)
    oml = sb("oml", 4)
    hng = sb("hng", 4)
    fdw = sb("fdw", 132)
    U = sb("U", 9216)

    class Carve:
        def __init__(self, arena_from=67584):
            self.a_off = arena_from
            self.u_off = 0

        def _take(self, nbytes):
            nb2 = (nbytes + 3) // 4 * 2
            if self.a_off + nb2 <= 67584:
                a = arena[:, self.a_off:self.a_off + nb2]
                self.a_off += nb2
                return a, BF16
            c32 = nb2 // 2
            a = U[:, self.u_off:self.u_off + c32]
            self.u_off += c32
            assert self.u_off <= 9216, self.u_off
            return a, F32

        def f32(self, cols):
            a, dt = self._take(4 * cols)
            return a if dt == F32 else a.bitcast(F32)

        def bf(self, cols):
            a, dt = self._take(2 * cols)
            a = a if dt == BF16 else a.bitcast(BF16)
            return a[:, 0:cols]

    psum = [stack.enter_context(nc.psum_tensor("ps%d" % i, [128, 512], F32)) for i in range(8)]
    pctr = [0]

    def bank(fixed=None):
        if fixed is not None:
            return psum[fixed], ('ps', fixed)
        b = pctr[0] % 8
        pctr[0] += 1
        return psum[b], ('ps', b)

    P.pool(lambda e: e.memset(ones_f[:, :], 1.0), w=['ones_f'])
    P.pool(lambda e: e.memset(negones[:, :], -1.0), w=['negones'])
    P.pool(lambda e: e.memset(epsc[:, 0:1], EPS), w=['epsc'])
    P.pool(lambda e: e.memset(epsc[:, 1:2], 1.0), w=['epsc'])
    P.pool(lambda e: e.memset(o128[:, :], 1.0 / 128), w=['o128'])
    P.pool(lambda e: e.memset(o256[:, :], 1.0 / 256), w=['o256'])
    P.pool(lambda e: e.memset(ident_f[:, :], 0.0), w=['ident_f'])
    P.pool(lambda e: e.affine_select(out=ident_f[:, :], in_=ident_f[:, :], pattern=[[-1, 128]],
                                     compare_op=ALU.not_equal, fill=1.0, base=0, channel_multiplier=1),
           r=['ident_f'], w=['ident_f'])
    P.pool(lambda e: e.tensor_copy(out=ident_b[:, :], in_=ident_f[:, :]), r=['ident_f'], w=['ident_b'])
    P.pool(lambda e: e.affine_select(out=maskT[:, :], in_=ones_f[:, :], pattern=[[1, 128]],
                                     compare_op=ALU.is_ge, fill=0.0, base=0, channel_multiplier=-1),
           r=['ones_f'], w=['maskT'])
    P.dma(lambda e: e.dma_start(out=valid_sb[:, :], in_=valid[:, :]), w=['valid_sb'], key='c0')
    P.dma(lambda e: e.dma_start(out=gfin[:, :], in_=gfin_in[:, :]), w=['gfin'], key='c0', cont=True)

    wctr = [0]
    stg_all = list(stg) + [U[:, i * 1024:(i + 1) * 1024] for i in range(8)]

    def load_weight(dst_ap, dst_tok, src_rows_ap, ncols, scale_ap=None, scale_tok=None):
        c0 = 0
        while c0 < ncols:
            cw = min(1024, ncols - c0)
            s = wctr[0] % len(stg_all)
            wctr[0] += 1
            st, stok = stg_all[s], 'stg%d' % s
            P.dma(lambda e, st=st, c0=c0, cw=cw: e.dma_start(out=st[:, 0:cw], in_=src_rows_ap[:, c0:c0 + cw]),
                  w=[stok], key=stok)
            d = dst_ap[:, c0:c0 + cw]
            dst_tok = ('Wp', wctr[0])
            eng = ('act', 'dve')[wctr[0] % 2] if scale_ap is not None else ('act', 'dve', 'pool', 'dve')[wctr[0] % 4]
            if scale_ap is None:
                if eng == 'act':
                    P.act(lambda e, d=d, st=st, cw=cw: e.copy(out=d, in_=st[:, 0:cw]), r=[stok], w=[dst_tok])
                else:
                    P.add(eng, lambda e, d=d, st=st, cw=cw: e.tensor_copy(out=d, in_=st[:, 0:cw]), r=[stok], w=[dst_tok])
            else:
                if eng == 'act':
                    P.act(lambda e, d=d, st=st, cw=cw: e.activation(out=d, in_=st[:, 0:cw], func=AF.Copy, scale=scale_ap),
                          r=[stok, scale_tok], w=[dst_tok])
                else:
                    P.add(eng, lambda e, d=d, st=st, cw=cw: e.tensor_scalar(out=d, in0=st[:, 0:cw], scalar1=scale_ap,
                                                                           scalar2=None, op0=ALU.mult),
                          r=[stok, scale_tok], w=[dst_tok])
            c0 += cw

    xn_s = [xn, None]
    xnT_s = [xnT, None]
    xn_tok = ['xn0', 'xn1']

    def pro_load(src_ap, T, xslot, srctok=None):
        x_t, xtok = xt[xslot], 'xt%d' % xslot
        P.dma(lambda e: e.dma_start(out=x_t[:T, :], in_=src_ap), r=([srctok] if srctok else []), w=[xtok], key=xtok)

    def prologue(T, xslot, slot):
        x_t, xtok = xt[xslot], 'xt%d' % xslot
        xn_, xnT_ = xn_s[slot], xnT_s[slot]
        xnt, xTt = xn_tok[slot], 'xnT%d' % slot
        b0 = 16 * slot
        smt = 'smp%d' % slot
        P.act(lambda e: e.activation(out=junk[:T, :], in_=x_t[:T, :], func=AF.Square, accum_out=sm[:T, b0:b0 + 1]),
              r=[xtok], w=['junk', smt])
        P.act(lambda e: e.activation(out=sm[:T, b0 + 1:b0 + 2], in_=sm[:T, b0:b0 + 1], func=AF.Ln, scale=1.0 / D, bias=epsc[:T, 0:1]),
              r=[smt, 'epsc'], w=[smt])
        P.act(lambda e: e.activation(out=sm[:T, b0 + 2:b0 + 3], in_=sm[:T, b0 + 1:b0 + 2], func=AF.Exp, scale=-0.5), r=[smt], w=[smt])
        P.act(lambda e: e.activation(out=xn_[:T, :], in_=x_t[:T, :], func=AF.Copy, scale=sm[:T, b0 + 2:b0 + 3]),
              r=[xtok, smt], w=[xnt])
        for half in range(2):
            ps, ptok = bank()
            for kk in range(4):
                k = half * 4 + kk
                P.pe(lambda e: e.matmul(ps[:, kk * 128:kk * 128 + T], lhsT=xn_[:T, k * 128:(k + 1) * 128],
                                        rhs=ident_b[:T, :T], start=True, stop=True),
                     r=[xnt, 'ident_b'], w=[ptok])
            dst = xnT_[:, half * 512:(half + 1) * 512].rearrange("p (k t) -> p k t", t=128)[:, :, :T]
            src = ps[:, :].rearrange("p (k t) -> p k t", t=128)[:, :, :T]
            if half == 0:
                P.dve(lambda e: e.tensor_copy(out=dst, in_=src), r=[ptok], w=[xTt])
            else:
                P.act(lambda e: e.copy(out=dst, in_=src), r=[ptok], w=[xTt])
        return x_t, xtok, xnT_, xTt

    def proj_feat(wv, wtok, col, T, xT, xTtok, nk=8):
        ps, ptok = bank()
        for k in range(nk):
            P.pe(lambda e: e.matmul(ps[:, :T], lhsT=wv[:, k, col:col + 128], rhs=xT[:, k * 128:k * 128 + T],
                                    start=(k == 0), stop=(k == nk - 1)),
                 r=[wtok, xTtok], w=[ptok])
        return ps, ptok

    def proj_tok(wv, wtok, col, ncol, T, xT, xTtok):
        ps, ptok = bank()
        for k in range(8):
            P.pe(lambda e: e.matmul(ps[:T, :ncol], lhsT=xT[:, k * 128:k * 128 + T], rhs=wv[:, k, col:col + ncol],
                                    start=(k == 0), stop=(k == 7)),
                 r=[wtok, xTtok], w=[ptok])
        return ps, ptok

    def run_tiles(items):
        pros = {}
        n_it = len(items)
        for idx, it in enumerate(items):
            if idx == 0:
                for k_ in range(min(2, n_it)):
                    pro_load(items[k_]['src'], items[k_]['T'], k_ % 3, items[k_]['srctok'])
                pros[0] = prologue(items[0]['T'], 0, 0)
            if idx + 2 < n_it:
                n2 = items[idx + 2]
                pro_load(n2['src'], n2['T'], (idx + 2) % 3, n2['srctok'])
            if idx + 1 < n_it:
                pros[idx + 1] = prologue(items[idx + 1]['T'], (idx + 1) % 3, (idx + 1) % 2)
            if it.get('pre') is not None:
                it['pre']()
            it['call'](pros.pop(idx), idx % 2)

    def w3(off, nk, ncol):
        return arena[:, off:off + nk * ncol].rearrange("p (k c) -> p k c", c=ncol)

    def pass_A1(l, halo, km, halo_out):
        first_tile = (W - halo) // 128
        first_out = (W - halo_out) // 128
        wa = w3(0, 8, 3328)
        P.barrier()
        P.dma(lambda e: e.dma_start(out=gT[:, :], in_=gmixT[l]), w=['gT'], key='c1')
        for src, dst in ((dwT_in, dwT), (dwb_in, dwb), (lng_in, lng), (lnb_in, lnb), (hng_in, hng)):
            P.dma(lambda e, src=src, dst=dst: e.dma_start(out=dst[:, :], in_=src[l]), w=['lconst'], key='c1', cont=True)
        P.dma(lambda e: e.dma_start(out=lbl[:, 0:4], in_=lbl_in[0]), w=['lconst'], key='c1', cont=True)
        P.dma(lambda e: e.dma_start(out=lbl[:, 4:8], in_=lbl_in[1]), w=['lconst'], key='c1', cont=True)
        c = Carve(8 * 3328)
        Rt = c.f32(2560)
        kms = c.f32(512 + halo)
        P.dma(lambda e: e.dma_start(out=Rt, in_=R_in[l]), w=['R'], key='c1', cont=True)
        P.dma(lambda e: e.dma_start(out=kms, in_=km), w=['km'], key='c1', cont=True)
        if l == 0:
            P.dve(lambda e: e.memset(lb[:, :], 0.0), w=['lb'])
        else:
            P.dve(lambda e: e.tensor_tensor(out=lb[:, :], in0=lbl[:, 4:8], in1=lbl[:, 0:4], op=ALU.subtract),
                  r=['lconst'], w=['lb'])
            P.act(lambda e: e.activation(out=lb[:, :], in_=lb[:, :], func=AF.Exp, scale=-1.0), r=['lb'], w=['lb'])
            P.act(lambda e: e.activation(out=lb[:, :], in_=lb[:, :], func=AF.Ln, bias=epsc[:, 1:2], scale=1.0), r=['lb', 'epsc'], w=['lb'])
            P.act(lambda e: e.activation(out=lb[:, :], in_=lb[:, :], func=AF.Exp, scale=-1.0), r=['lb'], w=['lb'])
        P.dve(lambda e: e.tensor_scalar(out=oml[:, :], in0=lb[:, :], scalar1=-1.0, scalar2=1.0, op0=ALU.mult, op1=ALU.add),
              r=['lb'], w=['oml'])
        for k in range(8):
            load_weight(wa[:, k, :], 'W', w_in[l, k * 128:(k + 1) * 128, 0:3328], 3328, gT[:, k:k + 1], 'gT')
        xn_s[1] = c.bf(1024)
        xnT_s[1] = c.bf(1024)
        diag = c.bf(62 * 128)
        for j in range(2):
            for tap in range(31):
                eng = 'pool' if (tap % 2) else 'dve'
                P.add(eng, lambda e, j=j, tap=tap: e.tensor_scalar(
                    out=diag[:, (j * 31 + tap) * 128:(j * 31 + tap + 1) * 128], in0=ident_f[:, :],
                    scalar1=dwT[:, j * 31 + tap:j * 31 + tap + 1], scalar2=None, op0=ALU.mult),
                    r=['ident_f', 'lconst'], w=['diag'])
        u32 = c.f32(256)
        ubf = c.bf(2 * 160)
        ubf3 = ubf.rearrange("p (j n) -> p j n", n=160)
        sg = [c.f32(128), c.f32(128)]
        hsb = c.f32(256)
        hsq = c.f32(256)
        st4 = c.f32(512)
        hn = [c.f32(128), c.f32(128)]
        feat = c.bf(1024)
        qT = c.bf(256)
        kT32 = c.f32(256)
        kTb = [c.bf(2 * 640), c.bf(2 * 640)]
        Vb = [c.bf(5 * 256), c.bf(5 * 256)]
        v32 = c.f32(256)
        sc = [c.f32(640), c.f32(640)]
        Pbf = [c.bf(640), c.bf(640)]
        PT = [c.bf(640), c.bf(640)]
        obf = c.bf(256)
        at_sm = c.f32(16)
        HS = []
        for s_ in range(4):
            d_ = dict(q32=c.f32(128), ff=c.f32(128), lf=c.f32(128), kk32=c.f32(128), Bc=c.f32(128), nB=c.f32(128),
                      E1=c.f32(128), E2=c.f32(128), qh=c.bf(128), kh=c.bf(128), qt=c.bf(128), Kb=c.bf(128),
                      ATm=[c.bf(128), c.bf(128)], KbT=[c.bf(128), c.bf(128)], osq=c.f32(128), sgl=c.f32(128), on=c.f32(128),
                      rstd_h=c.f32(128), hs=c.f32(8), o32=c.f32(128), sig=c.f32(384))
            HS.append(d_)
        vbf = c.bf(1024)
        S32 = c.f32(512)
        Sbf = c.bf(512)
        FT = ['feat%d' % i for i in range(8)]

        def init_states():
            P.pool(lambda e: e.memset(ubf, 0.0), w=['ubf'])
            P.pool(lambda e: e.memset(kTb[0], 0.0), w=['kTb0'])
            P.pool(lambda e: e.memset(Vb[0], 0.0), w=['Vb0'])
            P.pool(lambda e: e.memset(S32, 0.0), w=['S32_%d' % h for h in range(4)])
            P.pool(lambda e: e.memset(Sbf, 0.0), w=['Sbf_%d' % h for h in range(4)])

        def tile(pro, T, feat_dst, fttok_d, cur, kmoff, state_out, shift, warm=False):
            KW = 512 + T
            nxt = 1 - cur
            kb, kbtok = kTb[cur], 'kTb%d' % cur
            vb, vbtok = Vb[cur], 'Vb%d' % cur
            x_t, xtok, xT, xTtok = pro
            so = state_out or {}

            def pf(col):
                return proj_feat(wa, 'W', col, T, xT, xTtok)

            def brA():
                for j in range(2):
                    p1, t1 = pf(j * 128)
                    p2, t2 = pf(256 + j * 128)
                    sgj = sg[j]
                    P.act(lambda e: e.activation(out=sgj[:, :T], in_=p2[:, :T], func=AF.Exp, scale=-1.0), r=[t2], w=['sg%d' % j])
                    P.act(lambda e: e.activation(out=sgj[:, :T], in_=sgj[:, :T], func=AF.Ln, bias=epsc[:, 1:2], scale=1.0), r=['sg%d' % j, 'epsc'], w=['sg%d' % j])
                    P.act(lambda e: e.activation(out=sgj[:, :T], in_=sgj[:, :T], func=AF.Exp, scale=-1.0), r=['sg%d' % j], w=['sg%d' % j])
                    P.dve(lambda e: e.tensor_tensor(out=u32[:, j * 128:j * 128 + T], in0=p1[:, :T], in1=sgj[:, :T], op=ALU.mult),
                          r=[t1, 'sg%d' % j], w=['u32_%d' % j])
                    P.pool(lambda e: e.tensor_copy(out=ubf[:, j * 160 + 30:j * 160 + 30 + T], in_=u32[:, j * 128:j * 128 + T]),
                           r=['u32_%d' % j], w=['ubf'])
                yield
                for j in range(2):
                    ps, ptok = bank()
                    for tap in range(31):
                        P.pe(lambda e: e.matmul(ps[:, :T], lhsT=diag[:, (j * 31 + tap) * 128:(j * 31 + tap + 1) * 128],
                                                rhs=ubf[:, j * 160 + tap:j * 160 + tap + T], start=(tap == 0), stop=(tap == 30)),
                             r=['diag', 'ubf'], w=[ptok])
                    P.act(lambda e: e.activation(out=hsb[:, j * 128:j * 128 + T], in_=ps[:, :T], func=AF.Identity,
                                                 bias=dwb[:, j:j + 1], scale=1.0), r=[ptok, 'lconst'], w=['hsb%d' % j])
                    P.act(lambda e: e.activation(out=hsq[:, j * 128:j * 128 + T], in_=ps[:, :T], func=AF.Square,
                                                 bias=dwb[:, j:j + 1], scale=1.0), r=[ptok, 'lconst'], w=['hsq%d' % j])
                if so.get('conv') is not None:
                    for j in range(2):
                        P.dma(lambda e: e.dma_start(out=so['conv'][:, j * 30:(j + 1) * 30],
                                                    in_=u32[:, j * 128 + T - 30:j * 128 + T]), r=['u32_%d' % j], key='so')
                if shift:
                    P.pool(lambda e: e.tensor_copy(out=ubf3[:, :, 0:30], in_=ubf3[:, :, T:T + 30]), r=['ubf'], w=['ubf'])
                yield
                if warm:
                    return
                pm, pmt = bank()
                pq, pqt = bank()
                for j in range(2):
                    P.pe(lambda e: e.matmul(pm[:, :T], lhsT=o256[:, :], rhs=hsb[:, j * 128:j * 128 + T], start=(j == 0), stop=(j == 1)),
                         r=['o256', 'hsb%d' % j], w=[pmt])
                for j in range(2):
                    P.pe(lambda e: e.matmul(pq[:, :T], lhsT=o256[:, :], rhs=hsq[:, j * 128:j * 128 + T], start=(j == 0), stop=(j == 1)),
                         r=['o256', 'hsq%d' % j], w=[pqt])
                msb, tt1, var, rstd = st4[:, 0:128], st4[:, 128:256], st4[:, 256:384], st4[:, 384:512]
                P.dve(lambda e: e.tensor_copy(out=msb[:, :T], in_=pm[:, :T]), r=[pmt], w=['msb'])
                P.dve(lambda e: e.tensor_tensor(out=tt1[:, :T], in0=msb[:, :T], in1=msb[:, :T], op=ALU.mult), r=['msb'], w=['tt1'])
                P.dve(lambda e: e.tensor_tensor(out=var[:, :T], in0=pq[:, :T], in1=tt1[:, :T], op=ALU.subtract), r=[pqt, 'tt1'], w=['var'])
                P.act(lambda e: e.activation(out=var[:, :T], in_=var[:, :T], func=AF.Ln, scale=1.0, bias=epsc[:, 0:1]),
                      r=['var', 'epsc'], w=['var'])
                P.act(lambda e: e.activation(out=rstd[:, :T], in_=var[:, :T], func=AF.Exp, scale=-0.5), r=['var'], w=['rstd'])
                for j in range(2):
                    hnj = hn[j]
                    P.pool(lambda e: e.tensor_tensor(out=hnj[:, :T], in0=hsb[:, j * 128:j * 128 + T], in1=msb[:, :T], op=ALU.subtract),
                           r=['hsb%d' % j, 'msb'], w=['hn%d' % j])
                    P.pool(lambda e: e.tensor_tensor(out=hnj[:, :T], in0=hnj[:, :T], in1=rstd[:, :T], op=ALU.mult),
                           r=['hn%d' % j, 'rstd'], w=['hn%d' % j])
                    P.dve(lambda e: e.tensor_scalar(out=hnj[:, :T], in0=hnj[:, :T], scalar1=lng[:, j:j + 1], scalar2=lnb[:, j:j + 1],
                                                    op0=ALU.mult, op1=ALU.add), r=['hn%d' % j, 'lconst'], w=['hn%d' % j])
                    sgj = sg[j]
                    P.act(lambda e: e.activation(out=sgj[:, :T], in_=hnj[:, :T], func=AF.Exp, scale=-1.0), r=['hn%d' % j], w=['sg%d' % j])
                    P.act(lambda e: e.activation(out=sgj[:, :T], in_=sgj[:, :T], func=AF.Ln, bias=epsc[:, 1:2], scale=1.0), r=['sg%d' % j, 'epsc'], w=['sg%d' % j])
                    P.act(lambda e: e.activation(out=sgj[:, :T], in_=sgj[:, :T], func=AF.Exp, scale=-1.0), r=['sg%d' % j], w=['sg%d' % j])
                    P.dve(lambda e: e.tensor_tensor(out=feat[:, j * 128:j * 128 + T], in0=hnj[:, :T], in1=sgj[:, :T], op=ALU.mult),
                          r=['hn%d' % j, 'sg%d' % j], w=[FT[j]])
                yield

            nb = (KW + 127) // 128

            def C_common():
                for j in range(2):
                    pqq, tq = pf(2560 + j * 128)
                    P.dve(lambda e: e.tensor_copy(out=qT[:, j * 128:j * 128 + T], in_=pqq[:, :T]), r=[tq], w=['qT%d' % j])
                    pk, tk_ = pf(2816 + j * 128)
                    P.act(lambda e: e.copy(out=kT32[:, j * 128:j * 128 + T], in_=pk[:, :T]), r=[tk_], w=['kT32_%d' % j])
                    P.pool(lambda e: e.tensor_copy(out=kb[:, j * 640 + 512:j * 640 + 512 + T], in_=kT32[:, j * 128:j * 128 + T]),
                           r=['kT32_%d' % j], w=[kbtok])
                pv, tv = proj_tok(wa, 'W', 3072, 256, T, xT, xTtok)
                P.act(lambda e: e.copy(out=v32[:T, :], in_=pv[:T, 0:256]), r=[tv], w=['v32'])
                P.pool(lambda e: e.tensor_copy(out=vb[:T, 4 * 256:5 * 256], in_=v32[:T, :]), r=['v32'], w=[vbtok])
                if state_out is not None:
                    for j in range(2):
                        P.dma(lambda e: e.dma_start(out=so['k'][j], in_=kT32[:, j * 128:j * 128 + T]), r=['kT32_%d' % j], key='so')
                    P.dma(lambda e: e.dma_start(out=so['v'], in_=v32[:T, :]), r=['v32'], key='so')

            def C_head(h):
                j, pb = h // 2, 64 * (h % 2)
                s_ = h % 2
                scs, Ps, PTs = sc[s_], Pbf[s_], PT[s_]
                sct, Pt_, PTt = 'sc%d' % s_, 'Pbf%d' % s_, 'PT%d' % s_
                ps1, t1 = bank()
                ps2, t2 = bank()
                P.pe(lambda e: e.matmul(ps1[:T, 0:512], lhsT=qT[pb:pb + 64, j * 128:j * 128 + T],
                                        rhs=kb[pb:pb + 64, j * 640:j * 640 + 512], start=True, stop=True),
                     r=['qT%d' % j, kbtok], w=[t1])
                P.pe(lambda e: e.matmul(ps2[:T, 0:T], lhsT=qT[pb:pb + 64, j * 128:j * 128 + T],
                                        rhs=kb[pb:pb + 64, j * 640 + 512:j * 640 + 512 + T], start=True, stop=True),
                     r=['qT%d' % j, kbtok], w=[t2])
                P.dve(lambda e: e.scalar_tensor_tensor(out=scs[:T, 0:512], in0=ps1[:T, 0:512], scalar=0.125,
                                                       in1=Rt[:T, h * 640:h * 640 + 512], op0=ALU.mult, op1=ALU.add),
                      r=[t1, 'R'], w=[sct])
                P.dve(lambda e: e.scalar_tensor_tensor(out=scs[:T, 512:512 + T], in0=ps2[:T, 0:T], scalar=0.125,
                                                       in1=Rt[:T, h * 640 + 512:h * 640 + 512 + T], op0=ALU.mult, op1=ALU.add),
                      r=[t2, 'R'], w=[sct])
                yield
                if kmoff is not None:
                    wd = min(KW, 512 + halo - kmoff)
                    P.pool(lambda e: e.tensor_tensor(out=scs[:T, 0:wd], in0=scs[:T, 0:wd], in1=kms[:T, kmoff:kmoff + wd], op=ALU.add),
                           r=[sct, 'km'], w=[sct])
                    yield
                P.dve(lambda e: e.reduce_max(out=at_sm[:T, h:h + 1], in_=scs[:T, 0:KW], axis=AX.X), r=[sct], w=['mx%d' % h])
                P.dve(lambda e: e.tensor_scalar(out=at_sm[:T, 4 + h:5 + h], in0=at_sm[:T, h:h + 1], scalar1=-1.0, scalar2=None, op0=ALU.mult),
                      r=['mx%d' % h], w=['nmx%d' % h])
                yield
                P.act(lambda e: e.activation(out=Ps[:T, 0:KW], in_=scs[:T, 0:KW], func=AF.Exp, bias=at_sm[:T, 4 + h:5 + h], scale=1.0,
                                             accum_out=at_sm[:T, 8 + h:9 + h]), r=[sct, 'nmx%d' % h], w=[Pt_, 'rsum%d' % h])
                yield
                P.dve(lambda e: e.reciprocal(out=at_sm[:T, 12 + h:13 + h], in_=at_sm[:T, 8 + h:9 + h]), r=['rsum%d' % h], w=['rinv%d' % h])
                pt1, tt1_ = bank()
                pt2, tt2_ = bank()
                for jb in range(nb):
                    wb = min(128, KW - 128 * jb)
                    dst = pt1[:wb, jb * 128:jb * 128 + T] if jb < 4 else pt2[:wb, 0:T]
                    P.pe(lambda e: e.matmul(dst, lhsT=Ps[:T, jb * 128:jb * 128 + wb], rhs=ident_b[:T, :T], start=True, stop=True),
                         r=[Pt_, 'ident_b'], w=[tt1_ if jb < 4 else tt2_])
                P.act(lambda e: e.copy(out=PTs[:, 0:512], in_=pt1[:, 0:512]), r=[tt1_], w=[PTt])
                wl = KW - 512
                P.dve(lambda e: e.tensor_copy(out=PTs[:wl, 512:512 + T], in_=pt2[:wl, 0:T]), r=[tt2_], w=[PTt])
                yield
                po, pot = bank()
                for jb in range(nb):
                    wb = min(128, KW - 128 * jb)
                    P.pe(lambda e: e.matmul(po[:T, 0:64], lhsT=PTs[:wb, jb * 128:jb * 128 + T],
                                            rhs=vb[:wb, jb * 256 + h * 64:jb * 256 + (h + 1) * 64],
                                            start=(jb == 0), stop=(jb == nb - 1)),
                         r=[PTt, vbtok], w=[pot])
                P.act(lambda e: e.activation(out=obf[:T, h * 64:(h + 1) * 64], in_=po[:T, 0:64], func=AF.Copy,
                                             scale=at_sm[:T, 12 + h:13 + h]), r=[pot, 'rinv%d' % h], w=['obf%d' % h])
                yield

            def C_final():
                if warm:
                    if shift:
                        P.pool(lambda e: e.tensor_copy(out=kTb[nxt].rearrange("p (j n) -> p j n", n=640)[:, :, 0:512],
                                                       in_=kb.rearrange("p (j n) -> p j n", n=640)[:, :, 128:640]),
                               r=[kbtok], w=['kTb%d' % nxt])
                        P.pool(lambda e: e.tensor_copy(out=Vb[nxt][:, 0:1024], in_=vb[:, 256:1280]), r=[vbtok], w=['Vb%d' % nxt])
                    return
                pc, pct = bank()
                for j in range(2):
                    P.pe(lambda e: e.matmul(pc[:, j * 128:j * 128 + T], lhsT=obf[:T, j * 128:(j + 1) * 128], rhs=ident_b[:T, :T],
                                            start=True, stop=True), r=['obf%d' % (2 * j), 'obf%d' % (2 * j + 1), 'ident_b'], w=[pct])
                P.dve(lambda e: e.tensor_copy(out=feat[:, 768:1024].rearrange("p (j t) -> p j t", t=128)[:, :, :T],
                                              in_=pc[:, 0:256].rearrange("p (j t) -> p j t", t=128)[:, :, :T]), r=[pct], w=[FT[6], FT[7]])
                if shift:
                    P.pool(lambda e: e.tensor_copy(out=kTb[nxt].rearrange("p (j n) -> p j n", n=640)[:, :, 0:512],
                                                   in_=kb.rearrange("p (j n) -> p j n", n=640)[:, :, 128:640]),
                           r=[kbtok], w=['kTb%d' % nxt])
                    P.pool(lambda e: e.tensor_copy(out=Vb[nxt][:, 0:1024], in_=vb[:, 256:1280]), r=[vbtok], w=['Vb%d' % nxt])

            chunks = [(0, 64), (64, 64)] if T == 128 else [(0, T)]

            def B_common():
                for ci, (c0, L) in enumerate(chunks):
                    pvv, tvv = bank()
                    for k in range(8):
                        P.pe(lambda e: e.matmul(pvv[:L, 0:512], lhsT=xT[:, k * 128 + c0:k * 128 + c0 + L],
                                                rhs=wa[:, k, 1536:2048], start=(k == 0), stop=(k == 7)),
                             r=['W', xTtok], w=[tvv])
                    if ci == 0:
                        P.act(lambda e: e.copy(out=vbf[:L, ci * 512:(ci + 1) * 512], in_=pvv[:L, 0:512]), r=[tvv], w=['vbf%d' % ci])
                    else:
                        P.dve(lambda e: e.tensor_copy(out=vbf[:L, ci * 512:(ci + 1) * 512], in_=pvv[:L, 0:512]), r=[tvv], w=['vbf%d' % ci])

            def B_head(h):
                s_ = h
                B_ = HS[s_]
                q32, ff, lf, kk32, Bc, nB, E1, E2 = (B_[n_] for n_ in ('q32', 'ff', 'lf', 'kk32', 'Bc', 'nB', 'E1', 'E2'))
                qh, kh, qt, Kb, osq, sgl, on, rstd_h, hs, o32 = (B_[n_] for n_ in ('qh', 'kh', 'qt', 'Kb', 'osq', 'sgl', 'on', 'rstd_h', 'hs', 'o32'))

                def tk(n_):
                    return '%s_%d' % (n_, s_)
                ps3, t3 = bank()
                for gi, col in enumerate((512 + h * 128, 1024 + h * 128, 2048 + h * 128)):
                    for k in range(8):
                        P.pe(lambda e: e.matmul(ps3[:, gi * 128:gi * 128 + T], lhsT=wa[:, k, col:col + 128], rhs=xT[:, k * 128:k * 128 + T],
                                                start=(k == 0), stop=(k == 7)), r=['W', xTtok], w=[t3])
                sig = B_['sig']
                sig3 = sig.rearrange("p (g t) -> p g t", t=128)[:, :, :T]
                ps33 = ps3[:, 0:384].rearrange("p (g t) -> p g t", t=128)[:, :, :T]
                P.act(lambda e: e.activation(out=sig3, in_=ps33, func=AF.Exp, scale=-1.0), r=[t3], w=[tk('sig')])
                P.dve(lambda e: e.tensor_copy(out=q32[:, :T], in_=ps3[:, 0:T]), r=[t3], w=[tk('q32')])
                P.dve(lambda e: e.tensor_copy(out=sgl[:, :T], in_=ps3[:, 256:256 + T]), r=[t3], w=[tk('sgl')])
                yield
                P.act(lambda e: e.activation(out=sig3, in_=sig3, func=AF.Ln, bias=epsc[:, 1:2], scale=1.0), r=[tk('sig'), 'epsc'], w=[tk('sig')])
                yield
                P.act(lambda e: e.activation(out=sig3, in_=sig3, func=AF.Exp, scale=-1.0), r=[tk('sig')], w=[tk('sig')])
                yield
                P.dve(lambda e: e.tensor_tensor(out=q32[:, :T], in0=q32[:, :T], in1=sig[:, 0:T], op=ALU.mult),
                      r=[tk('q32'), tk('sig')], w=[tk('q32')])
                P.dve(lambda e: e.tensor_scalar(out=ff[:, :T], in0=sig[:, 128:128 + T], scalar1=oml[:, h:h + 1], scalar2=lb[:, h:h + 1],
                                                op0=ALU.mult, op1=ALU.add), r=[tk('sig'), 'oml', 'lb'], w=[tk('ff')])
                P.dve(lambda e: e.tensor_tensor(out=sgl[:, :T], in0=sgl[:, :T], in1=sig[:, 256:256 + T], op=ALU.mult),
                      r=[tk('sgl'), tk('sig')], w=[tk('sgl')])
                yield
                P.act(lambda e: e.activation(out=lf[:, :T], in_=ff[:, :T], func=AF.Ln), r=[tk('ff')], w=[tk('lf')])
                P.dve(lambda e: e.tensor_scalar(out=kk32[:, :T], in0=ff[:, :T], scalar1=-1.0, scalar2=1.0, op0=ALU.mult, op1=ALU.add),
                      r=[tk('ff')], w=[tk('kk32')])
                yield
                P.dve(lambda e: e.tensor_tensor_scan(out=Bc[:, :T], data0=ones_f[:, :T], data1=lf[:, :T], initial=0.0,
                                                     op0=ALU.mult, op1=ALU.add), r=['ones_f', tk('lf')], w=[tk('Bc')])
                yield
                P.pool(lambda e: e.tensor_tensor(out=nB[:, :T], in0=Bc[:, :T], in1=negones[:, :T], op=ALU.mult),
                       r=[tk('Bc'), 'negones'], w=[tk('nB')])
                yield
                for ci, (c0, L) in enumerate(chunks):
                    r_ = c0 + L // 2 - 1
                    en = c0 + L - 1
                    sl = slice(c0, c0 + L)
                    hc = hs[:, 4 * ci:4 * ci + 4]
                    hct = tk('hs%d' % ci)
                    ATm, KbT = B_['ATm'][ci], B_['KbT'][ci]
                    att, kbt = tk('ATm%d' % ci), tk('KbT%d' % ci)
                    P.act(lambda e: e.activation(out=E1[:, sl], in_=Bc[:, sl], func=AF.Exp, bias=nB[:, r_:r_ + 1], scale=1.0),
                          r=[tk('Bc'), tk('nB'), tk('q32')], w=[tk('E1')])
                    P.act(lambda e: e.activation(out=E2[:, sl], in_=Bc[:, sl], func=AF.Exp, bias=Bc[:, r_:r_ + 1], scale=-1.0),
                          r=[tk('Bc')], w=[tk('E2')])
                    if c0 == 0:
                        P.act(lambda e: e.activation(out=hc[:, 0:1], in_=Bc[:, r_:r_ + 1], func=AF.Exp), r=[tk('Bc')], w=[hct])
                        P.act(lambda e: e.activation(out=hc[:, 2:3], in_=Bc[:, en:en + 1], func=AF.Exp), r=[tk('Bc')], w=[hct])
                    else:
                        P.act(lambda e: e.activation(out=hc[:, 0:1], in_=Bc[:, r_:r_ + 1], func=AF.Exp,
                                                     bias=nB[:, c0 - 1:c0], scale=1.0), r=[tk('Bc'), tk('nB')], w=[hct])
                        P.act(lambda e: e.activation(out=hc[:, 2:3], in_=Bc[:, en:en + 1], func=AF.Exp,
                                                     bias=nB[:, c0 - 1:c0], scale=1.0), r=[tk('Bc'), tk('nB')], w=[hct])
                    P.act(lambda e: e.activation(out=hc[:, 1:2], in_=Bc[:, en:en + 1], func=AF.Exp,
                                                 bias=nB[:, r_:r_ + 1], scale=1.0), r=[tk('Bc'), tk('nB')], w=[hct])
                    yield
                    P.dve(lambda e: e.tensor_tensor(out=qh[:, sl], in0=q32[:, sl], in1=E1[:, sl], op=ALU.mult), r=[tk('q32'), tk('E1')], w=[tk('qh')])
                    P.pool(lambda e: e.tensor_tensor(out=kh[:, sl], in0=kk32[:, sl], in1=E2[:, sl], op=ALU.mult), r=[tk('kk32'), tk('E2')], w=[tk('kh')])
                    P.dve(lambda e: e.scalar_tensor_tensor(out=qt[:, sl], in0=E1[:, sl], scalar=hc[:, 0:1], in1=q32[:, sl],
                                                           op0=ALU.mult, op1=ALU.mult), r=[tk('E1'), hct, tk('q32')], w=[tk('qt')])
                    P.dve(lambda e: e.scalar_tensor_tensor(out=Kb[:, sl], in0=E2[:, sl], scalar=hc[:, 1:2], in1=kk32[:, sl],
                                                           op0=ALU.mult, op1=ALU.mult), r=[tk('E2'), hct, tk('kk32')], w=[tk('Kb')])
                    yield
                    pa, pat = bank()
                    P.pe(lambda e: e.matmul(pa[:L, :L], lhsT=kh[:, sl], rhs=qh[:, sl], start=True, stop=True),
                         r=[tk('kh'), tk('qh')], w=[pat])
                    pkt, pktt = bank()
                    P.pe(lambda e: e.matmul(pkt[:L, 0:128], lhsT=Kb[:, sl], rhs=ident_b[:, :], start=True, stop=True),
                         r=[tk('Kb'), 'ident_b'], w=[pktt])
                    P.dve(lambda e: e.tensor_tensor(out=ATm[:L, :L], in0=pa[:L, :L], in1=maskT[:L, :L], op=ALU.mult),
                          r=[pat, 'maskT'], w=[att])
                    P.dve(lambda e: e.tensor_copy(out=KbT[:L, :], in_=pkt[:L, 0:128]), r=[pktt], w=[kbt])
                    yield
                    poh, poht = bank()
                    P.pe(lambda e: e.matmul(poh[:, 0:L], lhsT=vbf[:L, ci * 512 + h * 128:ci * 512 + (h + 1) * 128],
                                            rhs=ATm[:L, :L], start=True, stop=False),
                         r=['vbf%d' % ci, att], w=[poht])
                    P.pe(lambda e: e.matmul(poh[:, 0:L], lhsT=Sbf[:, h * 128:(h + 1) * 128], rhs=qt[:, sl], start=False, stop=True),
                         r=['Sbf_%d' % h, tk('qt')], w=[poht])
                    pds, pdst = bank()
                    P.pe(lambda e: e.matmul(pds[:, 0:128], lhsT=KbT[:L, :],
                                            rhs=vbf[:L, ci * 512 + h * 128:ci * 512 + (h + 1) * 128], start=True, stop=True),
                         r=[kbt, 'vbf%d' % ci], w=[pdst])
                    P.dve(lambda e: e.tensor_copy(out=o32[:, sl], in_=poh[:, 0:L]), r=[poht], w=[tk('o32')])
                    P.dve(lambda e: e.scalar_tensor_tensor(out=S32[:, h * 128:(h + 1) * 128], in0=S32[:, h * 128:(h + 1) * 128],
                                                           scalar=hc[:, 2:3], in1=pds[:, 0:128], op0=ALU.mult, op1=ALU.add),
                          r=['S32_%d' % h, hct, pdst], w=['S32_%d' % h])
                    yield
                    P.pool(lambda e: e.tensor_copy(out=Sbf[:, h * 128:(h + 1) * 128], in_=S32[:, h * 128:(h + 1) * 128]),
                           r=['S32_%d' % h], w=['Sbf_%d' % h])
                    yield
                if warm:
                    return
                P.pool(lambda e: e.tensor_tensor(out=osq[:, :T], in0=o32[:, :T], in1=o32[:, :T], op=ALU.mult), r=[tk('o32')], w=[tk('osq')])
                yield
                pms, pmst = bank()
                P.pe(lambda e: e.matmul(pms[:, :T], lhsT=o128[:, :], rhs=osq[:, :T], start=True, stop=True), r=['o128', tk('osq')], w=[pmst])
                P.act(lambda e: e.activation(out=on[:, :T], in_=pms[:, :T], func=AF.Ln, scale=1.0, bias=epsc[:, 0:1]),
                      r=[pmst, 'epsc'], w=[tk('on')])
                yield
                P.act(lambda e: e.activation(out=rstd_h[:, :T], in_=on[:, :T], func=AF.Exp, scale=-0.5), r=[tk('on')], w=[tk('rstd_h')])
                yield
                P.dve(lambda e: e.tensor_tensor(out=on[:, :T], in0=o32[:, :T], in1=rstd_h[:, :T], op=ALU.mult),
                      r=[tk('o32'), tk('rstd_h')], w=[tk('on')])
                yield
                P.pool(lambda e: e.tensor_tensor(out=feat[:, 256 + h * 128:256 + h * 128 + T], in0=on[:, :T], in1=sgl[:, :T], op=ALU.mult),
                       r=[tk('on'), tk('sgl')], w=[FT[2 + h]])
                yield

            C_common()
            B_common()
            def lane(*gs):
                for g_ in gs:
                    yield from g_
            gens = [brA()] + ([] if warm else [lane(C_head(0), C_head(2)), lane(C_head(1), C_head(3))]) + [B_head(h) for h in range(4)]
            while gens:
                for g in list(gens):
                    try:
                        next(g)
                    except StopIteration:
                        gens.remove(g)
            C_final()
            if so.get('hgrn') is not None:
                P.dma(lambda e: e.dma_start(out=so['hgrn'], in_=S32), r=['S32_%d' % h for h in range(4)], key='so')
            if not warm:
                P.dma(lambda e: e.dma_start(out=feat_dst, in_=feat), r=FT, w=[fttok_d], key='fo')

        P.barrier()
        init_states()
        items = []
        cur = 0
        for i in range(first_tile, NT):
            src = (xp if l == 0 else x1_d)[i * 128:(i + 1) * 128, :]
            kmoff = (i - first_tile) * 128
            if kmoff >= 512 + halo:
                kmoff = None
            so = None
            if i >= NT - 4:
                q = i - (NT - 4)
                so = dict(k=[kp_o[l][:, j * 512 + q * 128:j * 512 + (q + 1) * 128] for j in range(2)],
                          v=vp_o[l][q * 128:(q + 1) * 128, :])
                if i == NT - 1:
                    so['conv'] = convp_o[l]
                    so['hgrn'] = hgrnp_o[l]
            items.append(dict(src=src, srctok=(('x1', 'p', i) if l == 1 else None), T=128,
                              call=(lambda pro, slot, i=i, cur=cur, kmoff=kmoff, so=so:
                                    tile(pro, 128, ft_d[i], ('ft', 'p', i), cur, kmoff, so, True, warm=(i < first_out)))))
            cur = 1 - cur
        S_all = ['S32_%d' % h for h in range(4)]
        Sb_all = ['Sbf_%d' % h for h in range(4)]

        def load_states(n):
            P.dma(lambda e: e.dma_start(out=stg[0][:, 0:60], in_=sconvT[l, n]), w=['stg0'], key='si')
            P.pool(lambda e: e.tensor_copy(out=ubf3[:, :, 0:30], in_=stg[0][:, 0:60].rearrange("p (j n) -> p j n", n=30)),
                   r=['stg0'], w=['ubf'])
            P.dma(lambda e: e.dma_start(out=S32, in_=shgrn[l, n]), w=S_all, key='si')
            P.pool(lambda e: e.tensor_copy(out=Sbf, in_=S32), r=S_all, w=Sb_all)
            P.dma(lambda e: e.dma_start(out=stg[1][:, 0:1024], in_=skT[l, n]), w=['stg1'], key='si')
            P.pool(lambda e: e.tensor_copy(out=kTb[0].rearrange("p (j n) -> p j n", n=640)[:, :, 0:512],
                                           in_=stg[1][:, 0:1024].rearrange("p (j n) -> p j n", n=512)), r=['stg1'], w=['kTb0'])
            P.dma(lambda e: e.dma_start(out=stg[0][:, 0:1024], in_=sv[l, n]), w=['stg0'], key='si')
            P.pool(lambda e: e.tensor_copy(out=Vb[0][:, 0:1024], in_=stg[0][:, 0:1024]), r=['stg0'], w=['Vb0'])

        for n in range(NSEQ):
            src = (xs if l == 0 else x1s_d)[n * TS:(n + 1) * TS, :]
            so = dict(conv=convs_o[l, n], hgrn=hgrns_o[l, n], k=[ks_o[l, n][:, j * 32:(j + 1) * 32] for j in range(2)], v=vs_o[l, n])
            items.append(dict(src=src, srctok=(('x1', 's', n) if l == 1 else None), T=TS,
                              pre=(lambda n=n: load_states(n)),
                              call=(lambda pro, slot, n=n, so=so: tile(pro, TS, fts_d[n], ('ft', 's', n), 0, None, so, False))))
        run_tiles(items)

    def pass_A2(l, halo):
        first_tile = (W - halo) // 128
        wg = w3(0, 8, 3072)
        wco = w3(24576, 2, 1024)
        whg = w3(24576 + 2048, 4, 1024)
        wat = w3(24576 + 2048 + 4096, 2, 1024)
        wmx = w3(24576 + 2048 + 4096 + 2048, 8, 1024)
        P.barrier()
        P.dma(lambda e: e.dma_start(out=gT[:, :], in_=gmixT[l]), w=['gT'], key='c1')
        P.dma(lambda e: e.dma_start(out=hng[:, :], in_=hng_in[l]), w=['lconst'], key='c1', cont=True)
        for k in range(8):
            load_weight(wg[:, k, :], 'W', w_in[l, k * 128:(k + 1) * 128, 3328:6400], 3072, gT[:, k:k + 1], 'gT')
        for j in range(2):
            load_weight(wco[:, j, :], 'W', w_conv_out[l, j * 128:(j + 1) * 128, :], 1024)
        for h in range(4):
            load_weight(whg[:, h, :], 'W', w_hgrn_out[l, h * 128:(h + 1) * 128, :], 1024, hng[:, h:h + 1], 'lconst')
        for j in range(2):
            load_weight(wat[:, j, :], 'W', w_attn_out[l, j * 128:(j + 1) * 128, :], 1024)
        for k in range(8):
            load_weight(wmx[:, k, :], 'W', w_mix_out[l, k * 128:(k + 1) * 128, :], 1024)
        c = Carve(24576 + 2048 + 4096 + 2048 + 8192)
        xn_s[1] = c.bf(1024)
        xnT_s[1] = c.bf(1024)
        ftile = [c.bf(1024), c.bf(1024)]
        gsb = [c.f32(512) for _ in range(3)]
        tmp = [c.f32(512) for _ in range(2)]
        m32 = [c.f32(1024), c.f32(1024)]
        mbf = [c.bf(1024), c.bf(1024)]
        mT = [c.bf(1024), c.bf(1024)]
        hout_s = [hout, c.f32(1024)]
        branches = ((0, wco, (0, 1)), (1024, whg, (2, 3, 4, 5)), (2048, wat, (6, 7)))

        def tile(pro, fsrc, fsrctok, dst_ap, dsttok, T, slot, vcol):
            x_t, xtok, xT, xTtok = pro
            ft, fttok = ftile[slot], 'ft%d' % slot
            m32_, mbf_, mT_, ho = m32[slot], mbf[slot], mT[slot], hout_s[slot]
            m32t, mbft, mTt, hot = 'm32_%d' % slot, 'mbf%d' % slot, 'mT%d' % slot, 'hout%d' % slot
            P.dma(lambda e: e.dma_start(out=ft, in_=fsrc), r=[fsrctok], w=[fttok], key=fttok)
            for cb in range(2):
                for bi, (goff, wv, chunks) in enumerate(branches):
                    pg, tg = proj_tok(wg, 'W', goff + cb * 512, 512, T, xT, xTtok)
                    g_, gt = gsb[bi], 'gsb%d' % bi
                    P.act(lambda e: e.activation(out=g_[:T, :], in_=pg[:T, :], func=AF.Sigmoid), r=[tg], w=[gt])
                    py, ty = bank()
                    n = len(chunks)
                    for ci, ch in enumerate(chunks):
                        P.pe(lambda e: e.matmul(py[:T, :], lhsT=ft[:, ch * 128:ch * 128 + T], rhs=wv[:, ci, cb * 512:(cb + 1) * 512],
                                                start=(ci == 0), stop=(ci == n - 1)), r=[fttok, 'W'], w=[ty])
                    mc = m32_[:T, cb * 512:(cb + 1) * 512]
                    mct = '%s_%d' % (m32t, cb)
                    if bi == 0:
                        P.dve(lambda e: e.tensor_tensor(out=mc, in0=py[:T, :], in1=g_[:T, :], op=ALU.mult), r=[ty, gt], w=[mct])
                    else:
                        t_, tt = tmp[bi - 1], 'tmp%d' % (bi - 1)
                        P.dve(lambda e: e.tensor_tensor(out=t_[:T, :], in0=py[:T, :], in1=g_[:T, :], op=ALU.mult), r=[ty, gt], w=[tt])
                        if bi == 1:
                            P.pool(lambda e: e.tensor_tensor(out=mc, in0=mc, in1=t_[:T, :], op=ALU.add), r=[mct, tt], w=[mct])
                        else:
                            P.pool(lambda e: e.tensor_tensor(out=mbf_[:T, cb * 512:(cb + 1) * 512], in0=mc, in1=t_[:T, :], op=ALU.add),
                                   r=[mct, tt], w=['%s_%d' % (mbft, cb)])
            for half in range(2):
                ps, ptok = bank()
                for kk in range(4):
                    k = half * 4 + kk
                    P.pe(lambda e: e.matmul(ps[:, kk * 128:kk * 128 + T], lhsT=mbf_[:T, k * 128:(k + 1) * 128],
                                            rhs=ident_b[:T, :T], start=True, stop=True), r=['%s_%d' % (mbft, half), 'ident_b'], w=[ptok])
                dst = mT_[:, half * 512:(half + 1) * 512].rearrange("p (k t) -> p k t", t=128)[:, :, :T]
                srcp = ps[:, :].rearrange("p (k t) -> p k t", t=128)[:, :, :T]
                if half == 0:
                    P.dve(lambda e: e.tensor_copy(out=dst, in_=srcp), r=[ptok], w=[mTt])
                else:
                    P.act(lambda e: e.copy(out=dst, in_=srcp), r=[ptok], w=[mTt])
            for cb in range(2):
                ps, ptok = bank()
                for k in range(8):
                    P.pe(lambda e: e.matmul(ps[:T, :], lhsT=mT_[:, k * 128:k * 128 + T], rhs=wmx[:, k, cb * 512:(cb + 1) * 512],
                                            start=(k == 0), stop=(k == 7)), r=[mTt, 'W'], w=[ptok])
                P.dve(lambda e: e.tensor_tensor(out=ho[:T, cb * 512:(cb + 1) * 512], in0=ps[:T, :], in1=x_t[:T, cb * 512:(cb + 1) * 512],
                                                op=ALU.add), r=[ptok, xtok], w=[hot])
            if vcol is not None:
                P.dve(lambda e: e.tensor_scalar(out=ho[:T, :], in0=ho[:T, :], scalar1=valid_sb[:T, vcol:vcol + 1], scalar2=None, op0=ALU.mult),
                       r=[hot, 'valid_sb'], w=[hot])
            P.dma(lambda e: e.dma_start(out=dst_ap, in_=ho[:T, :]), r=[hot], w=[dsttok], key='ho%d' % slot)

        P.barrier()
        items = []
        for i in range(first_tile, NT):
            src = (xp if l == 0 else x1_d)[i * 128:(i + 1) * 128, :]
            items.append(dict(src=src, srctok=(('x1', 'p', i) if l == 1 else None), T=128,
                              call=(lambda pro, slot, i=i: tile(pro, ft_d[i], ('ft', 'p', i), hm_d[i * 128:(i + 1) * 128, :], ('hm', 'p', i),
                                                                128, slot, i if i < W // 128 else None))))
        for n in range(NSEQ):
            src = (xs if l == 0 else x1s_d)[n * TS:(n + 1) * TS, :]
            items.append(dict(src=src, srctok=(('x1', 's', n) if l == 1 else None), T=TS,
                              call=(lambda pro, slot, n=n: tile(pro, fts_d[n], ('ft', 's', n), hms_d[n * TS:(n + 1) * TS, :], ('hm', 's', n),
                                                                TS, slot, None))))
        run_tiles(items)

    def pass_B(l, halo):
        first_tile = (W - halo) // 128
        wup = w3(0, 8, 5632)
        wdn = w3(45056, 22, 1024)
        P.barrier()
        P.dma(lambda e: e.dma_start(out=gT[:, :], in_=gffnT[l]), w=['gT'], key='c1')
        P.dma(lambda e: e.dma_start(out=fdw[:, :], in_=fdw_in[l]), w=['lconst'], key='c1', cont=True)
        for k in range(8):
            load_weight(wup[:, k, :], 'W', w_ffn_up[l, k * 128:(k + 1) * 128, :], 5632, gT[:, k:k + 1], 'gT')
        for cch in range(22):
            load_weight(wdn[:, cch, :], 'W', w_ffn_down[l, cch * 128:(cch + 1) * 128, :], 1024)
        c = Carve()
        ub = c.f32(44 * 130)
        ub3 = ub.rearrange("p (c n) -> p c n", n=130)
        t1b = [[c.f32(128), c.f32(128)] for _ in range(2)]
        ucb = [[c.f32(128), c.f32(128)] for _ in range(2)]
        sa = [c.f32(128), c.f32(128)]
        gTt = c.bf(22 * 128)
        xn_s[1] = xn_s[0]
        xn_tok[1] = 'xn0'
        xnT_s[1] = c.bf(1024)
        UB = [('ub', cc) for cc in range(44)]

        def tile(pro, dst_ap, dsttok, T, final_out, state_out):
            x_t, xtok, xT, xTtok = pro

            def S1(c2):
                st = c2 % 2
                for half in range(2):
                    cc = c2 + 22 * half
                    ps, ptok = proj_feat(wup, 'W', cc * 128, T, xT, xTtok)
                    t1, t1t = t1b[st][half], 't1_%d_%d' % (st, half)
                    P.act(lambda e: e.copy(out=ub[:, cc * 130 + 2:cc * 130 + 2 + T], in_=ps[:, :T]), r=[ptok], w=[('ub', cc)])
                    P.pool(lambda e: e.tensor_tensor(out=t1[:, :T], in0=ub[:, cc * 130:cc * 130 + T],
                                                     in1=fdw[:, cc * 3:cc * 3 + 1].to_broadcast([128, T]), op=ALU.mult),
                           r=[('ub', cc), 'lconst'], w=[t1t])

            def S2(c2):
                st = c2 % 2
                for half in range(2):
                    cc = c2 + 22 * half
                    t1, t1t = t1b[st][half], 't1_%d_%d' % (st, half)
                    uc, uct = ucb[st][half], 'uc_%d_%d' % (st, half)
                    P.dve(lambda e: e.scalar_tensor_tensor(out=t1[:, :T], in0=ub[:, cc * 130 + 1:cc * 130 + 1 + T],
                                                           scalar=fdw[:, cc * 3 + 1:cc * 3 + 2], in1=t1[:, :T], op0=ALU.mult, op1=ALU.add),
                          r=[('ub', cc), 'lconst', t1t], w=[t1t])
                    P.dve(lambda e: e.scalar_tensor_tensor(out=uc[:, :T], in0=ub[:, cc * 130 + 2:cc * 130 + 2 + T],
                                                           scalar=fdw[:, cc * 3 + 2:cc * 3 + 3], in1=t1[:, :T], op0=ALU.mult, op1=ALU.add),
                          r=[('ub', cc), 'lconst', t1t], w=[uct])

            def S3(c2):
                st = c2 % 2
                sa_, sat = sa[st], 'sa%d' % st
                P.act(lambda e: e.activation(out=sa_[:, :T], in_=ucb[st][0][:, :T], func=AF.Silu), r=['uc_%d_0' % st], w=[sat])
                P.pool(lambda e: e.tensor_tensor(out=gTt[:, c2 * 128:c2 * 128 + T], in0=sa_[:, :T], in1=ucb[st][1][:, :T], op=ALU.mult),
                       r=[sat, 'uc_%d_1' % st], w=[('gTt', c2)])

            for step in range(22 + 2):
                if step < 22:
                    S1(step)
                if 1 <= step < 23:
                    S2(step - 1)
                if step >= 2:
                    S3(step - 2)
            if state_out is not None:
                P.dma(lambda e: e.dma_start(out=state_out.rearrange("p (c n) -> p c n", n=2), in_=ub3[:, :, T:T + 2]), r=UB, key='so')
            else:
                P.pool(lambda e: e.tensor_copy(out=ub3[:, :, 0:2], in_=ub3[:, :, T:T + 2]), r=UB, w=UB)
            for cb in range(2):
                ps, ptok = bank()
                for c2 in range(22):
                    P.pe(lambda e: e.matmul(ps[:T, :], lhsT=gTt[:, c2 * 128:c2 * 128 + T], rhs=wdn[:, c2, cb * 512:(cb + 1) * 512],
                                            start=(c2 == 0), stop=(c2 == 21)), r=[('gTt', c2), 'W'], w=[ptok])
                P.dve(lambda e: e.tensor_tensor(out=hout[:T, cb * 512:(cb + 1) * 512], in0=ps[:T, :], in1=x_t[:T, cb * 512:(cb + 1) * 512],
                                                op=ALU.add), r=[ptok, xtok], w=['hout'])
            if final_out is None:
                P.dma(lambda e: e.dma_start(out=dst_ap, in_=hout[:T, :]), r=['hout'], w=[dsttok], key='ho')
            elif final_out is not False:
                P.act(lambda e: e.activation(out=junk[:T, :], in_=hout[:T, :], func=AF.Square, accum_out=sm[:T, 8:9]),
                      r=['hout'], w=['junk', 'sm8'])
                P.act(lambda e: e.activation(out=sm[:T, 9:10], in_=sm[:T, 8:9], func=AF.Ln, scale=1.0 / D, bias=epsc[:T, 0:1]),
                      r=['sm8', 'epsc'], w=['sm9'])
                P.act(lambda e: e.activation(out=sm[:T, 10:11], in_=sm[:T, 9:10], func=AF.Exp, scale=-0.5), r=['sm9'], w=['sm10'])
                P.dve(lambda e: e.scalar_tensor_tensor(out=hout[:T, :], in0=hout[:T, :], scalar=sm[:T, 10:11], in1=gfin[:T, :],
                                                       op0=ALU.mult, op1=ALU.mult), r=['hout', 'sm10', 'gfin'], w=['hout'])
                P.dma(lambda e: e.dma_start(out=final_out, in_=hout[:T, :]), r=['hout'], key='ho')

        P.barrier()
        P.pool(lambda e: e.memset(ub, 0.0), w=UB)
        items = []
        for i in range(first_tile, NT):
            src = hm_d[i * 128:(i + 1) * 128, :]
            so = ffnp_o[l] if i == NT - 1 else None
            if l == 0:
                call = (lambda pro, slot, i=i, so=so: tile(pro, x1_d[i * 128:(i + 1) * 128, :], ('x1', 'p', i), 128, None, so))
            else:
                fo = y_o[(i - W // 128) * 128:(i - W // 128 + 1) * 128, :] if i >= W // 128 else False
                call = (lambda pro, slot, fo=fo, so=so: tile(pro, None, None, 128, fo, so))
            items.append(dict(src=src, srctok=('hm', 'p', i), T=128, call=call))

        def load_ffn_state(n):
            P.dma(lambda e: e.dma_start(out=stg[0][:, 0:88], in_=sffnT[l, n]), w=['stg0'], key='si')
            P.pool(lambda e: e.tensor_copy(out=ub3[:, :, 0:2], in_=stg[0][:, 0:88].rearrange("p (c n) -> p c n", n=2)), r=['stg0'], w=UB)

        for n in range(NSEQ):
            src = hms_d[n * TS:(n + 1) * TS, :]
            if l == 0:
                call = (lambda pro, slot, n=n: tile(pro, x1s_d[n * TS:(n + 1) * TS, :], ('x1', 's', n), TS, None, ffns_o[l, n]))
            else:
                call = (lambda pro, slot, n=n: tile(pro, None, None, TS, ys_o[n * TS:(n + 1) * TS, :], ffns_o[l, n]))
            items.append(dict(src=src, srctok=('hm', 's', n), T=TS, pre=(lambda n=n: load_ffn_state(n)), call=call))
        run_tiles(items)

    import os as _os
    npass = int(_os.environ.get("MK_NPASS", "6"))
    plist = [lambda: pass_A1(0, hA0, km_in[0], hB0), lambda: pass_A2(0, hB0), lambda: pass_B(0, hB0),
             lambda: pass_A1(1, hA1, km_in[1], hB1), lambda: pass_A2(1, hB1), lambda: pass_B(1, hB1)]
    for pf in plist[:npass]:
        pf()
    P.emit(nc, stack)
    stack.close()
    return nc


def _rel_table(rel_bias):
    p = np.arange(128)[:, None]
    x = np.arange(640)[None, :]
    idx = np.clip(512 + p - x, -128, 128) + 128
    R = rel_bias[:, :, idx]
    R = np.ascontiguousarray(R.transpose(0, 2, 1, 3)).copy()
    R[:, 0:64, :, 576:640] = NEG
    R[:, 64:128, :, 0:64] = NEG
    return R.reshape(2, 128, 2560).astype(np.float32)


def kernel(x_prompt, x_sample, state_conv, state_hgrn, cache_attn_k, cache_attn_v, state_ffn,
           w_in, conv_dw_w, conv_dw_b, conv_ln_g, conv_ln_b, w_conv_out,
           hgrn_lb_logits, hgrn_norm_g, w_hgrn_out, attn_rel_bias, w_attn_out,
           w_mix_out, g_mix, w_ffn_up, ffn_dw_w, w_ffn_down, g_ffn, g_final,
           _seg=SEG, _halos=(H_A0, H_B0, H_A1, H_B1)):
    f = lambda a: np.ascontiguousarray(np.asarray(a, dtype=np.float32))
    x_prompt, x_sample = f(x_prompt), f(x_sample)
    seg, halos = _seg, _halos
    W = halos[0]
    NTOK = W + seg
    NT = NTOK // 128
    nb, seq = x_prompt.shape[0], x_prompt.shape[1]
    cps = seq // seg
    assert nb * cps == N_CORES
    nc = build_program(seg, halos)

    def fm(v, nch):
        v = f(v)
        return np.ascontiguousarray(v.reshape(v.shape[:-1] + (nch, 128)).swapaxes(-1, -2))

    shared = dict(
        w_in=f(w_in), w_conv_out=f(w_conv_out), w_hgrn_out=f(w_hgrn_out), w_attn_out=f(w_attn_out),
        w_mix_out=f(w_mix_out), w_ffn_up=f(w_ffn_up), w_ffn_down=f(w_ffn_down),
        gmixT=fm(g_mix, 8), gffnT=fm(g_ffn, 8),
        dwT=np.ascontiguousarray(f(conv_dw_w).reshape(2, 31, 2, 128).transpose(0, 3, 2, 1)).reshape(2, 128, 62),
        dwb=fm(conv_dw_b, 2), lng=fm(conv_ln_g, 2), lnb=fm(conv_ln_b, 2),
        lbl=fm(hgrn_lb_logits, 4), hng=fm(hgrn_norm_g, 4),
        fdw=np.ascontiguousarray(f(ffn_dw_w).reshape(2, 3, 44, 128).transpose(0, 3, 2, 1)).reshape(2, 128, 132),
        R=_rel_table(f(attn_rel_bias)),
        gfin=np.ascontiguousarray(np.broadcast_to(f(g_final)[None, :], (128, D))),
    )
    sc_, sh_, sk_, sv_, sf_ = f(state_conv), f(state_hgrn), f(cache_attn_k), f(cache_attn_v), f(state_ffn)
    in_maps = []
    for c in range(N_CORES):
        b, ci = c // cps, c % cps
        s0 = ci * seg
        xpc = np.zeros((NTOK, D), np.float32)
        lo = max(0, s0 - W)
        xpc[W - (s0 - lo):] = x_prompt[b, lo:s0 + seg]
        pos = np.arange(s0 - W, s0 + seg)
        vt = (pos >= 0).astype(np.float32)
        valid = np.ascontiguousarray(vt.reshape(NT, 128).T)
        kms = []
        for hl in (halos[0], halos[2]):
            row = np.full((512 + hl,), NEG, np.float32)
            row[512:] = np.where(np.arange(s0 - hl, s0) >= 0, 0.0, NEG)
            kms.append(np.ascontiguousarray(np.broadcast_to(row[None], (128, 512 + hl))))
        sl = slice(NSEQ * c, NSEQ * (c + 1))
        m = dict(shared)
        m.update(
            xp=xpc, valid=valid, km0=kms[0], km1=kms[1],
            xs=np.ascontiguousarray(x_sample[sl].reshape(NSEQ * TS, D)),
            sconvT=np.ascontiguousarray(sc_[:, sl].reshape(2, NSEQ, 30, 2, 128).transpose(0, 1, 4, 3, 2)).reshape(2, NSEQ, 128, 60),
            shgrn=np.ascontiguousarray(sh_[:, sl].transpose(0, 1, 3, 2, 4)).reshape(2, NSEQ, 128, 512),
            skT=np.ascontiguousarray(sk_[:, sl].reshape(2, NSEQ, 512, 2, 128).transpose(0, 1, 4, 3, 2)).reshape(2, NSEQ, 128, 1024),
            sv=np.ascontiguousarray(sv_[:, sl].reshape(2, NSEQ, 4, 128, 256).transpose(0, 1, 3, 2, 4)).reshape(2, NSEQ, 128, 1024),
            sffnT=np.ascontiguousarray(sf_[:, sl].reshape(2, NSEQ, 2, 44, 128).transpose(0, 1, 4, 3, 2)).reshape(2, NSEQ, 128, 88),
        )
        in_maps.append(m)
    res = run_bass_kernel_spmd(nc, in_maps, core_ids=list(range(N_CORES))).results
    ndb = x_sample.shape[0]
    y_prompt = np.stack([np.concatenate([res[b * cps + ci]["y"] for ci in range(cps)], 0) for b in range(nb)])
    y_sample = np.concatenate([res[c]["ys"].reshape(NSEQ, TS, D) for c in range(N_CORES)], 0)
    last = [b * cps + cps - 1 for b in range(nb)]

    def convT(a):
        return a.reshape(a.shape[:-2] + (128, 2, 30)).swapaxes(-3, -1).reshape(a.shape[:-2] + (30, 256))

    def ffnT(a):
        return a.reshape(a.shape[:-2] + (128, 44, 2)).swapaxes(-3, -1).reshape(a.shape[:-2] + (2, 5632))

    def kT(a, n):
        return a.reshape(a.shape[:-2] + (128, 2, n)).swapaxes(-3, -1).reshape(a.shape[:-2] + (n, 4, 64))

    def hg(a):
        return a.reshape(a.shape[:-2] + (128, 4, 128)).swapaxes(-3, -2)

    new_conv_p = np.stack([convT(res[c]["convp"]) for c in last], 1)
    new_hgrn_p = np.stack([hg(res[c]["hgrnp"]) for c in last], 1)
    new_k_p = np.stack([kT(res[c]["kp"], 512) for c in last], 1)
    new_v_p = np.stack([res[c]["vp"].reshape(2, 512, 4, 64) for c in last], 1)
    new_ffn_p = np.stack([ffnT(res[c]["ffnp"]) for c in last], 1)
    new_conv_s = np.concatenate([convT(res[c]["convs"]) for c in range(N_CORES)], 1)
    new_hgrn_s = np.concatenate([hg(res[c]["hgrns"]) for c in range(N_CORES)], 1)
    new_k_s = np.concatenate([kT(res[c]["ks"], TS) for c in range(N_CORES)], 1)
    new_v_s = np.concatenate([res[c]["vs"].reshape(2, NSEQ, TS, 4, 64) for c in range(N_CORES)], 1)
    new_ffn_s = np.concatenate([ffnT(res[c]["ffns"]) for c in range(N_CORES)], 1)
    outs = (y_prompt, y_sample, new_conv_p, new_conv_s, new_hgrn_p, new_hgrn_s,
            new_k_p, new_v_p, new_k_s, new_v_s, new_ffn_p, new_ffn_s)
    return tuple(np.ascontiguousarray(o, dtype=np.float32) for o in outs)
```
